# Optimizing a Trainium2 kernel written in Bass

```python
import math
import jax, jax.numpy as jnp
from jax import lax
import numpy as np

D_MODEL = 2048
BATCH = 8
SEQ = 2048
DEPTH = 1

NSA_HEADS = 8
NSA_KV_HEADS = 2
NSA_GROUP = NSA_HEADS // NSA_KV_HEADS
HEAD_DIM = 128
NSA_DIM = NSA_HEADS * HEAD_DIM
NSA_KV_DIM = NSA_KV_HEADS * HEAD_DIM
CMP_BLOCK = 32
CMP_STRIDE = 16
SEL_BLOCK = 64
SEL_TOPK = 16
WINDOW = 512
WIN_Q_BLOCK = 128
SEL_Q_BLOCK = 64
ROPE_THETA = 10000.0
FORCE_SCORE = 1e9
DN_HEADS = 8
DN_HEAD_DIM = 128
DN_DIM = DN_HEADS * DN_HEAD_DIM
DN_CHUNK = 64
CONV_K = 4
D_FF = -(-(8 * D_MODEL) // (3 * 256)) * 256
NORM_EPS = 1e-6
NEG_INF = -1e30
IN_SPLITS = (NSA_DIM, 6 * NSA_KV_DIM, 3 * NSA_HEADS, 3 * DN_DIM, DN_DIM, DN_HEADS, DN_HEADS, 2 * D_MODEL)
N_IN = sum(IN_SPLITS)

kernel_name = 'hybrid_nsa_gdn_block'


def rms_norm(x, w):
    xf = x.astype(jnp.float32)
    y = xf * lax.rsqrt(jnp.mean(xf * xf, axis=-1, keepdims=True) + NORM_EPS)
    return (y * w.astype(jnp.float32)).astype(x.dtype)


def l2norm(x):
    return x * lax.rsqrt(jnp.sum(x * x, axis=-1, keepdims=True) + NORM_EPS)


def rope_tables(seq):
    inv = 1.0 / (ROPE_THETA ** (jnp.arange(0, HEAD_DIM, 2, dtype=jnp.float32) / HEAD_DIM))
    ang = jnp.arange(seq, dtype=jnp.float32)[:, None] * inv[None, :]
    return jnp.cos(ang), jnp.sin(ang)


def apply_rope(x, cos, sin):
    x1, x2 = jnp.split(x.astype(jnp.float32), 2, axis=-1)
    c = cos[None, :, None, :]
    s = sin[None, :, None, :]
    return jnp.concatenate([x1 * c - x2 * s, x2 * c + x1 * s], axis=-1).astype(x.dtype)


def masked_softmax(s, mask):
    return jax.nn.softmax(jnp.where(mask, s.astype(jnp.float32), NEG_INF), axis=-1)


def split_points():
    return [int(v) for v in np.cumsum(IN_SPLITS)[:-1]]


def cmp_to_sel_overlap(n_cmp, n_sel):
    cs = np.arange(n_cmp)[:, None] * CMP_STRIDE
    ss = np.arange(n_sel)[None, :] * SEL_BLOCK
    ov = np.clip(np.minimum(cs + CMP_BLOCK, ss + SEL_BLOCK) - np.maximum(cs, ss), 0, None)
    return jnp.asarray(ov / CMP_BLOCK, dtype=jnp.float32)


def compress_blocks(t, pe, w1, w2):
    b, s, hk, d = t.shape
    n_cmp = (s - CMP_BLOCK) // CMP_STRIDE + 1
    idx = np.arange(n_cmp)[:, None] * CMP_STRIDE + np.arange(CMP_BLOCK)[None, :]
    blk = jnp.swapaxes(t[:, idx], 2, 3) + pe
    blk = blk.reshape(b, n_cmp, hk, CMP_BLOCK * d)
    return jax.nn.gelu(blk @ w1) @ w2


def nsa_mixer(q, kv, gate_logits, cmp_pe_k, cmp_w1_k, cmp_w2_k, cmp_pe_v, cmp_w1_v, cmp_w2_v, cos, sin):
    b, s, _, d = q.shape
    hk, g = NSA_KV_HEADS, NSA_GROUP
    scale = HEAD_DIM ** -0.5
    kv = kv.reshape(b, s, 6, hk, d)
    k_c = apply_rope(kv[:, :, 0], cos, sin)
    v_c = kv[:, :, 1]
    k_s = apply_rope(kv[:, :, 2], cos, sin)
    v_s = kv[:, :, 3]
    k_w = apply_rope(kv[:, :, 4], cos, sin)
    v_w = kv[:, :, 5]
    qg = q.reshape(b, s, hk, g, d)
    t_pos = jnp.arange(s)

    kc = compress_blocks(k_c, cmp_pe_k, cmp_w1_k, cmp_w2_k)
    vc = compress_blocks(v_c, cmp_pe_v, cmp_w1_v, cmp_w2_v)
    n_cmp = kc.shape[1]
    s_cmp = jnp.einsum('bshgd,bchd->bhgsc', qg, kc) * scale
    cmp_end = jnp.arange(n_cmp) * CMP_STRIDE + CMP_BLOCK - 1
    cmp_valid = cmp_end[None, :] <= t_pos[:, None]
    p_cmp = masked_softmax(s_cmp, cmp_valid) * jnp.any(cmp_valid, axis=-1)[:, None].astype(jnp.float32)
    o_cmp = jnp.einsum('bhgsc,bchd->bshgd', p_cmp, vc.astype(jnp.float32))

    n_sel = s // SEL_BLOCK
    imp = jnp.einsum('bhgsc,cj->bhsj', p_cmp, cmp_to_sel_overlap(n_cmp, n_sel))
    blk_t = (t_pos // SEL_BLOCK)[:, None]
    j = jnp.arange(n_sel)[None, :]
    forced = (j == 0) | (j == blk_t) | (j == blk_t - 1)
    imp = jnp.where(forced, FORCE_SCORE, jnp.where(j > blk_t, -FORCE_SCORE, imp))
    n_top = min(SEL_TOPK, n_sel)
    _, sel_idx = lax.top_k(imp, n_top)
    kb = k_s.reshape(b, n_sel, SEL_BLOCK, hk, d).transpose(0, 3, 1, 2, 4)
    vb = v_s.reshape(b, n_sel, SEL_BLOCK, hk, d).transpose(0, 3, 1, 2, 4)
    nqb = s // SEL_Q_BLOCK
    q_blocks = qg.reshape(b, nqb, SEL_Q_BLOCK, hk, g, d).transpose(1, 0, 2, 3, 4, 5)
    idx_blocks = sel_idx.reshape(b, hk, nqb, SEL_Q_BLOCK, n_top).transpose(2, 0, 1, 3, 4)
    pos_blocks = t_pos.reshape(nqb, SEL_Q_BLOCK)
    gather = jax.vmap(jax.vmap(lambda blocks, ix: blocks[ix]))

    def sel_block(args):
        qb, ib, tb = args
        kg = gather(kb, ib)
        vg = gather(vb, ib)
        sc = jnp.einsum('bqhgd,bhqnkd->bhgqnk', qb, kg) * scale
        kpos = ib[..., None] * SEL_BLOCK + jnp.arange(SEL_BLOCK)
        valid = (kpos <= tb[None, None, :, None, None]).reshape(b, hk, 1, SEL_Q_BLOCK, n_top * SEL_BLOCK)
        p = masked_softmax(sc.reshape(b, hk, g, SEL_Q_BLOCK, n_top * SEL_BLOCK), valid)
        p = p.reshape(b, hk, g, SEL_Q_BLOCK, n_top, SEL_BLOCK)
        return jnp.einsum('bhgqnk,bhqnkd->bqhgd', p, vg.astype(jnp.float32))

    o_sel = lax.map(sel_block, (q_blocks, idx_blocks, pos_blocks))
    o_sel = o_sel.transpose(1, 0, 2, 3, 4, 5).reshape(b, s, hk, g, d)

    nb = s // WIN_Q_BLOCK
    n_prev = WINDOW // WIN_Q_BLOCK
    pad = ((0, 0), (WINDOW, 0), (0, 0), (0, 0))
    kp = jnp.pad(k_w, pad).reshape(b, nb + n_prev, WIN_Q_BLOCK, hk, d)
    vp = jnp.pad(v_w, pad).reshape(b, nb + n_prev, WIN_Q_BLOCK, hk, d)
    k_band = jnp.concatenate([kp[:, i:i + nb] for i in range(n_prev + 1)], axis=2)
    v_band = jnp.concatenate([vp[:, i:i + nb] for i in range(n_prev + 1)], axis=2)
    qw = qg.reshape(b, nb, WIN_Q_BLOCK, hk, g, d)
    s_win = jnp.einsum('bnqhgd,bnkhd->bhgnqk', qw, k_band) * scale
    qpos = t_pos.reshape(nb, WIN_Q_BLOCK)
    kpos = (jnp.arange(nb) * WIN_Q_BLOCK - WINDOW)[:, None] + jnp.arange((n_prev + 1) * WIN_Q_BLOCK)[None, :]
    diff = qpos[:, :, None] - kpos[:, None, :]
    win_valid = (diff >= 0) & (diff < WINDOW) & (kpos[:, None, :] >= 0)
    p_win = masked_softmax(s_win, win_valid)
    o_win = jnp.einsum('bhgnqk,bnkhd->bnqhgd', p_win, v_band.astype(jnp.float32)).reshape(b, s, hk, g, d)

    gates = jax.nn.sigmoid(gate_logits.astype(jnp.float32)).reshape(b, s, hk, g, 3)
    o = gates[..., 0:1] * o_cmp + gates[..., 1:2] * o_sel + gates[..., 2:3] * o_win
    return o.reshape(b, s, NSA_DIM)


def gated_deltanet(qkv, z, a, beta_logit, conv_w, a_log, dt_bias, norm_w):
    b, s, c3 = qkv.shape
    h, dk, c = DN_HEADS, DN_HEAD_DIM, DN_CHUNK
    n = s // c
    qkv = lax.conv_general_dilated(qkv, conv_w.astype(qkv.dtype), window_strides=(1,), padding=[(CONV_K - 1, 0)],
                                   dimension_numbers=('NWC', 'WIO', 'NWC'), feature_group_count=c3)
    qkv = jax.nn.silu(qkv.astype(jnp.float32))
    q, k, v = [t.reshape(b, s, h, dk) for t in jnp.split(qkv, 3, axis=-1)]
    q = l2norm(q) * dk ** -0.5
    k = l2norm(k)
    beta = jax.nn.sigmoid(beta_logit.astype(jnp.float32))
    gdec = -jnp.exp(a_log.astype(jnp.float32)) * jax.nn.softplus(a.astype(jnp.float32) + dt_bias.astype(jnp.float32))

    def chunks(t):
        return t.reshape(b, n, c, h, -1).transpose(0, 3, 1, 2, 4)

    q, k, v = chunks(q), chunks(k), chunks(v)
    beta = chunks(beta[..., None])[..., 0]
    gc = jnp.cumsum(chunks(gdec[..., None])[..., 0], axis=-1)
    lower_incl = np.tril(np.ones((c, c), dtype=bool))
    strict = np.tril(np.ones((c, c), dtype=bool), -1)
    decay = jnp.exp(jnp.where(lower_incl, gc[..., :, None] - gc[..., None, :], -jnp.inf))
    kb = k * beta[..., None]
    lmat = jnp.where(strict, jnp.einsum('bhnid,bhnjd->bhnij', kb, k) * decay, 0.0)
    eye = jnp.eye(c, dtype=jnp.float32)
    tinv = lax.linalg.triangular_solve(eye + lmat, jnp.broadcast_to(eye, lmat.shape), left_side=True, lower=True)
    u = tinv @ (v * beta[..., None])
    w = tinv @ (kb * jnp.exp(gc)[..., None])
    attn = jnp.einsum('bhnid,bhnjd->bhnij', q, k) * decay
    g_last = gc[..., -1]
    k_dec = k * jnp.exp(g_last[..., None] - gc)[..., None]
    q_dec = q * jnp.exp(gc)[..., None]

    def step(state, xs):
        q_i, k_i, u_i, w_i, attn_i, gl_i = xs
        v_new = u_i - jnp.einsum('bhck,bhkv->bhcv', w_i, state)
        o_i = jnp.einsum('bhck,bhkv->bhcv', q_i, state) + jnp.einsum('bhij,bhjv->bhiv', attn_i, v_new)
        state = state * jnp.exp(gl_i)[..., None, None] + jnp.einsum('bhck,bhcv->bhkv', k_i, v_new)
        return state, o_i

    xs = tuple(jnp.moveaxis(t, 2, 0) for t in (q_dec, k_dec, u, w, attn, g_last))
    state0 = jnp.zeros((b, h, dk, dk), jnp.float32)
    _, o = lax.scan(step, state0, xs)
    o = o.transpose(1, 0, 3, 2, 4).reshape(b, s, h, dk)
    o = rms_norm(o, norm_w) * jax.nn.silu(z.astype(jnp.float32).reshape(b, s, h, dk))
    return o.reshape(b, s, DN_DIM)


def setup_inputs(seed: int = 0) -> dict:
    key = jax.random.key(seed)
    ks = jax.random.split(key, 24)

    def nrm(k, shape, fan_in):
        return jax.random.normal(k, shape, jnp.float32) * fan_in ** -0.5

    def gain(k, shape):
        return 1.0 + 0.02 * jax.random.normal(k, shape, jnp.float32)

    dt = jnp.exp(jax.random.uniform(ks[5], (DEPTH, DN_HEADS), jnp.float32, math.log(1e-3), math.log(1e-1)))
    return {
        'x': jax.random.normal(ks[0], (BATCH, SEQ, D_MODEL), jnp.float32),
        'norm1_w': gain(ks[1], (DEPTH, D_MODEL)),
        'w_in': nrm(ks[2], (DEPTH, D_MODEL, N_IN), D_MODEL),
        'conv_w': nrm(ks[3], (DEPTH, CONV_K, 1, 3 * DN_DIM), CONV_K),
        'a_log': jnp.log(jax.random.uniform(ks[4], (DEPTH, DN_HEADS), jnp.float32, 1.0, 16.0)),
        'dt_bias': dt + jnp.log(-jnp.expm1(-dt)),
        'dn_norm_w': gain(ks[6], (DEPTH, DN_HEAD_DIM)),
        'cmp_pe_k': 0.02 * jax.random.normal(ks[7], (DEPTH, CMP_BLOCK, HEAD_DIM), jnp.float32),
        'cmp_w1_k': nrm(ks[8], (DEPTH, CMP_BLOCK * HEAD_DIM, HEAD_DIM), CMP_BLOCK * HEAD_DIM),
        'cmp_w2_k': nrm(ks[9], (DEPTH, HEAD_DIM, HEAD_DIM), HEAD_DIM),
        'cmp_pe_v': 0.02 * jax.random.normal(ks[10], (DEPTH, CMP_BLOCK, HEAD_DIM), jnp.float32),
        'cmp_w1_v': nrm(ks[11], (DEPTH, CMP_BLOCK * HEAD_DIM, HEAD_DIM), CMP_BLOCK * HEAD_DIM),
        'cmp_w2_v': nrm(ks[12], (DEPTH, HEAD_DIM, HEAD_DIM), HEAD_DIM),
        'w_up_nsa': nrm(ks[13], (DEPTH, NSA_DIM, D_MODEL), NSA_DIM),
        'w_up_dn': nrm(ks[14], (DEPTH, DN_DIM, D_MODEL), DN_DIM),
        'w_o': nrm(ks[15], (DEPTH, D_MODEL, D_MODEL), D_MODEL),
        'norm2_w': gain(ks[16], (DEPTH, D_MODEL)),
        'w_ffn_gate': nrm(ks[17], (DEPTH, D_MODEL, D_FF), D_MODEL),
        'w_ffn_up': nrm(ks[18], (DEPTH, D_MODEL, D_FF), D_MODEL),
        'w_ffn_down': nrm(ks[19], (DEPTH, D_FF, D_MODEL), D_FF),
        'norm_f_w': gain(ks[20], (D_MODEL,)),
    }


def reference(x, norm1_w, w_in, conv_w, a_log, dt_bias, dn_norm_w, cmp_pe_k, cmp_w1_k, cmp_w2_k,
              cmp_pe_v, cmp_w1_v, cmp_w2_v, w_up_nsa, w_up_dn, w_o, norm2_w, w_ffn_gate, w_ffn_up,
              w_ffn_down, norm_f_w):
    b, s, _ = x.shape
    cos, sin = rope_tables(s)
    for l in range(DEPTH):
        h = rms_norm(x, norm1_w[l])
        proj = h @ w_in[l]
        nsa_q, nsa_kv, nsa_g, dn_qkv, dn_z, dn_a, dn_b, merge_g = jnp.split(proj, split_points(), axis=-1)
        q = apply_rope(nsa_q.reshape(b, s, NSA_HEADS, HEAD_DIM), cos, sin)
        o_nsa = nsa_mixer(q, nsa_kv, nsa_g, cmp_pe_k[l], cmp_w1_k[l], cmp_w2_k[l],
                          cmp_pe_v[l], cmp_w1_v[l], cmp_w2_v[l], cos, sin).astype(x.dtype)
        o_dn = gated_deltanet(dn_qkv, dn_z, dn_a, dn_b, conv_w[l], a_log[l], dt_bias[l], dn_norm_w[l]).astype(x.dtype)
        g_nsa, g_dn = jnp.split(jax.nn.sigmoid(merge_g), 2, axis=-1)
        mixed = g_nsa * (o_nsa @ w_up_nsa[l]) + g_dn * (o_dn @ w_up_dn[l])
        x = x + mixed @ w_o[l]
        h2 = rms_norm(x, norm2_w[l])
        x = x + (jax.nn.silu(h2 @ w_ffn_gate[l]) * (h2 @ w_ffn_up[l])) @ w_ffn_down[l]
    return rms_norm(x, norm_f_w)
```

```python
import math
from contextlib import ExitStack

import numpy as np
import ml_dtypes
import concourse.bass as bass
import concourse.mybir as mybir
from concourse.bass_utils import run_bass_kernel_spmd

F32 = mybir.dt.float32
BF16 = mybir.dt.bfloat16
AF = mybir.ActivationFunctionType
ALU = mybir.AluOpType
AX = mybir.AxisListType

S = 2048
D = 2048
NIN = 10792
DFF = 5632
EPS = 1e-6
SAME_ENGINE_SYNC = True


class Prog:
    ENGS = ('tensor', 'vector', 'scalar', 'gpsimd', 'sync')
    DMAQ = ('sync', 'scalar', 'gpsimd')
    NDS = 6

    def __init__(self, nc):
        self.nc = nc
        self.ops = {e: [] for e in self.ENGS}
        self.state = {}
        self.esem = {e: nc.alloc_semaphore('s_' + e) for e in self.ENGS}
        self.dsem = {e: [nc.alloc_semaphore('d_%s_%d' % (e, i)) for i in range(self.NDS)] for e in self.DMAQ}
        self.dcnt = {e: 0 for e in self.DMAQ}
        self.dlast = {}
        self.base = {e: 0 for e in self.ENGS}
        self.known_c = {e: {} for e in self.ENGS}
        self.known_d = {e: {} for e in self.ENGS}
        self.stats = {}

    def add(self, eng, fn, reads=(), writes=(), signal=True, dma=False):
        idx = len(self.ops[eng])
        deps = []
        for k in reads:
            st = self.state.get(k)
            if st is not None and st['w'] is not None:
                deps.append(st['w'])
        for k in writes:
            st = self.state.get(k)
            if st is not None:
                if st['w'] is not None:
                    deps.append(st['w'])
                deps.extend(st['rc'].values())
                deps.extend(st['rd'])
        slot = None
        if dma:
            n = self.dcnt[eng]
            self.dcnt[eng] += 1
            slot = n % self.NDS
            me = ('d', (eng, slot), 16 * (n // self.NDS + 1))
            prev = self.dlast.get((eng, slot))
            if prev is not None:
                deps.append(prev)
            self.dlast[(eng, slot)] = me
        else:
            me = ('c', eng, idx)
        self.ops[eng].append(dict(fn=fn, deps=deps, signal=signal, dma=dma, slot=slot))
        for k in reads:
            st = self.state.setdefault(k, dict(w=None, rc={}, rd=[]))
            if dma:
                st['rd'].append(me)
            else:
                st['rc'][eng] = me
        for k in writes:
            self.state[k] = dict(w=me, rc={}, rd=[])

    def mm(self, out, lhsT, rhs, start, stop, reads, writes, signal=None):
        self.add('tensor', lambda e: e.matmul(out, lhsT, rhs, start=start, stop=stop),
                 reads, writes, signal=stop if signal is None else signal)

    def tr(self, out, in_, ident, reads, writes, signal=True):
        self.add('tensor', lambda e: e.transpose(out, in_, ident), reads, writes, signal=signal)

    def op(self, eng, meth, *args, reads=(), writes=(), **kw):
        self.add(eng, lambda e: getattr(e, meth)(*args, **kw), reads, writes)

    def dma(self, q, out, in_, reads, writes):
        self.add(q, lambda e: e.dma_start(out=out, in_=in_), reads, writes, dma=True)

    def barrier(self):
        deps = []
        for e in self.ENGS:
            for j in range(len(self.ops[e]) - 1, -1, -1):
                op = self.ops[e][j]
                if op['fn'] is not None and not op['dma']:
                    assert op['signal'], 'last op on %s before barrier must signal' % e
                    deps.append(('c', e, j))
                    break
        deps.extend(self.dlast.values())
        for e in self.ENGS:
            self.ops[e].append(dict(fn=None, deps=list(deps), signal=False, dma=False, slot=None))
        self.state = {}

    def flush(self):
        self.barrier()
        ops = self.ops
        cnt = {}
        nxt = {}
        for e in self.ENGS:
            c = self.base[e]
            cl = []
            for op in ops[e]:
                if op['fn'] is not None and not op['dma'] and op['signal']:
                    c += 1
                cl.append(c)
            cnt[e] = cl
            nl = [None] * len(ops[e])
            nx = None
            for j in range(len(ops[e]) - 1, -1, -1):
                op = ops[e][j]
                if op['fn'] is not None and not op['dma'] and op['signal']:
                    nx = cl[j]
                nl[j] = nx
            nxt[e] = nl
        self._nxt = nxt
        with self.nc.Block() as block:
            @block.tensor
            def _(e):
                self.emit('tensor', e)

            @block.vector
            def _(e):
                self.emit('vector', e)

            @block.scalar
            def _(e):
                self.emit('scalar', e)

            @block.gpsimd
            def _(e):
                self.emit('gpsimd', e)

            @block.sync
            def _(e):
                self.emit('sync', e)
        for e in self.ENGS:
            if cnt[e]:
                self.base[e] = cnt[e][-1]
            self.ops[e] = []

    def emit(self, ename, eng):
        ops = self.ops
        nxt = self._nxt
        known_c = self.known_c[ename]
        known_d = self.known_d[ename]
        n_wait = 0
        for i, op in enumerate(ops[ename]):
            for d in op['deps']:
                if d[0] == 'c':
                    _, f, j = d
                    if f == ename:
                        if ename == 'tensor' or not SAME_ENGINE_SYNC or j >= i:
                            continue
                        if not ops[f][j]['signal']:
                            continue
                    need = nxt[f][j]
                    assert need is not None, 'dependency on %s op %d never signals' % (f, j)
                    if known_c.get(f, 0) >= need:
                        continue
                    eng.wait_ge(self.esem[f], need)
                    known_c[f] = need
                    n_wait += 1
                else:
                    _, key, val = d
                    if known_d.get(key, 0) >= val:
                        continue
                    eng.wait_ge(self.dsem[key[0]][key[1]], val)
                    known_d[key] = val
                    n_wait += 1
            if op['fn'] is None:
                continue
            ins = op['fn'](eng)
            if op['dma']:
                ins.then_inc(self.dsem[ename][op['slot']], 16)
            elif op['signal']:
                ins.then_inc(self.esem[ename], 1)
        self.stats[ename] = self.stats.get(ename, 0) + len(ops[ename])
        self.stats[ename + '_w'] = self.stats.get(ename + '_w', 0) + n_wait


class Rot:
    def __init__(self, tiles, name):
        self.tiles = tiles
        self.name = name
        self.i = 0

    def next(self):
        j = self.i % len(self.tiles)
        self.i += 1
        return self.tiles[j], (self.name, j)


_UID = [0]


def _mk(es, nc, name, shape, dt):
    _UID[0] += 1
    return es.enter_context(nc.sbuf_tensor('%s_u%d' % (name, _UID[0]), shape, dt))


def _rot(es, nc, name, n, shape, dt):
    return Rot([_mk(es, nc, '%s%d' % (name, i), shape, dt) for i in range(n)], name)


def host_consts():
    c = {}
    inv = 1.0 / (10000.0 ** (np.arange(0, 128, 2, dtype=np.float32) / 128.0))
    ang = np.arange(S, dtype=np.float32)[:, None] * inv[None, :].astype(np.float32)
    ang = ang.astype(np.float32)
    cos = np.cos(ang).astype(np.float32).T
    sin = np.sin(ang).astype(np.float32).T
    c['ropec'] = np.ascontiguousarray(np.concatenate([cos, cos], 0))
    c['ropes'] = np.ascontiguousarray(np.concatenate([-sin, sin], 0))
    c['ident_bf'] = np.eye(128, dtype=np.float32).astype(ml_dtypes.bfloat16)
    c['ident_f'] = np.eye(128, dtype=np.float32)
    bf = ml_dtypes.bfloat16
    cc_ = np.arange(128)[:, None]
    qq = np.arange(S)[None, :]
    c['maskC'] = ((16 * cc_ + 31 <= qq) & (cc_ < 127)).astype(np.float32).astype(bf)
    cs = np.arange(128)[:, None] * 16
    ssb = np.arange(32)[None, :] * 64
    ov = np.clip(np.minimum(cs + 32, ssb + 64) - np.maximum(cs, ssb), 0, None) / 32.0
    ov[127] = 0
    c['ov'] = ov.astype(np.float32).astype(bf)
    c['E'] = (np.arange(S)[None, :] // 64 == np.arange(32)[:, None]).astype(np.float32).astype(bf)
    k_ = np.arange(128)[:, None]
    q_ = np.arange(512)[None, :]
    m8 = np.zeros((128, 8, 512), np.float32)
    for r in range(-4, 4):
        diff = q_ - (128 * r + k_)
        m8[:, r + 4, :] = ((diff >= 0) & (diff < 512))
    c['masks8'] = m8.astype(bf)
    c['mb8'] = ((m8 - 1.0) * 30000.0).astype(bf)
    c['maskCb'] = ((c['maskC'].astype(np.float32) - 1.0) * 30000.0).astype(bf)
    c['tinyrow'] = np.full((1, 128), 1e-30, np.float32).astype(bf)
    c['onesrow'] = np.ones((1, 512), np.float32).astype(bf)
    sel = np.zeros((24, 24, 128), np.float32)
    for i in range(24):
        sel[i, i, :] = 1
    c['sel24'] = sel.reshape(24, 24 * 128)
    t_ = np.arange(S)[:, None]
    j_ = np.arange(32)[None, :]
    blk = t_ // 64
    forced = (j_ == 0) | (j_ == blk) | (j_ == blk - 1)
    fut = j_ > blk
    fm_mul = (~(forced | fut)).astype(np.float32)
    fm_add = np.where(forced, 1e9, np.where(fut, -1e9, 0.0)).astype(np.float32)
    c['fm_mul'] = np.ascontiguousarray(fm_mul.reshape(16, 128, 32).transpose(1, 0, 2))
    c['fm_add'] = np.ascontiguousarray(fm_add.reshape(16, 128, 32).transpose(1, 0, 2))
    c['ones_bf'] = np.ones((128, 128), np.float32).astype(bf)
    p_ = np.arange(128)[:, None]
    f_ = np.arange(128)[None, :]
    same = (p_ // 64) == (f_ // 64)
    c['tri'] = ((p_ <= f_) & same).astype(np.float32)
    c['sellast'] = (p_ == 64 * (f_ // 64) + 63).astype(np.float32)
    c['sel63'] = np.broadcast_to(p_ == 63, (128, 128)).astype(np.float32).copy()
    c['sel127'] = np.broadcast_to(p_ == 127, (128, 128)).astype(np.float32).copy()
    c['mblow'] = np.where((f_ <= p_) & same, 0.0, 1e5).astype(np.float32)
    c['strictm'] = ((f_ < p_) & same).astype(np.float32)
    c['ones_f'] = np.ones((128, 128), np.float32)
    c['negmask'] = np.where((f_ <= p_) & same, 0.0, -1e5).astype(np.float32)
    return c


def dense_gen(P, nc, es, name, aT, aT_key, KC, Wv, blocks, banks, T=S, wbufs=3, wcols=512, wrot=None):
    if wrot is None:
        wrot = _rot(es, nc, name + '_w', wbufs, [128, KC, wcols], BF16)
    bi = [0]

    def nextbank():
        b = banks[bi[0] % len(banks)]
        bi[0] += 1
        return b

    for (c0, ncols, mode, epi) in blocks:
        wt, wk = wrot.next()
        P.dma('gpsimd', wt[:, :, 0:ncols], Wv[:, :, c0:c0 + ncols], reads=[], writes=[wk])
        if mode == 'feat':
            for m0 in range(0, ncols, 128):
                mw = min(128, ncols - m0)
                for t0 in range(0, T, 512):
                    ps, pk = nextbank()
                    for kc in range(KC):
                        P.mm(ps[0:mw, 0:512], wt[:, kc, m0:m0 + mw], aT[:, kc, t0:t0 + 512],
                             start=(kc == 0), stop=(kc == KC - 1), reads=[wk, aT_key], writes=[pk])
                    epi(ps[0:mw, 0:512], pk, dict(c0=c0 + m0, mw=mw, t0=t0))
                    yield
        else:
            for t0 in range(0, T, 128):
                ps, pk = nextbank()
                for kc in range(KC):
                    P.mm(ps[:, 0:ncols], aT[:, kc, t0:t0 + 128], wt[:, kc, 0:ncols],
                         start=(kc == 0), stop=(kc == KC - 1), reads=[wk, aT_key], writes=[pk])
                epi(ps[:, 0:ncols], pk, dict(c0=c0, nw=ncols, t0=t0))
                yield


def dense(*a, **kw):
    for _ in dense_gen(*a, **kw):
        pass


def run_streams(gens):
    gens = [g for g in gens if g is not None]
    while gens:
        for g in list(gens):
            try:
                next(g)
            except StopIteration:
                gens.remove(g)


def gdn_stepA_gen(P, nc, ea, G):
    tb = G['tbanks']
    sbank = (tb[0][0][:, :].bitcast(F32), tb[0][1])
    tbank = tb[1]
    ident_bf = G['ident_bf']
    epsb = G['epsb']

    def t8(ps):
        return ps[:, :].rearrange('p (a b) -> p a b', a=8)
    convw = _mk(ea, nc, 'convw', [128, 24, 4], F32)
    P.dma('sync', convw[:], G['convw'][:, :, :], [], ['convw'])
    ones_b = _mk(ea, nc, 'ones_bA', [128, 128], BF16)
    P.dma('sync', ones_b[:], G['ones_bf'][:, :], [], ['ones_bA'])
    xprot = _rot(ea, nc, 'xp', 2, [128, S + 3], F32)
    accrot = _rot(ea, nc, 'acc', 2, [128, S], F32)
    yrot = _rot(ea, nc, 'yy', 2, [128, S], F32)
    sq = _mk(ea, nc, 'sq', [128, S], BF16)
    outrot = _rot(ea, nc, 'qkT', 2, [128, S], BF16)
    rnrot = _rot(ea, nc, 'rn', 2, [128, 512], F32)
    st = _mk(ea, nc, 'tokst', [128, 16, 128], BF16)
    for i in range(2):
        P.op('vector', 'memset', xprot.tiles[i][:, 0:3], 0.0, reads=[], writes=[('xp', i, 'pad')])

    def stageX(c):
        xp, xk = xprot.next()
        acc, ak = accrot.next()
        P.dma('sync', xp[:, 3:S + 3], G['dnqkvT_d'][c * 128:(c + 1) * 128, :], [('dnqkvT', c, t4) for t4 in range(4)], [xk])
        P.op('vector', 'tensor_scalar', acc[:], xp[:, 0:S], convw[:, c, 0:1], None, ALU.mult, reads=[xk, xk + ('pad',), 'convw'], writes=[ak])
        yield
        for j in range(1, 4):
            P.op('vector', 'scalar_tensor_tensor', acc[:], xp[:, j:j + S], convw[:, c, j:j + 1], acc[:], ALU.mult, ALU.add,
                 reads=[xk, xk + ('pad',), 'convw', ak], writes=[ak])
            yield
        return_val[c] = (acc, ak)

    return_val = {}

    def stageY(c):
        which, h = c // 8, c % 8
        acc, ak = return_val[c]
        y, yk = yrot.next()
        P.op('scalar', 'activation', y[:], acc[:], AF.Silu, reads=[ak], writes=[yk])
        yield
        if which < 2:
            P.op('scalar', 'activation', sq[:], y[:], AF.Square, reads=[yk], writes=['sq'])
            ot, otk = outrot.next()
            for t4 in range(4):
                ps, pk = sbank
                P.mm(ps[:, :], ones_b[:, :], sq[:, t4 * 512:(t4 + 1) * 512], True, True, ['ones_bA', 'sq'], [pk])
                rn, rk = rnrot.next()
                P.op('scalar', 'activation', rn[:], ps[:, :], AF.Ln, bias=epsb[:, 0:1], reads=[pk, 'epsb'], writes=[rk])
                P.op('scalar', 'activation', rn[:], rn[:], AF.Exp, scale=-0.5, reads=[rk], writes=[rk])
                P.op('vector', 'scalar_tensor_tensor', ot[:, t4 * 512:(t4 + 1) * 512], y[:, t4 * 512:(t4 + 1) * 512],
                     (128 ** -0.5) if which == 0 else 1.0, rn[:], ALU.mult, ALU.mult, reads=[yk, rk], writes=[(otk, t4)])
                yield
            dstd = G['gqT_d'] if which == 0 else G['gkT_d']
            P.dma('scalar', dstd[h, :, :], ot[:, :], [(otk, t4) for t4 in range(4)], [])
            srcT = ot
            srck = [(otk, t4) for t4 in range(4)]
        else:
            P.op('scalar', 'copy', sq[:], y[:], reads=[yk], writes=['sq'])
            srcT = sq
            srck = ['sq']
        if which >= 1:
            for half in range(2):
                pt, pk = tbank
                for j in range(8):
                    tt = half * 8 + j
                    P.tr(pt[:, j * 128:(j + 1) * 128], srcT[:, tt * 128:(tt + 1) * 128], ident_bf[:, :],
                         srck + ['ident_bf'], [pk], signal=(j == 7))
                P.op('vector' if half == 0 else 'scalar', 'tensor_copy' if half == 0 else 'copy',
                     st[:, half * 8:(half + 1) * 8, :], t8(pt), reads=[pk], writes=[('tokst', half)])
                yield
            dstd = G['ktok_d'] if which == 1 else G['vtok2_d']
            P.dma('scalar', dstd[h].rearrange('(tt p) d -> p tt d', p=128), st[:, :, :], [('tokst', 0), ('tokst', 1)], [])
        yield

    for _ in stageX(0):
        yield
    for c in range(24):
        gx = stageX(c + 1) if c + 1 < 24 else iter(())
        gy = stageY(c)
        alive = [gx, gy]
        while alive:
            for g in list(alive):
                try:
                    next(g)
                except StopIteration:
                    alive.remove(g)
            yield


def phase_nsa(P, nc, G):
    banks = G['banks']
    tb = G['tbanks']
    SC = 128 ** -0.5
    sbank = [banks[0], banks[1], banks[2]]
    obank = [banks[3], banks[4]]
    ubank = [banks[5], (tb[0][0][:, :].bitcast(F32), tb[0][1])]
    gbank = (tb[1][0][:, :].bitcast(F32), tb[1][1])
    mbank = gbank
    cnt = dict(s=0, o=0)
    ident_f = G['ident_f']
    ident_bf = G['ident_bf']
    with ExitStack() as es:
        def ld(name, shape, dt, src, q='sync'):
            t = _mk(es, nc, name, shape, dt)
            P.dma(q, t[:], src, [], [name])
            return t
        maskCb = ld('maskCb', [128, S], BF16, G['maskCb'][:, :])
        mb8 = ld('mb8', [128, 8, 512], BF16, G['mb8'][:, :, :], 'scalar')
        ovt = ld('ovt', [128, 32], BF16, G['ov'][:, :])
        Et = ld('Et', [32, S], BF16, G['E'][:, :], 'scalar')
        ones = ld('ones', [128, 128], BF16, G['ones_bf'][:, :])
        tinyr = ld('tinyr', [1, 128], BF16, G['tinyrow'][:, :])
        onesr = ld('onesr', [1, 512], BF16, G['onesrow'][:, :])
        fm_mul = ld('fm_mul', [128, 16, 32], F32, G['fm_mul'][:, :, :], 'scalar')
        fm_add = ld('fm_add', [128, 16, 32], F32, G['fm_add'][:, :, :])
        W1 = []
        W2 = []
        peT = []
        for i, nm in enumerate(('k', 'v')):
            w1 = _mk(es, nc, 'cw1' + nm, [128, 32, 128], BF16)
            P.dma('gpsimd', w1[:], G['cmp_w1_' + nm].ap().rearrange('(l d) f -> d l f', d=128), [], ['cw1' + nm])
            w2 = _mk(es, nc, 'cw2' + nm, [128, 128], BF16)
            P.dma('gpsimd', w2[:], G['cmp_w2_' + nm][:, :], [], ['cw2' + nm])
            pt = _mk(es, nc, 'cpe' + nm, [128, 32], BF16)
            P.dma('gpsimd', pt[:], G['peT_' + nm][:, :], [], ['cpe' + nm])
            W1.append(w1); W2.append(w2); peT.append(pt)
        qtile = [_rot(es, nc, 'qt%d' % hl, 2, [128, 512], BF16) for hl in range(4)]
        ptrot = _rot(es, nc, 'pT', 4, [128, 512], BF16)
        rsrot = _rot(es, nc, 'rs', 2, [128, 512], F32)
        posrot = _rot(es, nc, 'pos', 2, [128, 512], F32)
        gbrot = _rot(es, nc, 'gbs', 4, [128, 512], F32)
        tgrot = _rot(es, nc, 'tg', 2, [128, 512], F32)
        tmrot = _rot(es, nc, 'tmpo', 2, [128, 512], F32)
        oacc = [_mk(es, nc, 'oacc%d' % i, [128, 512], F32) for i in range(4)]
        obf = _rot(es, nc, 'obf', 2, [128, 512], BF16)
        impacc = _mk(es, nc, 'impacc', [32, 512], F32)
        imptok = _mk(es, nc, 'imptok', [128, 4, 32], F32)
        impw = _mk(es, nc, 'impw', [128, 4, 32], F32)
        mx8 = _mk(es, nc, 'mx8', [128, 8], F32)
        thr = _mk(es, nc, 'thr', [128, 1], F32)
        selb = _mk(es, nc, 'selb', [128, 4, 32], F32)
        biasT = _mk(es, nc, 'biasT', [32, 512], BF16)

        for hk in range(2):
            with ExitStack() as eh:
                def ldh(name, shape, src, q):
                    t = _mk(eh, nc, name, shape, BF16)
                    P.dma(q, t[:], src, [], [name])
                    return t
                kcx = ldh('kcx', [128, S], G['kvT_d'][0, hk, :, :], 'sync')
                vcx = ldh('vcx', [128, S], G['kvT_d'][1, hk, :, :], 'scalar')
                ksT = ldh('ksT', [128, S], G['kvT_d'][2, hk, :, :], 'sync')
                kwT = ldh('kwT', [128, S], G['kvT_d'][4, hk, :, :], 'scalar')
                vs = ldh('vs', [128, 16, 128], G['vtok_d'][0].rearrange('(tt p) c -> p tt c', p=128)[:, :, hk * 128:(hk + 1) * 128], 'sync')
                vw = ldh('vw', [128, 16, 128], G['vtok_d'][1].rearrange('(tt p) c -> p tt c', p=128)[:, :, hk * 128:(hk + 1) * 128], 'scalar')
                kcT = _mk(eh, nc, 'kcT', [128, 128], BF16)
                vc = _mk(eh, nc, 'vc', [128, 128], BF16)
                cb = _mk(eh, nc, 'cb', [128, 1], F32)
                cu = _mk(eh, nc, 'cu', [128, 128], F32)
                ct = _mk(eh, nc, 'ct', [128, 128], F32)
                cg = _mk(eh, nc, 'cg', [128, 128], BF16)
                for i, (xT, xk) in enumerate(((kcx, 'kcx'), (vcx, 'vcx'))):
                    nm = 'kv'[i]
                    ps, pk = mbank
                    for l in range(32):
                        P.mm(ps[:, 0:127], W1[i][:, l, :], xT[:, l:l + 16 * 126 + 1:16], start=(l == 0), stop=(l == 31),
                             reads=['cw1' + nm, xk], writes=[pk])
                    ps2, pk2 = sbank[0]
                    for l in range(32):
                        P.mm(ps2[:, 0:1], W1[i][:, l, :], peT[i][:, l:l + 1], start=(l == 0), stop=(l == 31),
                             reads=['cw1' + nm, 'cpe' + nm], writes=[pk2])
                    P.op('vector', 'tensor_copy', cb[:], ps2[:, 0:1], reads=[pk2], writes=['cb'])
                    P.op('scalar', 'activation', cu[:, 0:127], ps[:, 0:127], AF.Identity, bias=cb[:, 0:1], reads=[pk, 'cb'], writes=['cu'])
                    P.op('vector', 'tensor_tensor', ct[:, 0:127], cu[:, 0:127], cu[:, 0:127], ALU.mult, reads=['cu'], writes=['ct'])
                    P.op('vector', 'tensor_scalar', ct[:, 0:127], ct[:, 0:127], 0.044715, 1.0, ALU.mult, ALU.add, reads=['ct'], writes=['ct'])
                    P.op('vector', 'tensor_tensor', ct[:, 0:127], ct[:, 0:127], cu[:, 0:127], ALU.mult, reads=['ct', 'cu'], writes=['ct'])
                    P.op('scalar', 'activation', ct[:, 0:127], ct[:, 0:127], AF.Tanh, scale=0.7978845608028654, reads=['ct'], writes=['ct'])
                    P.op('vector', 'scalar_tensor_tensor', ct[:, 0:127], ct[:, 0:127], 1.0, cu[:, 0:127], ALU.add, ALU.mult,
                         reads=['ct', 'cu'], writes=['ct'])
                    P.op('vector', 'tensor_scalar', cg[:, 0:127], ct[:, 0:127], 0.5, None, ALU.mult, reads=['ct'], writes=['cg'])
                    if i == 0:
                        P.mm(ps[:, 0:127], W2[0][:, :], cg[:, 0:127], True, True, ['cw2k', 'cg'], [pk])
                        P.op('vector', 'tensor_copy', kcT[:, 0:127], ps[:, 0:127], reads=[pk], writes=['kcT'])
                    else:
                        P.mm(ps[0:127, 0:128], cg[:, 0:127], W2[1][:, :], True, True, ['cw2v', 'cg'], [pk])
                        P.op('vector', 'tensor_copy', vc[0:127, :], ps[0:127, 0:128], reads=[pk], writes=['vc'])

                def load_q(qi_):
                    res = []
                    for hl in range(4):
                        h = hk * 4 + hl
                        t_, k_ = qtile[hl].next()
                        P.dma('sync', t_[:], G['qT_d'][h, :, qi_ * 512:(qi_ + 1) * 512], [], [k_])
                        res.append((t_, k_))
                    return res
                qnext = load_q(0)
                for qi in range(4):
                    q0 = qi * 512
                    qh = qnext
                    if qi + 1 < 4:
                        qnext = load_q(qi + 1)

                    tiles = []
                    for hl in range(4):
                        tiles.append(dict(br=0, hl=hl, ki=0, n=0, last=True))
                    ntile_cmp = 4
                    for hl in range(4):
                        kis = list(range(max(0, 4 * qi - 4), 4 * qi + 4))
                        for n_, ki in enumerate(kis):
                            tiles.append(dict(br=2, hl=hl, ki=ki, n=n_, last=(n_ == len(kis) - 1)))
                    for hl in range(4):
                        kis = list(range(0, 4 * qi + 4))
                        for n_, ki in enumerate(kis):
                            tiles.append(dict(br=1, hl=hl, ki=ki, n=n_, last=(n_ == len(kis) - 1)))
                    first_sel = next(i for i, t in enumerate(tiles) if t['br'] == 1)

                    def stage1(t):
                        hl, br, ki = t['hl'], t['br'], t['ki']
                        qt, qk = qh[hl]
                        pS, pSk = sbank[cnt['s'] % 3]; cnt['s'] += 1
                        t['pS'] = (pS, pSk)
                        if br == 0:
                            P.mm(pS[0:127, :], kcT[:, 0:127], qt[:, :], True, False, ['kcT', qk], [pSk], signal=False)
                            P.mm(pS[0:127, :], ident_bf[0:127, 0:127], maskCb[0:127, q0:q0 + 512], False, True, ['ident_bf', 'maskCb'], [pSk])
                        elif br == 1:
                            r = ki - 4 * qi
                            P.mm(pS[:, :], ksT[:, ki * 128:(ki + 1) * 128], qt[:, :], True, False, ['ksT', qk], [pSk], signal=False)
                            if r >= 0:
                                P.mm(pS[:, :], ident_bf[:, :], mb8[:, r + 4, :], False, False, ['ident_bf', 'mb8'], [pSk], signal=False)
                            P.mm(pS[:, :], Et[0:32, ki * 128:(ki + 1) * 128], biasT[0:32, :], False, True, ['Et', 'biasT'], [pSk])
                        else:
                            r = ki - 4 * qi
                            P.mm(pS[:, :], kwT[:, ki * 128:(ki + 1) * 128], qt[:, :], True, False, ['kwT', qk], [pSk], signal=False)
                            P.mm(pS[:, :], ident_bf[:, :], mb8[:, r + 4, :], False, True, ['ident_bf', 'mb8'], [pSk])
                        if t['n'] == 0:
                            t['acc'] = (obank[cnt['o'] % 2], ubank[cnt['o'] % 2]); cnt['o'] += 1
                            h = hk * 4 + hl
                            gidx = h * 3 + br
                            gb, gbk = gbrot.next()
                            P.dma('sync', gb[:, :], G['gT_d'][gidx:gidx + 1, q0:q0 + 512].partition_broadcast(128), [], [gbk])
                            t['gb'] = (gb, gbk)

                    def stage2(t, head_t):
                        hl, br, ki = t['hl'], t['br'], t['ki']
                        pS, pSk = t['pS']
                        (po, pok), (pu, puk) = head_t['acc']
                        np_ = 127 if br == 0 else 128
                        pT, pTk = ptrot.next()
                        P.op('scalar', 'activation', pT[0:np_, :], pS[0:np_, :], AF.Exp, scale=SC, reads=[pSk], writes=[pTk])
                        first, last = t['n'] == 0, t['last']
                        if br == 0:
                            vt, vk = vc[0:127, :], 'vc'
                        elif br == 1:
                            vt, vk = vs[:, ki, :], 'vs'
                        else:
                            vt, vk = vw[:, ki, :], 'vw'
                        P.mm(po[:, :], vt, pT[0:np_, :], first, last, [vk, pTk], [pok])
                        if br == 0:
                            P.mm(pu[:, :], ones[0:127, :], pT[0:127, :], True, False, ['ones', pTk], [puk], signal=False)
                            P.mm(pu[:, :], tinyr[0:1, :], onesr[0:1, :], False, True, ['tinyr', 'onesr'], [puk])
                            pi_, pik = mbank
                            P.mm(pi_[0:32, :], ovt[0:127, :], pT[0:127, :], True, True, ['ovt', pTk], [pik])
                        else:
                            P.mm(pu[:, :], ones[:, :], pT[:, :], first, last, ['ones', pTk], [puk])
                        if not last:
                            return
                        gb, gbk = head_t['gb']
                        tg, tk = tgrot.next()
                        pos, posk = posrot.next()
                        P.op('vector', 'tensor_copy', pos[:], po[:, :], reads=[pok], writes=[posk])
                        rs, rk = rsrot.next()
                        P.op('scalar', 'activation', rs[:], pu[:, :], AF.Ln, reads=[puk], writes=[rk])
                        P.op('scalar', 'activation', rs[:], rs[:], AF.Exp, scale=-1.0, reads=[rk], writes=[rk])
                        if br == 0:
                            if hl == 0:
                                P.op('vector', 'tensor_tensor', impacc[:, :], pi_[0:32, :], rs[0:32, :], ALU.mult, reads=[pik, rk], writes=['impacc'])
                            else:
                                it, itk = tmrot.next()
                                P.op('vector', 'tensor_tensor', it[0:32, :], pi_[0:32, :], rs[0:32, :], ALU.mult, reads=[pik, rk], writes=[itk])
                                P.op('gpsimd', 'tensor_tensor', impacc[:, :], impacc[:, :], it[0:32, :], ALU.add, reads=[itk, 'impacc'], writes=['impacc'])
                        P.op('vector', 'tensor_tensor', tg[:], rs[:], gb[:, :], ALU.mult, reads=[rk, gbk], writes=[tk])
                        if br == 0:
                            P.op('vector', 'tensor_tensor', oacc[hl][:], pos[:], tg[:], ALU.mult, reads=[posk, tk], writes=[('oacc', hl)])
                        else:
                            tm, tmk = tmrot.next()
                            P.op('vector', 'tensor_tensor', tm[:], pos[:], tg[:], ALU.mult, reads=[posk, tk], writes=[tmk])
                            P.op('gpsimd', 'tensor_tensor', oacc[hl][:], oacc[hl][:], tm[:], ALU.add, reads=[tmk, ('oacc', hl)], writes=[('oacc', hl)])
                        if br == 1:
                            h = hk * 4 + hl
                            ob, obk = obf.next()
                            P.op('scalar', 'copy', ob[:], oacc[hl][:], reads=[('oacc', hl)], writes=[obk])
                            P.dma('sync', G['onsaT_d'][h, :, q0:q0 + 512], ob[:], [obk], [])

                    def topk():
                        pm, pmk = mbank
                        for s4 in range(4):
                            P.tr(pm[:, s4 * 32:(s4 + 1) * 32], impacc[0:32, s4 * 128:(s4 + 1) * 128], ident_f[0:32, 0:32],
                                 ['impacc', 'ident_f'], [pmk], signal=(s4 == 3))
                        P.op('vector', 'tensor_copy', imptok[:, :, :], pm[:, 0:128].rearrange('p (a b) -> p a b', a=4), reads=[pmk], writes=['imptok'])
                        P.op('vector', 'tensor_tensor', imptok[:, :, :], imptok[:, :, :], fm_mul[:, qi * 4:(qi + 1) * 4, :], ALU.mult,
                             reads=['imptok', 'fm_mul'], writes=['imptok'])
                        P.op('vector', 'tensor_tensor', imptok[:, :, :], imptok[:, :, :], fm_add[:, qi * 4:(qi + 1) * 4, :], ALU.add,
                             reads=['imptok', 'fm_add'], writes=['imptok'])
                        for s4 in range(4):
                            P.op('vector', 'max', mx8[:, :], imptok[:, s4, :], reads=['imptok'], writes=['mx8'])
                            P.op('vector', 'match_replace', impw[:, s4, :], mx8[:, :], imptok[:, s4, :], -3e9, reads=['imptok', 'mx8'], writes=[('impw', s4)])
                            P.op('vector', 'max', mx8[:, :], impw[:, s4, :], reads=[('impw', s4)], writes=['mx8'])
                            P.op('vector', 'tensor_reduce', thr[:, :], mx8[:, :], AX.X, ALU.min, reads=['mx8'], writes=['thr'])
                            P.op('vector', 'tensor_scalar', selb[:, s4, :], imptok[:, s4, :], thr[:, 0:1], None, ALU.is_ge,
                                 reads=['imptok', 'thr'], writes=[('selb', s4)])
                            P.op('vector', 'tensor_scalar', selb[:, s4, :], selb[:, s4, :], 30000.0, -30000.0, ALU.mult, ALU.add,
                                 reads=[('selb', s4)], writes=[('selb', s4)])
                        for s4 in range(4):
                            P.tr(pm[0:32, s4 * 128:(s4 + 1) * 128], selb[:, s4, :], ident_f[:, :], [('selb', s4), 'ident_f'], [pmk], signal=(s4 == 3))
                        P.op('vector', 'tensor_copy', biasT[:, :], pm[0:32, :], reads=[pmk], writes=['biasT'])
                        if 'biasT' in G['dbg_t']:
                            P.dma('sync', G['dbg_t']['biasT'][hk, :, q0:q0 + 512], biasT[:, :], ['biasT'], [])

                    heads = {}
                    LOOK = 2
                    issued = 0

                    def issue_upto(lim):
                        nonlocal issued
                        while issued < min(lim, len(tiles)):
                            if tiles[issued]['br'] == 1 and not sel_ready[0]:
                                break
                            stage1(tiles[issued])
                            issued += 1
                    sel_ready = [False]
                    issue_upto(LOOK)
                    for n in range(len(tiles)):
                        t = tiles[n]
                        key = (t['br'], t['hl'])
                        if t['n'] == 0:
                            heads[key] = t
                        issue_upto(n + 1 + LOOK)
                        if issued <= n:
                            issue_upto(n + 1)
                        stage2(t, heads[key])
                        if n == ntile_cmp - 1:
                            topk()
                            sel_ready[0] = True
                P.flush()


def phase_gdn(P, nc, G):
    banks = G['banks']
    tb = G['tbanks']
    bi = [0, 0]

    def nb():
        b = banks[bi[0] % 6]
        bi[0] += 1
        return b

    def ntb():
        b = tb[bi[1] % 2]
        bi[1] += 1
        return b

    def b4(ps):
        return ps[:, :].rearrange('p (a b) -> p a b', a=4)

    def t8(ps):
        return ps[:, :].rearrange('p (a b) -> p a b', a=8)

    ident_f = G['ident_f']
    ident_bf = G['ident_bf']
    BIGK = ['gq', 'gk']
    with ExitStack() as es:
        def ld(name, shape, dt, src, q='sync'):
            t = _mk(es, nc, name, shape, dt)
            P.dma(q, t[:], src, [], [name])
            return t
        tri = ld('tri', [128, 128], F32, G['tri'][:, :])
        sellast = ld('sellast', [128, 128], F32, G['sellast'][:, :], 'scalar')
        sel63 = ld('sel63', [128, 128], F32, G['sel63'][:, :])
        sel127 = ld('sel127', [128, 128], F32, G['sel127'][:, :], 'scalar')
        mblow = ld('mblow', [128, 128], F32, G['mblow'][:, :])
        strict = ld('strict', [128, 128], F32, G['strictm'][:, :], 'scalar')
        ones_f = ld('ones_f', [128, 128], F32, G['ones_f'][:, :])
        ones_b = ld('ones_b', [128, 128], BF16, G['ones_bf'][:, :], 'scalar')
        alr = ld('alr', [128, 128], F32, G['alog_rep'][:, :], 'scalar')
        dtr = ld('dtr', [128, 128], F32, G['dtb_rep'][:, :])
        nwb = ld('nwb', [128, 1024], F32, G['nw_rep'][0:1, :].partition_broadcast(128), 'scalar')
        epsb = G['epsb']
        ktok_d = G['ktok_d']
        vtok2_d = G['vtok2_d']

        ab = ld('ab', [128, 16, 16], F32, G['dnab_d'].ap().rearrange('(tt p) c -> p tt c', p=128))
        names = ['beta', 'gg', 'gc', 'glsel', 'egc', 'ekd', 'negbeta', 'bgc', 'tmpa', 'tmpb']
        sc = {n: _mk(es, nc, 'sc_' + n, [128, 128], F32) for n in names}
        egl2 = _mk(es, nc, 'egl2', [128, 16, 2, 8], F32)

        def v3(t):
            return t[:, :].rearrange('p (a b) -> p a b', a=16)
        P.op('scalar', 'activation', v3(sc['beta']), ab[:, :, 8:16], AF.Sigmoid, reads=['ab'], writes=['beta'])
        P.op('vector', 'tensor_tensor', v3(sc['tmpa']), ab[:, :, 0:8], v3(dtr), ALU.add, reads=['ab', 'dtr'], writes=['tmpa'])
        P.op('scalar', 'activation', sc['tmpa'][:, :], sc['tmpa'][:, :], AF.Exp, reads=['tmpa'], writes=['tmpa'])
        P.op('vector', 'tensor_scalar', sc['tmpa'][:, :], sc['tmpa'][:, :], 1.0, None, ALU.add, reads=['tmpa'], writes=['tmpa'])
        P.op('scalar', 'activation', sc['tmpa'][:, :], sc['tmpa'][:, :], AF.Ln, reads=['tmpa'], writes=['tmpa'])
        P.op('scalar', 'activation', sc['tmpb'][:, :], alr[:, :], AF.Exp, reads=['alr'], writes=['tmpb'])
        P.op('vector', 'scalar_tensor_tensor', sc['gg'][:, :], sc['tmpa'][:, :], -1.0, sc['tmpb'][:, :], ALU.mult, ALU.mult,
             reads=['tmpa', 'tmpb'], writes=['gg'])
        ps, pk = nb()
        P.mm(ps[:, 0:128], tri[:, :], sc['gg'][:, :], True, True, ['tri', 'gg'], [pk])
        P.op('vector', 'tensor_copy', sc['gc'][:, :], ps[:, 0:128], reads=[pk], writes=['gc'])
        ps, pk = nb()
        P.mm(ps[:, 0:128], sellast[:, :], sc['gc'][:, :], True, True, ['sellast', 'gc'], [pk])
        P.op('vector', 'tensor_tensor', sc['tmpa'][:, :], ps[:, 0:128], sc['gc'][:, :], ALU.subtract, reads=[pk, 'gc'], writes=['tmpa'])
        P.op('scalar', 'activation', sc['ekd'][:, :], sc['tmpa'][:, :], AF.Exp, reads=['tmpa'], writes=['ekd'])
        for ci, selm in enumerate((sel63, sel127)):
            ps, pk = nb()
            P.mm(ps[:, 0:128], selm[:, :], sc['gc'][:, :], True, True, ['sel63', 'sel127', 'gc'], [pk])
            P.op('scalar', 'activation', egl2[:, :, ci, :], ps[:, 0:128].rearrange('p (a b) -> p a b', a=16), AF.Exp,
                 reads=[pk], writes=[('egl2', ci)])
        P.op('scalar', 'activation', sc['egc'][:, :], sc['gc'][:, :], AF.Exp, reads=['gc'], writes=['egc'])
        P.op('vector', 'tensor_scalar', sc['negbeta'][:, :], sc['beta'][:, :], -1.0, None, ALU.mult, reads=['beta'], writes=['negbeta'])
        P.op('vector', 'tensor_tensor', sc['bgc'][:, :], sc['beta'][:, :], sc['egc'][:, :], ALU.mult, reads=['beta', 'egc'], writes=['bgc'])
        if 'gdn_sc' in G['dbg_t']:
            for i_, n_ in enumerate(('beta', 'gg', 'gc', 'ekd')):
                P.dma('sync', G['dbg_t']['gdn_sc'][i_].rearrange('(tt p) h -> p tt h', p=128), v3(sc[n_]), [n_], [])

        gcT_s = _mk(es, nc, 'gcT_s', [8, S], F32)
        gc3 = v3(sc['gc']); egc3 = v3(sc['egc']); nb3 = v3(sc['negbeta']); bgc3 = v3(sc['bgc'])
        beta3 = v3(sc['beta']); ekd3 = v3(sc['ekd'])
        for q4 in range(4):
            ps, pk = nb()
            for j in range(4):
                tt = q4 * 4 + j
                P.tr(ps[0:8, j * 128:(j + 1) * 128], gc3[:, tt, :], ident_f[:, :], ['gc', 'ident_f'], [pk], signal=(j == 3))
            P.op('vector', 'tensor_copy', gcT_s[:, q4 * 512:(q4 + 1) * 512], ps[0:8, :], reads=[pk], writes=[('gcT_s', q4)])
        P.dma('sync', G['gcT_d'].ap().rearrange('tt o (h t) -> h (tt o) t', h=8), gcT_s[:, :].rearrange('h (tt t) -> h tt t', tt=16),
              [('gcT_s', q4) for q4 in range(4)], ['gcT_d'])

        def T3(name, dt):
            return _mk(es, nc, name, [128, 8, 128], dt)
        decay = T3('decay', F32)
        NM = [[T3('N%d' % i, BF16), T3('M%d' % i, BF16)] for i in range(2)]
        PTf = T3('PTf', F32); PTb = T3('PTb', BF16)
        Nf = T3('Nf', F32)
        attn = T3('attn', BF16)
        vb = T3('vb', BF16); kbg = T3('kbg', BF16)
        vn = T3('vn', BF16)
        tmpo = T3('tmpo', F32); oall = T3('oall', F32); o2 = T3('o2', F32); onb = T3('onb', BF16)
        ostg = T3('ostg', BF16)
        attnT2 = [T3('attnT%d' % i, BF16) for i in range(2)]
        kdec2 = [T3('kdec%d' % i, BF16) for i in range(2)]
        uf2 = [T3('uf%d' % i, F32) for i in range(2)]
        wT2 = [T3('wT%d' % i, BF16) for i in range(2)]
        ktok2 = [T3('ktok%d' % i, BF16) for i in range(2)]
        vtok2 = [T3('vtok%d' % i, BF16) for i in range(2)]
        zs2 = [_mk(es, nc, 'zs%d' % i, [128, 1024], F32) for i in range(3)]
        grow2 = [_mk(es, nc, 'grow%d' % i, [128, 1024], F32) for i in range(2)]
        qTt3 = [T3('qTt%d' % i, BF16) for i in range(3)]
        kTt2 = [T3('kTt%d' % i, BF16) for i in range(2)]
        negmask = ld('negmask', [128, 128], F32, G['negmask'][:, :])
        Sf = T3('Sf', F32); Sb = T3('Sb', BF16)
        ssq = _mk(es, nc, 'gssq', [128, 8], F32)
        P.op('vector', 'memset', Sf[:, :, :], 0.0, reads=[], writes=[('Sf', 0), ('Sf', 1)])
        P.op('vector', 'memset', Sb[:, :, :], 0.0, reads=[], writes=['Sb'])

        def bc_h(ap2):
            return ap2.unsqueeze(2).to_broadcast([128, 8, 128])

        def bc_m(ap2):
            return ap2.unsqueeze(1).to_broadcast([128, 8, 128])

        def loads(tt):
            pb = tt % 2
            tl = slice(tt * 128, (tt + 1) * 128)
            P.dma('sync', ktok2[pb][:, :, :], ktok_d[:, tl, :].rearrange('h p d -> p h d'), [], [('ktok', pb)])
            P.dma('scalar', vtok2[pb][:, :, :], vtok2_d[:, tl, :].rearrange('h p d -> p h d'), [], [('vtok', pb)])
            P.dma('sync', zs2[tt % 3][:, :], G['dnz_d'][tl, :], [], [('zs', tt % 3)])
            P.dma('scalar', grow2[pb][:, :], G['gcT_d'][tt, 0:1, :].partition_broadcast(128), ['gcT_d'], [('grow', pb)])
            P.dma('sync', qTt3[tt % 3][:, :, :], G['gqT_d'][:, :, tl].rearrange('h p t -> p h t'), [], [('qTt', tt % 3)])
            P.dma('scalar', kTt2[pb][:, :, :], G['gkT_d'][:, :, tl].rearrange('h p t -> p h t'), [], [('kTt', pb)])

        def prep(tt):
            pb = tt % 2
            tl = slice(tt * 128, (tt + 1) * 128)
            ktok, vtok, grow = ktok2[pb], vtok2[pb], grow2[pb]
            qTt, kTt = qTt3[tt % 3], kTt2[pb]
            attnT, kdec, uf, wT = attnT2[pb], kdec2[pb], uf2[pb], wT2[pb]
            if tt + 1 < 16:
                loads(tt + 1)
            P.op('vector', 'tensor_tensor', decay[:, :, :], bc_h(gc3[:, tt, :]), grow[:, :].rearrange('p (a b) -> p a b', a=8), ALU.subtract,
                 reads=['gc', ('grow', pb)], writes=['decay'])
            P.op('vector', 'tensor_tensor', decay[:, :, :], decay[:, :, :], bc_m(negmask[:, :]), ALU.add, reads=['decay', 'negmask'], writes=['decay'])
            P.op('scalar', 'activation', decay[:, :, :], decay[:, :, :], AF.Exp, reads=['decay'], writes=['decay'])
            yield
            for hb in range(2):
                ps, pk = nb()
                for hl in range(4):
                    h = hb * 4 + hl
                    P.mm(ps[:, hl * 128:(hl + 1) * 128], kTt[:, h, :], kTt[:, h, :], True, True, [('kTt', pb)], [pk])
                P.op('vector', 'tensor_tensor', Nf[:, hb * 4:hb * 4 + 4, :], b4(ps), decay[:, hb * 4:hb * 4 + 4, :], ALU.mult,
                     reads=[pk, 'decay'], writes=[('Nf', hb)])
                ps2, pk2 = nb()
                for hl in range(4):
                    h = hb * 4 + hl
                    P.mm(ps2[:, hl * 128:(hl + 1) * 128], qTt[:, h, :], kTt[:, h, :], True, True, [('qTt', tt % 3), ('kTt', pb)], [pk2])
                P.op('vector', 'tensor_tensor', attn[:, hb * 4:hb * 4 + 4, :], b4(ps2), decay[:, hb * 4:hb * 4 + 4, :], ALU.mult,
                     reads=[pk2, 'decay'], writes=[('attn', hb)])
                yield
            P.op('gpsimd', 'tensor_tensor', Nf[:, :, :], Nf[:, :, :], bc_h(nb3[:, tt, :]), ALU.mult,
                 reads=[('Nf', 0), ('Nf', 1), 'negbeta'], writes=[('Nf', 0), ('Nf', 1)])
            N1, M1 = NM[0]
            P.op('gpsimd', 'tensor_tensor', N1[:, :, :], Nf[:, :, :], bc_m(strict[:, :]), ALU.mult,
                 reads=[('Nf', 0), ('Nf', 1), 'strict'], writes=[('N', 0, 0), ('N', 0, 1)])
            P.op('gpsimd', 'tensor_tensor', vb[:, :, :], vtok[:, :, :], bc_h(beta3[:, tt, :]), ALU.mult, reads=[('vtok', pb), 'beta'], writes=['vb'])
            P.op('gpsimd', 'tensor_tensor', kbg[:, :, :], ktok[:, :, :], bc_h(bgc3[:, tt, :]), ALU.mult, reads=[('ktok', pb), 'bgc'], writes=['kbg'])
            P.op('gpsimd', 'tensor_tensor', kdec[:, :, :], ktok[:, :, :], bc_h(ekd3[:, tt, :]), ALU.mult, reads=[('ktok', pb), 'ekd'], writes=[('kdec', pb)])
            yield
            pt, ptk = ntb()
            for h in range(8):
                P.tr(pt[:, h * 128:(h + 1) * 128], N1[:, h, :], ident_bf[:, :], [('N', 0, 0), ('N', 0, 1), 'ident_bf'], [ptk], signal=(h == 7))
            P.op('vector', 'tensor_copy', M1[:, :, :], t8(pt), reads=[ptk], writes=[('M', 0, 0), ('M', 0, 1)])
            P.op('vector', 'tensor_tensor', PTf[:, :, :], t8(pt), bc_m(ident_f[:, :]), ALU.add, reads=[ptk, 'ident_f'], writes=[('PTf', 0), ('PTf', 1)])
            P.op('scalar', 'copy', PTb[:, :, :], PTf[:, :, :], reads=[('PTf', 0), ('PTf', 1)], writes=['PTb'])
            yield
            pt, ptk = ntb()
            for h in range(8):
                P.tr(pt[:, h * 128:(h + 1) * 128], attn[:, h, :], ident_bf[:, :], [('attn', 0), ('attn', 1), 'ident_bf'], [ptk], signal=(h == 7))
            P.op('scalar', 'copy', attnT[:, :, :], t8(pt), reads=[ptk], writes=[('attnT', pb)])
            yield
            cur = 0
            for k in range(1, 6):
                N1, M1 = NM[cur]
                N2, M2 = NM[1 - cur]
                for hb in range(2):
                    ps, pk = nb()
                    for hl in range(4):
                        h = hb * 4 + hl
                        P.mm(ps[:, hl * 128:(hl + 1) * 128], M1[:, h, :], N1[:, h, :], True, True, [('N', cur, hb), ('M', cur, hb)], [pk])
                    P.op('scalar' if hb == 0 else 'vector', 'copy' if hb == 0 else 'tensor_copy', N2[:, hb * 4:hb * 4 + 4, :], b4(ps),
                         reads=[pk], writes=[('N', 1 - cur, hb)])
                    if k < 5:
                        ps2, pk2 = nb()
                        for hl in range(4):
                            h = hb * 4 + hl
                            P.mm(ps2[:, hl * 128:(hl + 1) * 128], N1[:, h, :], M1[:, h, :], True, True, [('N', cur, hb), ('M', cur, hb)], [pk2])
                        P.op('vector' if hb == 0 else 'scalar', 'tensor_copy' if hb == 0 else 'copy', M2[:, hb * 4:hb * 4 + 4, :], b4(ps2),
                             reads=[pk2], writes=[('M', 1 - cur, hb)])
                    yield
                for hb in range(2):
                    ps, pk = nb()
                    for hl in range(4):
                        h = hb * 4 + hl
                        P.mm(ps[:, hl * 128:(hl + 1) * 128], N2[:, h, :], PTb[:, h, :], True, True, [('N', 1 - cur, hb), 'PTb'], [pk])
                    P.op('vector', 'tensor_tensor', PTf[:, hb * 4:hb * 4 + 4, :], b4(ps), PTf[:, hb * 4:hb * 4 + 4, :], ALU.add,
                         reads=[pk, ('PTf', hb)], writes=[('PTf', hb)])
                P.op('scalar', 'copy', PTb[:, :, :], PTf[:, :, :], reads=[('PTf', 0), ('PTf', 1)], writes=['PTb'])
                cur = 1 - cur
                yield
            for hb in range(2):
                ps, pk = nb()
                for hl in range(4):
                    h = hb * 4 + hl
                    P.mm(ps[:, hl * 128:(hl + 1) * 128], PTb[:, h, :], vb[:, h, :], True, True, ['PTb', 'vb'], [pk])
                P.op('scalar', 'copy', uf[:, hb * 4:hb * 4 + 4, :], b4(ps), reads=[pk], writes=[('uf', pb, hb)])
                ps2, pk2 = nb()
                for hl in range(4):
                    h = hb * 4 + hl
                    P.mm(ps2[:, hl * 128:(hl + 1) * 128], kbg[:, h, :], PTb[:, h, :], True, True, ['PTb', 'kbg'], [pk2])
                P.op('vector', 'tensor_copy', wT[:, hb * 4:hb * 4 + 4, :], b4(ps2), reads=[pk2], writes=[('wT', pb, hb)])
                yield

        def scan(tt):
            pb = tt % 2
            tl = slice(tt * 128, (tt + 1) * 128)
            attnT, kdec, uf, wT, zs = attnT2[pb], kdec2[pb], uf2[pb], wT2[pb], zs2[tt % 3]
            qTt = qTt3[tt % 3]
            for c in range(2):
                rows = slice(64 * c, 64 * c + 64)
                for hb in range(2):
                    hs = slice(hb * 4, hb * 4 + 4)
                    ps, pk = nb()
                    for hl in range(4):
                        h = hb * 4 + hl
                        P.mm(ps[:, hl * 128:(hl + 1) * 128], wT[:, h, :], Sb[:, h, :], True, True, [('wT', pb, hb), 'Sb'], [pk])
                    P.op('vector', 'tensor_tensor', vn[rows, hs, :], uf[rows, hs, :], b4(ps)[rows, :, :], ALU.subtract,
                         reads=[pk, ('uf', pb, hb)], writes=[('vn', hb)])
                    psq, pkq = nb()
                    for hl in range(4):
                        h = hb * 4 + hl
                        P.mm(psq[:, hl * 128:(hl + 1) * 128], qTt[:, h, :], Sb[:, h, :], True, True, [('qTt', tt % 3), 'Sb'], [pkq])
                    P.op('vector', 'tensor_tensor', tmpo[rows, hs, :], b4(psq)[rows, :, :], bc_h(egc3[:, tt, :])[rows, hs, :], ALU.mult,
                         reads=[pkq, 'egc'], writes=[('tmpo', hb)])
                    yield
                for hb in range(2):
                    hs = slice(hb * 4, hb * 4 + 4)
                    psa, pka = nb()
                    for hl in range(4):
                        h = hb * 4 + hl
                        P.mm(psa[:, hl * 128:(hl + 1) * 128], attnT[rows, h, :], vn[rows, h, :], True, True, [('attnT', pb), ('vn', hb)], [pka])
                    P.op('vector', 'tensor_tensor', oall[rows, hs, :], b4(psa)[rows, :, :], tmpo[rows, hs, :], ALU.add,
                         reads=[pka, ('tmpo', hb)], writes=[('oall', hb, c)])
                P.op('gpsimd', 'tensor_tensor', Sf[:, :, :], Sf[:, :, :], bc_h(egl2[:, tt, c, :]), ALU.mult,
                     reads=[('Sf', 0), ('Sf', 1), ('egl2', c)], writes=[('Sf', 0), ('Sf', 1)])
                yield
                for hb in range(2):
                    hs = slice(hb * 4, hb * 4 + 4)
                    psk, pkk = nb()
                    for hl in range(4):
                        h = hb * 4 + hl
                        P.mm(psk[:, hl * 128:(hl + 1) * 128], kdec[rows, h, :], vn[rows, h, :], True, True, [('kdec', pb), ('vn', hb)], [pkk])
                    P.op('vector', 'tensor_tensor', Sf[:, hs, :], b4(psk), Sf[:, hs, :], ALU.add, reads=[pkk, ('Sf', hb)], writes=[('Sf', hb)])
                P.op('scalar', 'copy', Sb[:, :, :], Sf[:, :, :], reads=[('Sf', 0), ('Sf', 1)], writes=['Sb'])
                yield
            okeys = [('oall', hb, c) for hb in range(2) for c in range(2)]
            P.op('gpsimd', 'tensor_tensor', o2[:, :, :], oall[:, :, :], oall[:, :, :], ALU.mult, reads=okeys, writes=['o2'])
            P.op('vector', 'tensor_reduce', ssq[:, :], o2[:, :, :], AX.X, ALU.add, reads=['o2'], writes=['gssq'])
            P.op('scalar', 'activation', ssq[:, :], ssq[:, :], AF.Sqrt, bias=epsb[:, 0:1], scale=1.0 / 128, reads=['gssq', 'epsb'], writes=['gssq'])
            P.op('vector', 'reciprocal', ssq[:, :], ssq[:, :], reads=['gssq'], writes=['gssq'])
            yield
            P.op('vector', 'tensor_tensor', o2[:, :, :], oall[:, :, :], bc_h(ssq[:, :]), ALU.mult, reads=okeys + ['gssq'], writes=['o2'])
            P.op('gpsimd', 'tensor_tensor', o2[:, :, :], o2[:, :, :], nwb[:, :].rearrange('p (a b) -> p a b', a=8), ALU.mult,
                 reads=['o2', 'nwb'], writes=['o2'])
            P.op('vector', 'tensor_tensor', onb[:, :, :], o2[:, :, :], zs[:, :].rearrange('p (a b) -> p a b', a=8), ALU.mult,
                 reads=['o2', ('zs', tt % 3)], writes=['onb'])
            yield
            pt, ptk = ntb()
            for h in range(8):
                P.tr(pt[:, h * 128:(h + 1) * 128], onb[:, h, :], ident_bf[:, :], ['onb', 'ident_bf'], [ptk], signal=(h == 7))
            P.op('scalar', 'copy', ostg[:, :, :], t8(pt), reads=[ptk], writes=['ostg'])
            P.dma('sync', G['odnT_d'][:, :, tl].rearrange('h p t -> p h t'), ostg[:, :, :], ['ostg'], [])
            yield

        loads(0)
        run_streams([prep(0)])
        for tt in range(16):
            run_streams([prep(tt + 1) if tt + 1 < 16 else None, scan(tt)])
        P.flush()


def norm_transpose(P, nc, es1, G, src_d, w_d, dstT, rstd_pre=None):
    tb = G['tbanks']
    ident_bf = G['ident_bf']
    epsb = G['epsb']
    w1b = _mk(es1, nc, 'nw_b', [128, D], F32)
    P.dma('scalar', w1b[:], w_d[0:1, :].partition_broadcast(128), [], ['nw_b'])
    xrot = _rot(es1, nc, 'nxt', 2, [128, D], F32)
    hbrot = _rot(es1, nc, 'nhb', 2, [128, D], BF16)
    junk = _mk(es1, nc, 'njunk', [128, D], BF16)
    ssq = _mk(es1, nc, 'nssq', [128, 16], F32)
    rstd = rstd_pre if rstd_pre is not None else _mk(es1, nc, 'nrstd', [128, 16], F32)
    for tt in range(16):
        xt, xk = xrot.next()
        hb, hk = hbrot.next()
        P.dma('sync', xt[:], src_d[tt * 128:(tt + 1) * 128, :], [], [xk])
        if rstd_pre is None:
            P.op('scalar', 'activation', junk[:], xt[:], AF.Square, accum_out=ssq[:, tt:tt + 1], reads=[xk], writes=['njunk', ('nssq', tt)])
            P.op('scalar', 'activation', rstd[:, tt:tt + 1], ssq[:, tt:tt + 1], AF.Sqrt, bias=epsb[:, 0:1], scale=1.0 / D,
                 reads=[('nssq', tt), 'epsb'], writes=[('nrstd', tt)])
            P.op('vector', 'reciprocal', rstd[:, tt:tt + 1], rstd[:, tt:tt + 1], reads=[('nrstd', tt)], writes=[('nrstd', tt)])
        P.op('vector', 'scalar_tensor_tensor', hb[:], xt[:], rstd[:, tt:tt + 1], w1b[:], ALU.mult, ALU.mult,
             reads=[xk, ('nrstd', tt), 'nrstd_all', 'nw_b'], writes=[hk])
        for half in range(2):
            pt, pk = tb[half]
            for j in range(8):
                kc = half * 8 + j
                P.tr(pt[:, j * 128:(j + 1) * 128], hb[:, kc * 128:(kc + 1) * 128], ident_bf[:], [hk, 'ident_bf'], [pk], signal=(j == 7))
            dst = dstT[:, half * 8:(half + 1) * 8, tt * 128:(tt + 1) * 128]
            src = pt[:, :].rearrange('p (j c) -> p j c', j=8)
            if half == 0:
                P.op('scalar', 'copy', dst, src, reads=[pk], writes=[('nT', tt, half)])
            else:
                P.op('vector', 'tensor_copy', dst, src, reads=[pk], writes=[('nT', tt, half)])


def phase_tail(P, nc, G):
    banks = G['banks']
    bi = [0]

    def nb():
        b = banks[bi[0] % 6]
        bi[0] += 1
        return b
    qi = [0]

    def oq():
        qi[0] += 1
        return ('sync', 'scalar')[qi[0] % 2]
    epsb = G['epsb']
    x1_d, x2_d, actT_d, mgT_d = G['x1_d'], G['x2_d'], G['actT_d'], G['mgT_d']
    with ExitStack() as es:
        ssq1 = _mk(es, nc, 'ssq1', [128, 16, 4], F32)
        ssq2 = _mk(es, nc, 'ssq2', [128, 16, 4], F32)
        rstd2 = _mk(es, nc, 'rstd2', [128, 16], F32)
        rstdf = _mk(es, nc, 'rstdf', [128, 16], F32)
        with ExitStack() as e1:
            mixedT = _mk(e1, nc, 'mixedT', [128, 16, S], BF16)
            with ExitStack() as e2:
                oa = _mk(e2, nc, 'onsaT_s', [128, 8, S], BF16)
                ob = _mk(e2, nc, 'odnT_s', [128, 8, S], BF16)
                P.dma('sync', oa[:, :, :], G['onsaT_d'].ap().rearrange('h p t -> p h t'), [], ['oa'])
                P.dma('scalar', ob[:, :, :], G['odnT_d'].ap().rearrange('h p t -> p h t'), [], ['ob'])
                wr = [_rot(e2, nc, 'wup%d' % i, 2, [128, 8, 512], BF16) for i in range(2)]
                grot = _rot(e2, nc, 'mg', 4, [128, 512], BF16)
                trot = _rot(e2, nc, 'mt', 4, [128, 512], F32)
                Wn = G['w_up_nsa'].ap().rearrange('(kc p) n -> p kc n', p=128)
                Wd = G['w_up_dn'].ap().rearrange('(kc p) n -> p kc n', p=128)
                for cb in range(4):
                    c0 = cb * 512
                    w1, w1k = wr[0].next()
                    w2, w2k = wr[1].next()
                    P.dma('gpsimd', w1[:, :, :], Wn[:, :, c0:c0 + 512], [], [w1k])
                    P.dma('gpsimd', w2[:, :, :], Wd[:, :, c0:c0 + 512], [], [w2k])
                    for m in range(4):
                        n0 = c0 + m * 128
                        for t4 in range(4):
                            t0 = t4 * 512
                            g1, g1k = grot.next()
                            g2, g2k = grot.next()
                            P.dma('sync', g1[:, :], mgT_d[n0:n0 + 128, t0:t0 + 512], [], [g1k])
                            P.dma('sync', g2[:, :], mgT_d[2048 + n0:2048 + n0 + 128, t0:t0 + 512], [], [g2k])
                            p1, p1k = nb()
                            for kc in range(8):
                                P.mm(p1[:, :], w1[:, kc, m * 128:(m + 1) * 128], oa[:, kc, t0:t0 + 512], kc == 0, kc == 7, [w1k, 'oa'], [p1k])
                            p2, p2k = nb()
                            for kc in range(8):
                                P.mm(p2[:, :], w2[:, kc, m * 128:(m + 1) * 128], ob[:, kc, t0:t0 + 512], kc == 0, kc == 7, [w2k, 'ob'], [p2k])
                            ta, tak = trot.next()
                            tb_, tbk = trot.next()
                            P.op('scalar', 'activation', g1[:, :], g1[:, :], AF.Sigmoid, reads=[g1k], writes=[g1k])
                            P.op('scalar', 'activation', g2[:, :], g2[:, :], AF.Sigmoid, reads=[g2k], writes=[g2k])
                            P.op('vector', 'tensor_tensor', ta[:, :], p1[:, :], g1[:, :], ALU.mult, reads=[p1k, g1k], writes=[tak])
                            P.op('vector', 'tensor_tensor', tb_[:, :], p2[:, :], g2[:, :], ALU.mult, reads=[p2k, g2k], writes=[tbk])
                            P.op('vector', 'tensor_tensor', mixedT[:, n0 // 128, t0:t0 + 512], ta[:, :], tb_[:, :], ALU.add,
                                 reads=[tak, tbk], writes=[('mixedT', n0 // 128, t4)])
                P.flush()
            if 'mixedT' in G['dbg_t']:
                P.dma('sync', G['dbg_t']['mixedT'].ap().rearrange('(kc p) t -> p kc t', p=128), mixedT[:, :, :], [], [])
                P.flush()
            with ExitStack() as e2:
                xr = _rot(e2, nc, 'xres', 3, [128, 512], F32)
                orr = _rot(e2, nc, 'x1o', 3, [128, 512], F32)
                junk = _mk(e2, nc, 'junk1', [128, 512], BF16)

                def epi_res(src_d, dst_d, ssq):
                    def epi(ps, pk, info):
                        t0, c0 = info['t0'], info['c0']
                        xt, xk = xr.next()
                        o, ok = orr.next()
                        P.dma('sync', xt[:, :], src_d[t0:t0 + 128, c0:c0 + 512], [], [xk])
                        P.op('vector', 'tensor_tensor', o[:, :], ps, xt[:, :], ALU.add, reads=[pk, xk], writes=[ok])
                        P.op('scalar', 'activation', junk[:, :], o[:, :], AF.Square, accum_out=ssq[:, t0 // 128, c0 // 512:c0 // 512 + 1],
                             reads=[ok], writes=['junk1', ('ssq', t0 // 128, c0 // 512)])
                        P.dma('scalar', dst_d[t0:t0 + 128, c0:c0 + 512], o[:, :], [ok], [])
                    return epi
                Wo = G['w_o'].ap().rearrange('(kc p) n -> p kc n', p=128)
                blocks = [(c0, 512, 'tok', epi_res(G['x'], x1_d, ssq1)) for c0 in range(0, D, 512)]
                dense(P, nc, e2, 'wo', mixedT, 'mixedT_all', 16, Wo, blocks, banks)
                P.flush()

        def finish_rstd(ssq, rstd):
            P.op('vector', 'tensor_reduce', rstd[:, :], ssq[:, :, :], AX.X, ALU.add, reads=[], writes=['nrstd_all'])
            P.op('scalar', 'activation', rstd[:, :], rstd[:, :], AF.Sqrt, bias=epsb[:, 0:1], scale=1.0 / D, reads=['nrstd_all', 'epsb'], writes=['nrstd_all'])
            P.op('vector', 'reciprocal', rstd[:, :], rstd[:, :], reads=['nrstd_all'], writes=['nrstd_all'])
        e_ffn = es
        KF = DFF // 128
        wdr = _rot(es, nc, 'wdn', 2, [128, KF, 512], BF16)
        Wdn = G['w_ffn_down'].ap().rearrange('(kc p) n -> p kc n', p=128)
        wd_pre = []
        with ExitStack() as e1:
            h2T = _mk(e1, nc, 'h2T', [128, 16, S], BF16)
            with ExitStack() as e2:
                finish_rstd(ssq1, rstd2)
                norm_transpose(P, nc, e2, G, x1_d, G['norm2_w'], h2T, rstd_pre=rstd2)
                P.flush()
            for cb in range(2):
                wt, wk = wdr.next()
                for part in range(4):
                    P.dma('gpsimd', wt[:, part * 11:(part + 1) * 11, :], Wdn[:, part * 11:(part + 1) * 11, cb * 512:(cb + 1) * 512], [], [(wk, part)])
                wd_pre.append((wt, wk))
            with ExitStack() as e2:
                wr = [_rot(e2, nc, 'wgu%d' % i, 2, [128, 16, 256], BF16) for i in range(2)]
                sgr = _rot(e2, nc, 'sg', 3, [128, 512], F32)
                acr = _rot(e2, nc, 'acb', 3, [128, 512], BF16)
                Wg = G['w_ffn_gate'].ap().rearrange('(kc p) n -> p kc n', p=128)
                Wu = G['w_ffn_up'].ap().rearrange('(kc p) n -> p kc n', p=128)
                for fb2 in range(DFF // 256):
                    c0 = fb2 * 256
                    w1, w1k = wr[0].next()
                    w2, w2k = wr[1].next()
                    P.dma('gpsimd', w1[:, :, :], Wg[:, :, c0:c0 + 256], [], [w1k])
                    P.dma('gpsimd', w2[:, :, :], Wu[:, :, c0:c0 + 256], [], [w2k])
                    for m in range(2):
                        fb = fb2 * 2 + m
                        for t4 in range(4):
                            t0 = t4 * 512
                            p1, p1k = nb()
                            for kc in range(16):
                                P.mm(p1[:, :], w1[:, kc, m * 128:(m + 1) * 128], h2T[:, kc, t0:t0 + 512], kc == 0, kc == 15, [w1k, 'h2T'], [p1k])
                            p2, p2k = nb()
                            for kc in range(16):
                                P.mm(p2[:, :], w2[:, kc, m * 128:(m + 1) * 128], h2T[:, kc, t0:t0 + 512], kc == 0, kc == 15, [w2k, 'h2T'], [p2k])
                            sg, sgk = sgr.next()
                            ac, ack = acr.next()
                            P.op('scalar', 'activation', sg[:, :], p1[:, :], AF.Silu, reads=[p1k], writes=[sgk])
                            P.op('vector', 'tensor_tensor', ac[:, :], p2[:, :], sg[:, :], ALU.mult, reads=[p2k, sgk], writes=[ack])
                            P.dma('scalar', actT_d[t4 * 4:(t4 + 1) * 4, :, fb, :].rearrange('a p t -> p a t'),
                                  ac[:, :].rearrange('p (a t) -> p a t', a=4), [ack], [])
                P.flush()
        if True:
            e1 = e_ffn
            atr = _rot(e1, nc, 'actt', 3, [128, KF, 128], BF16)
            xr = _rot(e1, nc, 'x1res', 3, [128, 512], F32)
            orr = _rot(e1, nc, 'x2o', 3, [128, 512], F32)
            junk = _mk(e1, nc, 'junk2', [128, 512], BF16)
            wfb = _mk(e1, nc, 'wfb', [128, D], F32)
            P.dma('sync', wfb[:], G['norm_f_w'][0:1, :].partition_broadcast(128), [], ['wfb'])
            x2rot = _rot(e1, nc, 'x2t', 2, [128, 1536], F32)
            outrot = _rot(e1, nc, 'outt', 2, [128, D], F32)
            for cb in range(4):
                c0 = cb * 512
                if cb < 2:
                    wt, wk = wd_pre[cb]
                else:
                    wt, wk = wdr.next()
                    for part in range(4):
                        P.dma('gpsimd', wt[:, part * 11:(part + 1) * 11, :], Wdn[:, part * 11:(part + 1) * 11, c0:c0 + 512], [], [(wk, part)])
                wks = [(wk, part) for part in range(4)]
                pend = []
                at0, ak0 = atr.next()
                P.dma('sync', at0[:, :, :], actT_d[0, :, :, :], [], [ak0])
                pend.append((at0, ak0))
                for tt in range(16):
                    t0 = tt * 128
                    if tt + 1 < 16:
                        at1, ak1 = atr.next()
                        P.dma('sync', at1[:, :, :], actT_d[tt + 1, :, :, :], [], [ak1])
                        pend.append((at1, ak1))
                    at, ak = pend.pop(0)
                    ps, pk = nb()
                    for kc in range(KF):
                        P.mm(ps[:, :], at[:, kc, :], wt[:, kc, :], kc == 0, kc == KF - 1, wks + [ak], [pk])
                    xt, xk = xr.next()
                    o, ok = orr.next()
                    P.dma('sync', xt[:, :], x1_d[t0:t0 + 128, c0:c0 + 512], [], [xk])
                    P.op('vector', 'tensor_tensor', o[:, :], ps[:, :], xt[:, :], ALU.add, reads=[pk, xk], writes=[ok])
                    P.op('scalar', 'activation', junk[:, :], o[:, :], AF.Square, accum_out=ssq2[:, tt, cb:cb + 1],
                         reads=[ok], writes=['junk2', ('ssq2', tt, cb)])
                    if cb < 3:
                        P.dma('scalar', x2_d[t0:t0 + 128, c0:c0 + 512], o[:, :], [ok], [('x2', tt, cb)])
                    else:
                        P.op('vector', 'tensor_reduce', rstdf[:, tt:tt + 1], ssq2[:, tt, :], AX.X, ALU.add,
                             reads=[('ssq2', tt, c_) for c_ in range(4)], writes=[('rstdf', tt)])
                        P.op('scalar', 'activation', rstdf[:, tt:tt + 1], rstdf[:, tt:tt + 1], AF.Sqrt, bias=epsb[:, 0:1], scale=1.0 / D,
                             reads=[('rstdf', tt), 'epsb'], writes=[('rstdf', tt)])
                        P.op('vector', 'reciprocal', rstdf[:, tt:tt + 1], rstdf[:, tt:tt + 1], reads=[('rstdf', tt)], writes=[('rstdf', tt)])
                        x2t, x2k = x2rot.next()
                        ot, otk = outrot.next()
                        P.dma('sync', x2t[:, 0:1536], x2_d[t0:t0 + 128, 0:1536], [('x2', tt, c_) for c_ in range(3)], [x2k])
                        P.op('vector', 'scalar_tensor_tensor', ot[:, 0:1536], x2t[:, 0:1536], rstdf[:, tt:tt + 1], wfb[:, 0:1536], ALU.mult, ALU.mult,
                             reads=[x2k, ('rstdf', tt), 'wfb'], writes=[(otk, 0)])
                        P.op('vector', 'scalar_tensor_tensor', ot[:, 1536:2048], o[:, :], rstdf[:, tt:tt + 1], wfb[:, 1536:2048], ALU.mult, ALU.mult,
                             reads=[ok, ('rstdf', tt), 'wfb'], writes=[(otk, 1)])
                        P.dma('scalar', G['out_d'][t0:t0 + 128, :], ot[:, :], [(otk, 0), (otk, 1)], [])
            P.flush()
        if False:
            finish_rstd(ssq2, rstdf)
            wfb = _mk(e1, nc, 'wfb', [128, D], F32)
            P.dma('scalar', wfb[:], G['norm_f_w'][0:1, :].partition_broadcast(128), [], ['wfb'])
            xr = _rot(e1, nc, 'x2t', 2, [128, D], F32)
            orr = _rot(e1, nc, 'outt', 2, [128, D], F32)
            for tt in range(16):
                xt, xk = xr.next()
                o, ok = orr.next()
                P.dma('sync', xt[:, :], x2_d[tt * 128:(tt + 1) * 128, :], [], [xk])
                P.op('vector', 'scalar_tensor_tensor', o[:, :], xt[:, :], rstdf[:, tt:tt + 1], wfb[:, :], ALU.mult, ALU.mult,
                     reads=[xk, 'nrstd_all', 'wfb'], writes=[ok])
                P.dma('scalar', G['out_d'][tt * 128:(tt + 1) * 128, :], o[:, :], [ok], [])
            P.flush()


def build(dbg=None, stop_after=99):
    nc = bass.Bass('TRN2', target_bir_lowering=False)
    P = Prog(nc)
    dbg = dbg or []

    def din(name, shape, dt=F32):
        return nc.dram_tensor(name, list(shape), dt, kind='ExternalInput')

    def dscr(name, shape, dt):
        return nc.dram_tensor(name, list(shape), dt)

    x = din('x', [S, D])
    norm1_w = din('norm1_w', [1, D])
    w_in = din('w_in', [D, NIN])
    ropec = din('ropec', [128, S])
    ropes = din('ropes', [128, S])
    ident_bf_d = din('ident_bf', [128, 128], BF16)
    ident_f_d = din('ident_f', [128, 128])
    out_d = nc.dram_tensor('out', [S, D], F32, kind='ExternalOutput')
    G = {}
    for nme, shape, dt in (('maskC', [128, S], BF16), ('ov', [128, 32], BF16), ('E', [32, S], BF16),
                           ('masks8', [128, 8, 512], BF16), ('mb8', [128, 8, 512], BF16), ('maskCb', [128, S], BF16), ('tinyrow', [1, 128], BF16), ('onesrow', [1, 512], BF16), ('sel24', [24, 24 * 128], F32),
                           ('fm_mul', [128, 16, 32], F32), ('fm_add', [128, 16, 32], F32), ('ones_bf', [128, 128], BF16),
                           ('cmp_w1_k', [4096, 128], F32), ('cmp_w2_k', [128, 128], F32),
                           ('cmp_w1_v', [4096, 128], F32), ('cmp_w2_v', [128, 128], F32),
                           ('peT_k', [128, 32], F32), ('peT_v', [128, 32], F32),
                           ('tri', [128, 128], F32), ('sellast', [128, 128], F32), ('sel63', [128, 128], F32),
                           ('sel127', [128, 128], F32), ('mblow', [128, 128], F32), ('strictm', [128, 128], F32),
                           ('ones_f', [128, 128], F32), ('negmask', [128, 128], F32), ('convw', [128, 24, 4], F32), ('alog_rep', [128, 128], F32),
                           ('dtb_rep', [128, 128], F32), ('nw_rep', [1, 1024], F32),
                           ('w_up_nsa', [1024, D], F32), ('w_up_dn', [1024, D], F32), ('w_o', [D, D], F32),
                           ('norm2_w', [1, D], F32), ('w_ffn_gate', [D, DFF], F32), ('w_ffn_up', [D, DFF], F32),
                           ('w_ffn_down', [DFF, D], F32), ('norm_f_w', [1, D], F32)):
        G[nme] = din(nme, shape, dt)

    qT_d = dscr('qT_d', [8, 128, S], BF16)
    kvT_d = dscr('kvT_d', [6, 2, 128, S], BF16)
    vtok_d = dscr('vtok_d', [2, S, 256], BF16)
    gT_d = dscr('gT_d', [24, S], F32)
    dnqkvT_d = dscr('dnqkvT_d', [3072, S], F32)
    dnz_d = dscr('dnz_d', [S, 1024], F32)
    dnab_d = dscr('dnab_d', [S, 16], F32)
    mgT_d = dscr('mgT_d', [4096, S], BF16)
    onsaT_d = dscr('onsaT_d', [8, 128, S], BF16)
    odnT_d = dscr('odnT_d', [8, 128, S], BF16)
    ktok_d = dscr('ktok_d', [8, S, 128], BF16)
    gcT_d = dscr('gcT_d', [16, 1, 1024], F32)
    gqT_d = dscr('gqT_d', [8, 128, S], BF16)
    gkT_d = dscr('gkT_d', [8, 128, S], BF16)
    x1_d = dscr('x1_d', [S, D], F32)
    x2_d = dscr('x2_d', [S, D], F32)
    actT_d = dscr('actT_d', [16, 128, DFF // 128, 128], BF16)
    vtok2_d = dscr('vtok2_d', [8, S, 128], BF16)

    dbg_t = {}
    for nme, shape, dt in dbg:
        dbg_t[nme] = nc.dram_tensor('dbg_' + nme, list(shape), dt, kind='ExternalOutput')

    banks = []
    for b in range(6):
        t = nc.alloc_psum_tensor('psb%d' % b, [128, 512], F32)
        banks.append((t, ('ps', b)))
    tbanks = []
    for b in range(2):
        t = nc.alloc_psum_tensor('pst%d' % b, [128, 1024], BF16)
        tbanks.append((t, ('pst', b)))

    with ExitStack() as gs:
        ident_bf = _mk(gs, nc, 'ident_bf_s', [128, 128], BF16)
        ident_f = _mk(gs, nc, 'ident_f_s', [128, 128], F32)
        P.dma('sync', ident_bf[:], ident_bf_d[:, :], [], ['ident_bf'])
        P.dma('sync', ident_f[:], ident_f_d[:, :], [], ['ident_f'])
        epsb = _mk(gs, nc, 'epsb', [128, 1], F32)
        P.add('vector', lambda e: e.memset(epsb[:], EPS), [], ['epsb'])

        G.update(banks=banks, tbanks=tbanks, ident_f=ident_f, ident_bf=ident_bf, dbg_t=dbg_t, qT_d=qT_d, kvT_d=kvT_d,
                 vtok_d=vtok_d, gT_d=gT_d, onsaT_d=onsaT_d, odnT_d=odnT_d, ktok_d=ktok_d, vtok2_d=vtok2_d,
                 dnqkvT_d=dnqkvT_d, dnz_d=dnz_d, dnab_d=dnab_d, epsb=epsb, gcT_d=gcT_d, gqT_d=gqT_d, gkT_d=gkT_d)
        with ExitStack() as es:
            hT = _mk(es, nc, 'hT', [128, 16, S], BF16)
            with ExitStack() as es1:
                w1b = _mk(es1, nc, 'w1b', [128, D], F32)
                P.dma('scalar', w1b[:], norm1_w[0:1, :].partition_broadcast(128), [], ['w1b'])
                xrot = _rot(es1, nc, 'xt', 2, [128, D], F32)
                hbrot = _rot(es1, nc, 'hb', 2, [128, D], BF16)
                junk = _mk(es1, nc, 'junk', [128, D], BF16)
                ssq = _mk(es1, nc, 'ssq', [128, 16], F32)
                rstd = _mk(es1, nc, 'rstd', [128, 16], F32)
                for tt in range(16):
                    xt, xk = xrot.next()
                    hb, hk = hbrot.next()
                    P.dma('sync', xt[:], x[tt * 128:(tt + 1) * 128, :], [], [xk])
                    P.add('scalar', lambda e, xt=xt, tt=tt: e.activation(
                        junk[:], xt[:], AF.Square, accum_out=ssq[:, tt:tt + 1]),
                        reads=[xk], writes=['junk', ('ssq', tt)])
                    P.add('scalar', lambda e, tt=tt: e.activation(
                        rstd[:, tt:tt + 1], ssq[:, tt:tt + 1], AF.Sqrt, bias=epsb[:, 0:1], scale=1.0 / D),
                        reads=[('ssq', tt), 'epsb'], writes=[('rstd', tt)])
                    P.add('vector', lambda e, tt=tt: e.reciprocal(rstd[:, tt:tt + 1], rstd[:, tt:tt + 1]),
                        reads=[('rstd', tt)], writes=[('rstd', tt)])
                    P.add('vector', lambda e, xt=xt, hb=hb, tt=tt: e.scalar_tensor_tensor(
                        hb[:], xt[:], rstd[:, tt:tt + 1], w1b[:], ALU.mult, ALU.mult),
                        reads=[xk, ('rstd', tt), 'w1b'], writes=[hk])
                    for half in range(2):
                        pt, pk = tbanks[half]
                        for j in range(8):
                            kc = half * 8 + j
                            P.tr(pt[:, j * 128:(j + 1) * 128], hb[:, kc * 128:(kc + 1) * 128], ident_bf[:],
                                 reads=[hk, 'ident_bf'], writes=[pk], signal=(j == 7))
                        eng = 'scalar' if half == 0 else 'vector'
                        dst = hT[:, half * 8:(half + 1) * 8, tt * 128:(tt + 1) * 128]
                        src = pt[:, :].rearrange('p (j c) -> p j c', j=8)
                        if eng == 'scalar':
                            P.add('scalar', lambda e, dst=dst, src=src: e.copy(dst, src),
                                  reads=[pk], writes=[('hT', tt, half)])
                        else:
                            P.add('vector', lambda e, dst=dst, src=src: e.tensor_copy(dst, src),
                                  reads=[pk], writes=[('hT', tt, half)])
                P.flush()
            if 'hT' in dbg_t:
                P.dma('sync', dbg_t['hT'].ap().rearrange('(kc p) t -> p kc t', p=128), hT[:], [], [])
                P.flush()

            if stop_after >= 2:
                with ExitStack() as es2:
                    stf = _rot(es2, nc, 'stf', 4, [128, 512], F32)
                    stb = _rot(es2, nc, 'stb', 4, [128, 512], BF16)
                    wrot_in = _rot(es2, nc, 'inproj_w', 3, [128, 16, 512], BF16)
                    er = ExitStack()
                    cc = _mk(er, nc, 'cc', [128, S], F32)
                    ss = _mk(er, nc, 'ss', [128, S], F32)
                    P.dma('sync', cc[:], ropec[:, :], [], ['cc'])
                    P.dma('scalar', ss[:], ropes[:, :], [], ['ss'])
                    stf2 = _rot(er, nc, 'stf2', 3, [128, 512], F32)
                    oq = ['sync', 'scalar']
                    oqi = [0]

                    def outq():
                        oqi[0] += 1
                        return oq[oqi[0] % 2]

                    def epi_rope(dst_fn):
                        def epi(ps, pk, info):
                            t0 = info['t0']
                            a, ak = stf.next()
                            b, bk = stf2.next()
                            o, ok = stb.next()
                            P.add('vector', lambda e: e.tensor_tensor(a[:], ps, cc[:, t0:t0 + 512], ALU.mult),
                                  reads=[pk, 'cc'], writes=[ak])
                            P.add('vector', lambda e: e.tensor_tensor(b[0:64, :], ps[64:128, :], ss[0:64, t0:t0 + 512], ALU.mult),
                                  reads=[pk, 'ss'], writes=[bk])
                            P.add('vector', lambda e: e.tensor_tensor(b[64:128, :], ps[0:64, :], ss[64:128, t0:t0 + 512], ALU.mult),
                                  reads=[pk, 'ss'], writes=[(bk, 1)])
                            P.add('vector', lambda e: e.tensor_tensor(o[:], a[:], b[:], ALU.add),
                                  reads=[ak, bk, (bk, 1)], writes=[ok])
                            P.dma(outq(), dst_fn(info), o[:], reads=[ok], writes=[])
                        return epi

                    def epi_copy_feat(dst_fn, dt, func=None, key_fn=None):
                        def epi(ps, pk, info):
                            mw = info['mw']
                            if dt == BF16:
                                o, ok = stb.next()
                            else:
                                o, ok = stf.next()
                            if func is None:
                                P.add('scalar', lambda e: e.copy(o[0:mw, :], ps), reads=[pk], writes=[ok])
                            else:
                                P.add('scalar', lambda e: e.activation(o[0:mw, :], ps, func), reads=[pk], writes=[ok])
                            P.dma(outq(), dst_fn(info), o[0:mw, :], reads=[ok], writes=([key_fn(info)] if key_fn else []))
                        return epi

                    def epi_copy_tok(dst_fn, dt, func=None):
                        def epi(ps, pk, info):
                            nw = info['nw']
                            if dt == BF16:
                                o, ok = stb.next()
                            else:
                                o, ok = stf.next()
                            if func is None:
                                P.add('scalar', lambda e: e.copy(o[:, 0:nw], ps), reads=[pk], writes=[ok])
                            else:
                                P.add('scalar', lambda e: e.activation(o[:, 0:nw], ps, func), reads=[pk], writes=[ok])
                            P.dma(outq(), dst_fn(info), o[:, 0:nw], reads=[ok], writes=[])
                        return epi

                    blocks_q = []
                    for c0 in range(0, 1024, 512):
                        blocks_q.append((c0, 512, 'feat', epi_rope(
                            lambda info: qT_d[info['c0'] // 128, :, info['t0']:info['t0'] + 512])))
                    for i in range(6):
                        base = 1024 + i * 256
                        if i in (0, 2, 4):
                            blocks_q.append((base, 256, 'feat', epi_rope(
                                lambda info, i=i, base=base: kvT_d[i, (info['c0'] - base) // 128, :, info['t0']:info['t0'] + 512])))
                        elif i == 1:
                            blocks_q.append((base, 256, 'feat', epi_copy_feat(
                                lambda info, i=i, base=base: kvT_d[i, (info['c0'] - base) // 128, :, info['t0']:info['t0'] + 512], BF16)))
                        else:
                            blocks_q.append((base, 256, 'tok', epi_copy_tok(
                                lambda info, i=i: vtok_d[(i - 3) // 2, info['t0']:info['t0'] + 128, :], BF16)))
                    blocks_q.append((2560, 24, 'feat', epi_copy_feat(
                        lambda info: gT_d[0:24, info['t0']:info['t0'] + 512], F32, AF.Sigmoid)))
                    blocks_dn = []
                    for c0 in range(2584, 5656, 512):
                        blocks_dn.append((c0, 512, 'feat', epi_copy_feat(
                            lambda info: dnqkvT_d[info['c0'] - 2584:info['c0'] - 2584 + 128, info['t0']:info['t0'] + 512], F32,
                            key_fn=lambda info: ('dnqkvT', (info['c0'] - 2584) // 128, info['t0'] // 512))))
                    blocks_m = []
                    blocks_mg = []
                    for c0 in range(6696, 10792, 512):
                        blocks_mg.append((c0, 512, 'feat', epi_copy_feat(
                            lambda info: mgT_d[info['c0'] - 6696:info['c0'] - 6696 + 128, info['t0']:info['t0'] + 512], BF16)))
                    for c0 in range(5656, 6680, 512):
                        blocks_m.append((c0, 512, 'tok', epi_copy_tok(
                            lambda info: dnz_d[info['t0']:info['t0'] + 128, info['c0'] - 5656:info['c0'] - 5656 + 512], F32, AF.Silu)))
                    blocks_m.append((6680, 16, 'tok', epi_copy_tok(
                        lambda info: dnab_d[info['t0']:info['t0'] + 128, :], F32)))
                    blocks_m = blocks_m + blocks_mg
                    Wv = w_in.ap().rearrange('(kc p) n -> p kc n', p=128)
                    dense(P, nc, es2, 'inproj', hT, 'hTall', 16, Wv, blocks_dn + blocks_q, banks, wrot=wrot_in)
                    P.flush()
                    er.close()
                    with ExitStack() as ea:
                        gen = dense_gen(P, nc, es2, 'inproj', hT, 'hTall', 16, Wv, blocks_m, banks, wrot=wrot_in)
                        run_streams([gen, gdn_stepA_gen(P, nc, ea, G)])
                        P.flush()

        import os as _os
        if stop_after >= 3 and not _os.environ.get('SKIP_NSA'):
            phase_nsa(P, nc, G)
        if stop_after >= 4 and not _os.environ.get('SKIP_GDN'):
            phase_gdn(P, nc, G)
        G.update(x1_d=x1_d, x2_d=x2_d, actT_d=actT_d, mgT_d=mgT_d, x=x, out_d=out_d)
        if stop_after >= 5:
            phase_tail(P, nc, G)

        for nme, src in (('qT', qT_d), ('kvT', kvT_d), ('vtok', vtok_d), ('gT', gT_d), ('dnqkvT', dnqkvT_d),
                         ('dnz', dnz_d), ('dnab', dnab_d), ('mgT', mgT_d), ('onsaT', onsaT_d), ('odnT', odnT_d), ('x1', x1_d), ('x2', x2_d)):
            if nme in dbg_t:
                P.dma('sync', dbg_t[nme].ap(), src.ap(), [], [])
        P.flush()
        print('PROG stats', P.stats)
    return nc


_CACHE = {}


def make_shared_inputs(inputs):
    m = dict(host_consts())
    g = lambda k: np.ascontiguousarray(np.asarray(inputs[k], dtype=np.float32))
    m['norm1_w'] = g('norm1_w').reshape(1, D)
    m['w_in'] = g('w_in').reshape(D, NIN)
    for nm in ('k', 'v'):
        m['cmp_w1_' + nm] = g('cmp_w1_' + nm).reshape(4096, 128)
        m['cmp_w2_' + nm] = g('cmp_w2_' + nm).reshape(128, 128)
        m['peT_' + nm] = np.ascontiguousarray(g('cmp_pe_' + nm).reshape(32, 128).T)
    m['convw'] = np.ascontiguousarray(g('conv_w').reshape(4, 24, 128).transpose(2, 1, 0))
    m['alog_rep'] = np.ascontiguousarray(np.broadcast_to(np.tile(g('a_log').reshape(8), 16)[None, :], (128, 128)))
    m['dtb_rep'] = np.ascontiguousarray(np.broadcast_to(np.tile(g('dt_bias').reshape(8), 16)[None, :], (128, 128)))
    m['nw_rep'] = np.ascontiguousarray(np.tile(g('dn_norm_w').reshape(128), 8)[None, :])
    m['w_up_nsa'] = g('w_up_nsa').reshape(1024, D)
    m['w_up_dn'] = g('w_up_dn').reshape(1024, D)
    m['w_o'] = g('w_o').reshape(D, D)
    m['norm2_w'] = g('norm2_w').reshape(1, D)
    m['w_ffn_gate'] = g('w_ffn_gate').reshape(D, DFF)
    m['w_ffn_up'] = g('w_ffn_up').reshape(D, DFF)
    m['w_ffn_down'] = g('w_ffn_down').reshape(DFF, D)
    m['norm_f_w'] = g('norm_f_w').reshape(1, D)
    return m


def kernel(**inputs):
    if 'nc' not in _CACHE:
        _CACHE['nc'] = build()
    nc = _CACHE['nc']
    shared = make_shared_inputs(inputs)
    x = np.asarray(inputs['x'], dtype=np.float32)
    B = x.shape[0]
    in_maps = []
    for b in range(B):
        m = dict(shared)
        m['x'] = np.ascontiguousarray(x[b])
        in_maps.append(m)
    res = run_bass_kernel_spmd(nc, in_maps, core_ids=list(range(B)))
    return np.stack([np.asarray(r['out'], dtype=np.float32) for r in res.results], axis=0)
```

```python
import math
from contextlib import ExitStack

import numpy as np
import ml_dtypes
import concourse.bass as bass
import concourse.mybir as mybir
from concourse.bass_utils import run_bass_kernel_spmd

F32 = mybir.dt.float32
BF16 = mybir.dt.bfloat16
AF = mybir.ActivationFunctionType
ALU = mybir.AluOpType
AX = mybir.AxisListType

S = 2048
D = 2048
NIN = 10792
DFF = 5632
EPS = 1e-6
SAME_ENGINE_SYNC = True


class Prog:
    ENGS = ('tensor', 'vector', 'scalar', 'gpsimd', 'sync')
    DMAQ = ('sync', 'scalar', 'gpsimd')
    NDS = 6

    def __init__(self, nc):
        self.nc = nc
        self.ops = {e: [] for e in self.ENGS}
        self.state = {}
        self.esem = {e: nc.alloc_semaphore('s_' + e) for e in self.ENGS}
        self.dsem = {e: [nc.alloc_semaphore('d_%s_%d' % (e, i)) for i in range(self.NDS)] for e in self.DMAQ}
        self.dcnt = {e: 0 for e in self.DMAQ}
        self.dlast = {}
        self.base = {e: 0 for e in self.ENGS}
        self.known_c = {e: {} for e in self.ENGS}
        self.known_d = {e: {} for e in self.ENGS}
        self.stats = {}

    def add(self, eng, fn, reads=(), writes=(), signal=True, dma=False):
        idx = len(self.ops[eng])
        deps = []
        for k in reads:
            st = self.state.get(k)
            if st is not None and st['w'] is not None:
                deps.append(st['w'])
        for k in writes:
            st = self.state.get(k)
            if st is not None:
                if st['w'] is not None:
                    deps.append(st['w'])
                deps.extend(st['rc'].values())
                deps.extend(st['rd'])
        slot = None
        if dma:
            n = self.dcnt[eng]
            self.dcnt[eng] += 1
            slot = n % self.NDS
            me = ('d', (eng, slot), 16 * (n // self.NDS + 1))
            prev = self.dlast.get((eng, slot))
            if prev is not None:
                deps.append(prev)
            self.dlast[(eng, slot)] = me
        else:
            me = ('c', eng, idx)
        self.ops[eng].append(dict(fn=fn, deps=deps, signal=signal, dma=dma, slot=slot))
        for k in reads:
            st = self.state.setdefault(k, dict(w=None, rc={}, rd=[]))
            if dma:
                st['rd'].append(me)
            else:
                st['rc'][eng] = me
        for k in writes:
            self.state[k] = dict(w=me, rc={}, rd=[])

    def mm(self, out, lhsT, rhs, start, stop, reads, writes, signal=None):
        self.add('tensor', lambda e: e.matmul(out, lhsT, rhs, start=start, stop=stop),
                 reads, writes, signal=stop if signal is None else signal)

    def tr(self, out, in_, ident, reads, writes, signal=True):
        self.add('tensor', lambda e: e.transpose(out, in_, ident), reads, writes, signal=signal)

    def op(self, eng, meth, *args, reads=(), writes=(), **kw):
        self.add(eng, lambda e: getattr(e, meth)(*args, **kw), reads, writes)

    def dma(self, q, out, in_, reads, writes):
        self.add(q, lambda e: e.dma_start(out=out, in_=in_), reads, writes, dma=True)

    def barrier(self):
        deps = []
        for e in self.ENGS:
            for j in range(len(self.ops[e]) - 1, -1, -1):
                op = self.ops[e][j]
                if op['fn'] is not None and not op['dma']:
                    assert op['signal'], 'last op on %s before barrier must signal' % e
                    deps.append(('c', e, j))
                    break
        deps.extend(self.dlast.values())
        for e in self.ENGS:
            self.ops[e].append(dict(fn=None, deps=list(deps), signal=False, dma=False, slot=None))
        self.state = {}

    def flush(self):
        self.barrier()
        ops = self.ops
        cnt = {}
        nxt = {}
        for e in self.ENGS:
            c = self.base[e]
            cl = []
            for op in ops[e]:
                if op['fn'] is not None and not op['dma'] and op['signal']:
                    c += 1
                cl.append(c)
            cnt[e] = cl
            nl = [None] * len(ops[e])
            nx = None
            for j in range(len(ops[e]) - 1, -1, -1):
                op = ops[e][j]
                if op['fn'] is not None and not op['dma'] and op['signal']:
                    nx = cl[j]
                nl[j] = nx
            nxt[e] = nl
        self._nxt = nxt
        with self.nc.Block() as block:
            @block.tensor
            def _(e):
                self.emit('tensor', e)

            @block.vector
            def _(e):
                self.emit('vector', e)

            @block.scalar
            def _(e):
                self.emit('scalar', e)

            @block.gpsimd
            def _(e):
                self.emit('gpsimd', e)

            @block.sync
            def _(e):
                self.emit('sync', e)
        for e in self.ENGS:
            if cnt[e]:
                self.base[e] = cnt[e][-1]
            self.ops[e] = []

    def emit(self, ename, eng):
        ops = self.ops
        nxt = self._nxt
        known_c = self.known_c[ename]
        known_d = self.known_d[ename]
        n_wait = 0
        for i, op in enumerate(ops[ename]):
            for d in op['deps']:
                if d[0] == 'c':
                    _, f, j = d
                    if f == ename:
                        if ename == 'tensor' or not SAME_ENGINE_SYNC or j >= i:
                            continue
                        if not ops[f][j]['signal']:
                            continue
                    need = nxt[f][j]
                    assert need is not None, 'dependency on %s op %d never signals' % (f, j)
                    if known_c.get(f, 0) >= need:
                        continue
                    eng.wait_ge(self.esem[f], need)
                    known_c[f] = need
                    n_wait += 1
                else:
                    _, key, val = d
                    if known_d.get(key, 0) >= val:
                        continue
                    eng.wait_ge(self.dsem[key[0]][key[1]], val)
                    known_d[key] = val
                    n_wait += 1
            if op['fn'] is None:
                continue
            ins = op['fn'](eng)
            if op['dma']:
                ins.then_inc(self.dsem[ename][op['slot']], 16)
            elif op['signal']:
                ins.then_inc(self.esem[ename], 1)
        self.stats[ename] = self.stats.get(ename, 0) + len(ops[ename])
        self.stats[ename + '_w'] = self.stats.get(ename + '_w', 0) + n_wait


class Rot:
    def __init__(self, tiles, name):
        self.tiles = tiles
        self.name = name
        self.i = 0

    def next(self):
        j = self.i % len(self.tiles)
        self.i += 1
        return self.tiles[j], (self.name, j)


_UID = [0]


def _mk(es, nc, name, shape, dt):
    _UID[0] += 1
    return es.enter_context(nc.sbuf_tensor('%s_u%d' % (name, _UID[0]), shape, dt))


def _rot(es, nc, name, n, shape, dt):
    return Rot([_mk(es, nc, '%s%d' % (name, i), shape, dt) for i in range(n)], name)


def host_consts():
    c = {}
    inv = 1.0 / (10000.0 ** (np.arange(0, 128, 2, dtype=np.float32) / 128.0))
    ang = np.arange(S, dtype=np.float32)[:, None] * inv[None, :].astype(np.float32)
    ang = ang.astype(np.float32)
    cos = np.cos(ang).astype(np.float32).T
    sin = np.sin(ang).astype(np.float32).T
    c['ropec'] = np.ascontiguousarray(np.concatenate([cos, cos], 0))
    c['ropes'] = np.ascontiguousarray(np.concatenate([-sin, sin], 0))
    c['ident_bf'] = np.eye(128, dtype=np.float32).astype(ml_dtypes.bfloat16)
    c['ident_f'] = np.eye(128, dtype=np.float32)
    bf = ml_dtypes.bfloat16
    cc_ = np.arange(128)[:, None]
    qq = np.arange(S)[None, :]
    c['maskC'] = ((16 * cc_ + 31 <= qq) & (cc_ < 127)).astype(np.float32).astype(bf)
    cs = np.arange(128)[:, None] * 16
    ssb = np.arange(32)[None, :] * 64
    ov = np.clip(np.minimum(cs + 32, ssb + 64) - np.maximum(cs, ssb), 0, None) / 32.0
    ov[127] = 0
    c['ov'] = ov.astype(np.float32).astype(bf)
    c['E'] = (np.arange(S)[None, :] // 64 == np.arange(32)[:, None]).astype(np.float32).astype(bf)
    k_ = np.arange(128)[:, None]
    q_ = np.arange(512)[None, :]
    m8 = np.zeros((128, 8, 512), np.float32)
    for r in range(-4, 4):
        diff = q_ - (128 * r + k_)
        m8[:, r + 4, :] = ((diff >= 0) & (diff < 512))
    c['masks8'] = m8.astype(bf)
    c['mb8'] = ((m8 - 1.0) * 30000.0).astype(bf)
    c['maskCb'] = ((c['maskC'].astype(np.float32) - 1.0) * 30000.0).astype(bf)
    c['tinyrow'] = np.full((1, 128), 1e-30, np.float32).astype(bf)
    c['onesrow'] = np.ones((1, 512), np.float32).astype(bf)
    sel = np.zeros((24, 24, 128), np.float32)
    for i in range(24):
        sel[i, i, :] = 1
    c['sel24'] = sel.reshape(24, 24 * 128)
    t_ = np.arange(S)[:, None]
    j_ = np.arange(32)[None, :]
    blk = t_ // 64
    forced = (j_ == 0) | (j_ == blk) | (j_ == blk - 1)
    fut = j_ > blk
    fm_mul = (~(forced | fut)).astype(np.float32)
    fm_add = np.where(forced, 1e9, np.where(fut, -1e9, 0.0)).astype(np.float32)
    c['fm_mul'] = np.ascontiguousarray(fm_mul.reshape(16, 128, 32).transpose(1, 0, 2))
    c['fm_add'] = np.ascontiguousarray(fm_add.reshape(16, 128, 32).transpose(1, 0, 2))
    c['ones_bf'] = np.ones((128, 128), np.float32).astype(bf)
    p_ = np.arange(128)[:, None]
    f_ = np.arange(128)[None, :]
    same = (p_ // 64) == (f_ // 64)
    c['tri'] = ((p_ <= f_) & same).astype(np.float32)
    c['sellast'] = (p_ == 64 * (f_ // 64) + 63).astype(np.float32)
    c['sel63'] = np.broadcast_to(p_ == 63, (128, 128)).astype(np.float32).copy()
    c['sel127'] = np.broadcast_to(p_ == 127, (128, 128)).astype(np.float32).copy()
    c['mblow'] = np.where((f_ <= p_) & same, 0.0, 1e5).astype(np.float32)
    c['strictm'] = ((f_ < p_) & same).astype(np.float32)
    c['ones_f'] = np.ones((128, 128), np.float32)
    c['negmask'] = np.where((f_ <= p_) & same, 0.0, -1e5).astype(np.float32)
    return c


def dense_gen(P, nc, es, name, aT, aT_key, KC, Wv, blocks, banks, T=S, wbufs=3, wcols=512, wrot=None):
    if wrot is None:
        wrot = _rot(es, nc, name + '_w', wbufs, [128, KC, wcols], BF16)
    bi = [0]

    def nextbank():
        b = banks[bi[0] % len(banks)]
        bi[0] += 1
        return b

    for (c0, ncols, mode, epi) in blocks:
        wt, wk = wrot.next()
        P.dma('gpsimd', wt[:, :, 0:ncols], Wv[:, :, c0:c0 + ncols], reads=[], writes=[wk])
        if mode == 'feat':
            for m0 in range(0, ncols, 128):
                mw = min(128, ncols - m0)
                for t0 in range(0, T, 512):
                    ps, pk = nextbank()
                    for kc in range(KC):
                        P.mm(ps[0:mw, 0:512], wt[:, kc, m0:m0 + mw], aT[:, kc, t0:t0 + 512],
                             start=(kc == 0), stop=(kc == KC - 1), reads=[wk, aT_key], writes=[pk])
                    epi(ps[0:mw, 0:512], pk, dict(c0=c0 + m0, mw=mw, t0=t0))
                    yield
        else:
            for t0 in range(0, T, 128):
                ps, pk = nextbank()
                for kc in range(KC):
                    P.mm(ps[:, 0:ncols], aT[:, kc, t0:t0 + 128], wt[:, kc, 0:ncols],
                         start=(kc == 0), stop=(kc == KC - 1), reads=[wk, aT_key], writes=[pk])
                epi(ps[:, 0:ncols], pk, dict(c0=c0, nw=ncols, t0=t0))
                yield


def dense(*a, **kw):
    for _ in dense_gen(*a, **kw):
        pass


def run_streams(gens):
    gens = [g for g in gens if g is not None]
    while gens:
        for g in list(gens):
            try:
                next(g)
            except StopIteration:
                gens.remove(g)


def gdn_stepA_gen(P, nc, ea, G):
    tb = G['tbanks']
    sbank = (tb[0][0][:, :].bitcast(F32), tb[0][1])
    tbank = tb[1]
    ident_bf = G['ident_bf']
    epsb = G['epsb']

    def t8(ps):
        return ps[:, :].rearrange('p (a b) -> p a b', a=8)
    convw = _mk(ea, nc, 'convw', [128, 24, 4], F32)
    P.dma('sync', convw[:], G['convw'][:, :, :], [], ['convw'])
    ones_b = _mk(ea, nc, 'ones_bA', [128, 128], BF16)
    P.dma('sync', ones_b[:], G['ones_bf'][:, :], [], ['ones_bA'])
    xprot = _rot(ea, nc, 'xp', 2, [128, S + 3], F32)
    accrot = _rot(ea, nc, 'acc', 2, [128, S], F32)
    yrot = _rot(ea, nc, 'yy', 2, [128, S], F32)
    sq = _mk(ea, nc, 'sq', [128, S], BF16)
    outrot = _rot(ea, nc, 'qkT', 2, [128, S], BF16)
    rnrot = _rot(ea, nc, 'rn', 2, [128, 512], F32)
    st = _mk(ea, nc, 'tokst', [128, 16, 128], BF16)
    for i in range(2):
        P.op('vector', 'memset', xprot.tiles[i][:, 0:3], 0.0, reads=[], writes=[('xp', i, 'pad')])

    def stageX(c):
        xp, xk = xprot.next()
        acc, ak = accrot.next()
        P.dma('sync', xp[:, 3:S + 3], G['dnqkvT_d'][c * 128:(c + 1) * 128, :], [('dnqkvT', c, t4) for t4 in range(4)], [xk])
        P.op('vector', 'tensor_scalar', acc[:], xp[:, 0:S], convw[:, c, 0:1], None, ALU.mult, reads=[xk, xk + ('pad',), 'convw'], writes=[ak])
        yield
        for j in range(1, 4):
            P.op('vector', 'scalar_tensor_tensor', acc[:], xp[:, j:j + S], convw[:, c, j:j + 1], acc[:], ALU.mult, ALU.add,
                 reads=[xk, xk + ('pad',), 'convw', ak], writes=[ak])
            yield
        return_val[c] = (acc, ak)

    return_val = {}

    def stageY(c):
        which, h = c // 8, c % 8
        acc, ak = return_val[c]
        y, yk = yrot.next()
        P.op('scalar', 'activation', y[:], acc[:], AF.Silu, reads=[ak], writes=[yk])
        yield
        if which < 2:
            P.op('scalar', 'activation', sq[:], y[:], AF.Square, reads=[yk], writes=['sq'])
            ot, otk = outrot.next()
            for t4 in range(4):
                ps, pk = sbank
                P.mm(ps[:, :], ones_b[:, :], sq[:, t4 * 512:(t4 + 1) * 512], True, True, ['ones_bA', 'sq'], [pk])
                rn, rk = rnrot.next()
                P.op('scalar', 'activation', rn[:], ps[:, :], AF.Ln, bias=epsb[:, 0:1], reads=[pk, 'epsb'], writes=[rk])
                P.op('scalar', 'activation', rn[:], rn[:], AF.Exp, scale=-0.5, reads=[rk], writes=[rk])
                P.op('vector', 'scalar_tensor_tensor', ot[:, t4 * 512:(t4 + 1) * 512], y[:, t4 * 512:(t4 + 1) * 512],
                     (128 ** -0.5) if which == 0 else 1.0, rn[:], ALU.mult, ALU.mult, reads=[yk, rk], writes=[(otk, t4)])
                yield
            dstd = G['gqT_d'] if which == 0 else G['gkT_d']
            P.dma('scalar', dstd[h, :, :], ot[:, :], [(otk, t4) for t4 in range(4)], [])
            srcT = ot
            srck = [(otk, t4) for t4 in range(4)]
        else:
            P.op('scalar', 'copy', sq[:], y[:], reads=[yk], writes=['sq'])
            srcT = sq
            srck = ['sq']
        if which >= 1:
            for half in range(2):
                pt, pk = tbank
                for j in range(8):
                    tt = half * 8 + j
                    P.tr(pt[:, j * 128:(j + 1) * 128], srcT[:, tt * 128:(tt + 1) * 128], ident_bf[:, :],
                         srck + ['ident_bf'], [pk], signal=(j == 7))
                P.op('vector' if half == 0 else 'scalar', 'tensor_copy' if half == 0 else 'copy',
                     st[:, half * 8:(half + 1) * 8, :], t8(pt), reads=[pk], writes=[('tokst', half)])
                yield
            dstd = G['ktok_d'] if which == 1 else G['vtok2_d']
            P.dma('scalar', dstd[h].rearrange('(tt p) d -> p tt d', p=128), st[:, :, :], [('tokst', 0), ('tokst', 1)], [])
        yield

    for _ in stageX(0):
        yield
    for c in range(24):
        gx = stageX(c + 1) if c + 1 < 24 else iter(())
        gy = stageY(c)
        alive = [gx, gy]
        while alive:
            for g in list(alive):
                try:
                    next(g)
                except StopIteration:
                    alive.remove(g)
            yield


def phase_nsa(P, nc, G):
    banks = G['banks']
    tb = G['tbanks']
    SC = 128 ** -0.5
    sbank = [banks[0], banks[1], banks[2]]
    obank = [banks[3], banks[4]]
    ubank = [banks[5], (tb[0][0][:, :].bitcast(F32), tb[0][1])]
    gbank = (tb[1][0][:, :].bitcast(F32), tb[1][1])
    mbank = gbank
    cnt = dict(s=0, o=0)
    ident_f = G['ident_f']
    ident_bf = G['ident_bf']
    with ExitStack() as es:
        def ld(name, shape, dt, src, q='sync'):
            t = _mk(es, nc, name, shape, dt)
            P.dma(q, t[:], src, [], [name])
            return t
        maskCb = ld('maskCb', [128, S], BF16, G['maskCb'][:, :])
        mb8 = ld('mb8', [128, 8, 512], BF16, G['mb8'][:, :, :], 'scalar')
        ovt = ld('ovt', [128, 32], BF16, G['ov'][:, :])
        Et = ld('Et', [32, S], BF16, G['E'][:, :], 'scalar')
        ones = ld('ones', [128, 128], BF16, G['ones_bf'][:, :])
        tinyr = ld('tinyr', [1, 128], BF16, G['tinyrow'][:, :])
        onesr = ld('onesr', [1, 512], BF16, G['onesrow'][:, :])
        fm_mul = ld('fm_mul', [128, 16, 32], F32, G['fm_mul'][:, :, :], 'scalar')
        fm_add = ld('fm_add', [128, 16, 32], F32, G['fm_add'][:, :, :])
        W1 = []
        W2 = []
        peT = []
        for i, nm in enumerate(('k', 'v')):
            w1 = _mk(es, nc, 'cw1' + nm, [128, 32, 128], BF16)
            P.dma('gpsimd', w1[:], G['cmp_w1_' + nm].ap().rearrange('(l d) f -> d l f', d=128), [], ['cw1' + nm])
            w2 = _mk(es, nc, 'cw2' + nm, [128, 128], BF16)
            P.dma('gpsimd', w2[:], G['cmp_w2_' + nm][:, :], [], ['cw2' + nm])
            pt = _mk(es, nc, 'cpe' + nm, [128, 32], BF16)
            P.dma('gpsimd', pt[:], G['peT_' + nm][:, :], [], ['cpe' + nm])
            W1.append(w1); W2.append(w2); peT.append(pt)
        qtile = [_rot(es, nc, 'qt%d' % hl, 2, [128, 512], BF16) for hl in range(4)]
        ptrot = _rot(es, nc, 'pT', 4, [128, 512], BF16)
        rsrot = _rot(es, nc, 'rs', 2, [128, 512], F32)
        posrot = _rot(es, nc, 'pos', 2, [128, 512], F32)
        gbrot = _rot(es, nc, 'gbs', 4, [128, 512], F32)
        tgrot = _rot(es, nc, 'tg', 2, [128, 512], F32)
        tmrot = _rot(es, nc, 'tmpo', 2, [128, 512], F32)
        oacc = [_mk(es, nc, 'oacc%d' % i, [128, 512], F32) for i in range(4)]
        obf = _rot(es, nc, 'obf', 2, [128, 512], BF16)
        impacc = _mk(es, nc, 'impacc', [32, 512], F32)
        imptok = _mk(es, nc, 'imptok', [128, 4, 32], F32)
        impw = _mk(es, nc, 'impw', [128, 4, 32], F32)
        mx8 = _mk(es, nc, 'mx8', [128, 8], F32)
        thr = _mk(es, nc, 'thr', [128, 1], F32)
        selb = _mk(es, nc, 'selb', [128, 4, 32], F32)
        biasT = _mk(es, nc, 'biasT', [32, 512], BF16)

        for hk in range(2):
            with ExitStack() as eh:
                def ldh(name, shape, src, q):
                    t = _mk(eh, nc, name, shape, BF16)
                    P.dma(q, t[:], src, [], [name])
                    return t
                kcx = ldh('kcx', [128, S], G['kvT_d'][0, hk, :, :], 'sync')
                vcx = ldh('vcx', [128, S], G['kvT_d'][1, hk, :, :], 'scalar')
                ksT = ldh('ksT', [128, S], G['kvT_d'][2, hk, :, :], 'sync')
                kwT = ldh('kwT', [128, S], G['kvT_d'][4, hk, :, :], 'scalar')
                vs = ldh('vs', [128, 16, 128], G['vtok_d'][0].rearrange('(tt p) c -> p tt c', p=128)[:, :, hk * 128:(hk + 1) * 128], 'sync')
                vw = ldh('vw', [128, 16, 128], G['vtok_d'][1].rearrange('(tt p) c -> p tt c', p=128)[:, :, hk * 128:(hk + 1) * 128], 'scalar')
                kcT = _mk(eh, nc, 'kcT', [128, 128], BF16)
                vc = _mk(eh, nc, 'vc', [128, 128], BF16)
                cb = _mk(eh, nc, 'cb', [128, 1], F32)
                cu = _mk(eh, nc, 'cu', [128, 128], F32)
                ct = _mk(eh, nc, 'ct', [128, 128], F32)
                cg = _mk(eh, nc, 'cg', [128, 128], BF16)
                for i, (xT, xk) in enumerate(((kcx, 'kcx'), (vcx, 'vcx'))):
                    nm = 'kv'[i]
                    ps, pk = mbank
                    for l in range(32):
                        P.mm(ps[:, 0:127], W1[i][:, l, :], xT[:, l:l + 16 * 126 + 1:16], start=(l == 0), stop=(l == 31),
                             reads=['cw1' + nm, xk], writes=[pk])
                    ps2, pk2 = sbank[0]
                    for l in range(32):
                        P.mm(ps2[:, 0:1], W1[i][:, l, :], peT[i][:, l:l + 1], start=(l == 0), stop=(l == 31),
                             reads=['cw1' + nm, 'cpe' + nm], writes=[pk2])
                    P.op('vector', 'tensor_copy', cb[:], ps2[:, 0:1], reads=[pk2], writes=['cb'])
                    P.op('scalar', 'activation', cu[:, 0:127], ps[:, 0:127], AF.Identity, bias=cb[:, 0:1], reads=[pk, 'cb'], writes=['cu'])
                    P.op('vector', 'tensor_tensor', ct[:, 0:127], cu[:, 0:127], cu[:, 0:127], ALU.mult, reads=['cu'], writes=['ct'])
                    P.op('vector', 'tensor_scalar', ct[:, 0:127], ct[:, 0:127], 0.044715, 1.0, ALU.mult, ALU.add, reads=['ct'], writes=['ct'])
                    P.op('vector', 'tensor_tensor', ct[:, 0:127], ct[:, 0:127], cu[:, 0:127], ALU.mult, reads=['ct', 'cu'], writes=['ct'])
                    P.op('scalar', 'activation', ct[:, 0:127], ct[:, 0:127], AF.Tanh, scale=0.7978845608028654, reads=['ct'], writes=['ct'])
                    P.op('vector', 'scalar_tensor_tensor', ct[:, 0:127], ct[:, 0:127], 1.0, cu[:, 0:127], ALU.add, ALU.mult,
                         reads=['ct', 'cu'], writes=['ct'])
                    P.op('vector', 'tensor_scalar', cg[:, 0:127], ct[:, 0:127], 0.5, None, ALU.mult, reads=['ct'], writes=['cg'])
                    if i == 0:
                        P.mm(ps[:, 0:127], W2[0][:, :], cg[:, 0:127], True, True, ['cw2k', 'cg'], [pk])
                        P.op('vector', 'tensor_copy', kcT[:, 0:127], ps[:, 0:127], reads=[pk], writes=['kcT'])
                    else:
                        P.mm(ps[0:127, 0:128], cg[:, 0:127], W2[1][:, :], True, True, ['cw2v', 'cg'], [pk])
                        P.op('vector', 'tensor_copy', vc[0:127, :], ps[0:127, 0:128], reads=[pk], writes=['vc'])

                def load_q(qi_):
                    res = []
                    for hl in range(4):
                        h = hk * 4 + hl
                        t_, k_ = qtile[hl].next()
                        P.dma('sync', t_[:], G['qT_d'][h, :, qi_ * 512:(qi_ + 1) * 512], [], [k_])
                        res.append((t_, k_))
                    return res
                qnext = load_q(0)
                for qi in range(4):
                    q0 = qi * 512
                    qh = qnext
                    if qi + 1 < 4:
                        qnext = load_q(qi + 1)

                    tiles = []
                    for hl in range(4):
                        tiles.append(dict(br=0, hl=hl, ki=0, n=0, last=True))
                    ntile_cmp = 4
                    for hl in range(4):
                        kis = list(range(max(0, 4 * qi - 4), 4 * qi + 4))
                        for n_, ki in enumerate(kis):
                            tiles.append(dict(br=2, hl=hl, ki=ki, n=n_, last=(n_ == len(kis) - 1)))
                    for hl in range(4):
                        kis = list(range(0, 4 * qi + 4))
                        for n_, ki in enumerate(kis):
                            tiles.append(dict(br=1, hl=hl, ki=ki, n=n_, last=(n_ == len(kis) - 1)))
                    first_sel = next(i for i, t in enumerate(tiles) if t['br'] == 1)

                    def stage1(t):
                        hl, br, ki = t['hl'], t['br'], t['ki']
                        qt, qk = qh[hl]
                        pS, pSk = sbank[cnt['s'] % 3]; cnt['s'] += 1
                        t['pS'] = (pS, pSk)
                        if br == 0:
                            P.mm(pS[0:127, :], kcT[:, 0:127], qt[:, :], True, False, ['kcT', qk], [pSk], signal=False)
                            P.mm(pS[0:127, :], ident_bf[0:127, 0:127], maskCb[0:127, q0:q0 + 512], False, True, ['ident_bf', 'maskCb'], [pSk])
                        elif br == 1:
                            r = ki - 4 * qi
                            P.mm(pS[:, :], ksT[:, ki * 128:(ki + 1) * 128], qt[:, :], True, False, ['ksT', qk], [pSk], signal=False)
                            if r >= 0:
                                P.mm(pS[:, :], ident_bf[:, :], mb8[:, r + 4, :], False, False, ['ident_bf', 'mb8'], [pSk], signal=False)
                            P.mm(pS[:, :], Et[0:32, ki * 128:(ki + 1) * 128], biasT[0:32, :], False, True, ['Et', 'biasT'], [pSk])
                        else:
                            r = ki - 4 * qi
                            P.mm(pS[:, :], kwT[:, ki * 128:(ki + 1) * 128], qt[:, :], True, False, ['kwT', qk], [pSk], signal=False)
                            P.mm(pS[:, :], ident_bf[:, :], mb8[:, r + 4, :], False, True, ['ident_bf', 'mb8'], [pSk])
                        if t['n'] == 0:
                            t['acc'] = (obank[cnt['o'] % 2], ubank[cnt['o'] % 2]); cnt['o'] += 1
                            h = hk * 4 + hl
                            gidx = h * 3 + br
                            gb, gbk = gbrot.next()
                            P.dma('sync', gb[:, :], G['gT_d'][gidx:gidx + 1, q0:q0 + 512].partition_broadcast(128), [], [gbk])
                            t['gb'] = (gb, gbk)

                    def stage2(t, head_t):
                        hl, br, ki = t['hl'], t['br'], t['ki']
                        pS, pSk = t['pS']
                        (po, pok), (pu, puk) = head_t['acc']
                        np_ = 127 if br == 0 else 128
                        pT, pTk = ptrot.next()
                        P.op('scalar', 'activation', pT[0:np_, :], pS[0:np_, :], AF.Exp, scale=SC, reads=[pSk], writes=[pTk])
                        first, last = t['n'] == 0, t['last']
                        if br == 0:
                            vt, vk = vc[0:127, :], 'vc'
                        elif br == 1:
                            vt, vk = vs[:, ki, :], 'vs'
                        else:
                            vt, vk = vw[:, ki, :], 'vw'
                        P.mm(po[:, :], vt, pT[0:np_, :], first, last, [vk, pTk], [pok])
                        if br == 0:
                            P.mm(pu[:, :], ones[0:127, :], pT[0:127, :], True, False, ['ones', pTk], [puk], signal=False)
                            P.mm(pu[:, :], tinyr[0:1, :], onesr[0:1, :], False, True, ['tinyr', 'onesr'], [puk])
                            pi_, pik = mbank
                            P.mm(pi_[0:32, :], ovt[0:127, :], pT[0:127, :], True, True, ['ovt', pTk], [pik])
                        else:
                            P.mm(pu[:, :], ones[:, :], pT[:, :], first, last, ['ones', pTk], [puk])
                        if not last:
                            return
                        gb, gbk = head_t['gb']
                        tg, tk = tgrot.next()
                        pos, posk = posrot.next()
                        P.op('vector', 'tensor_copy', pos[:], po[:, :], reads=[pok], writes=[posk])
                        rs, rk = rsrot.next()
                        P.op('scalar', 'activation', rs[:], pu[:, :], AF.Ln, reads=[puk], writes=[rk])
                        P.op('scalar', 'activation', rs[:], rs[:], AF.Exp, scale=-1.0, reads=[rk], writes=[rk])
                        if br == 0:
                            if hl == 0:
                                P.op('vector', 'tensor_tensor', impacc[:, :], pi_[0:32, :], rs[0:32, :], ALU.mult, reads=[pik, rk], writes=['impacc'])
                            else:
                                it, itk = tmrot.next()
                                P.op('vector', 'tensor_tensor', it[0:32, :], pi_[0:32, :], rs[0:32, :], ALU.mult, reads=[pik, rk], writes=[itk])
                                P.op('gpsimd', 'tensor_tensor', impacc[:, :], impacc[:, :], it[0:32, :], ALU.add, reads=[itk, 'impacc'], writes=['impacc'])
                        P.op('vector', 'tensor_tensor', tg[:], rs[:], gb[:, :], ALU.mult, reads=[rk, gbk], writes=[tk])
                        if br == 0:
                            P.op('vector', 'tensor_tensor', oacc[hl][:], pos[:], tg[:], ALU.mult, reads=[posk, tk], writes=[('oacc', hl)])
                        else:
                            tm, tmk = tmrot.next()
                            P.op('vector', 'tensor_tensor', tm[:], pos[:], tg[:], ALU.mult, reads=[posk, tk], writes=[tmk])
                            P.op('gpsimd', 'tensor_tensor', oacc[hl][:], oacc[hl][:], tm[:], ALU.add, reads=[tmk, ('oacc', hl)], writes=[('oacc', hl)])
                        if br == 1:
                            h = hk * 4 + hl
                            ob, obk = obf.next()
                            P.op('gpsimd', 'tensor_copy', ob[:], oacc[hl][:], reads=[('oacc', hl)], writes=[obk])
                            P.dma('sync', G['onsaT_d'][h, :, q0:q0 + 512], ob[:], [obk], [])

                    def topk():
                        pm, pmk = mbank
                        for s4 in range(4):
                            P.tr(pm[:, s4 * 32:(s4 + 1) * 32], impacc[0:32, s4 * 128:(s4 + 1) * 128], ident_f[0:32, 0:32],
                                 ['impacc', 'ident_f'], [pmk], signal=(s4 == 3))
                        P.op('vector', 'tensor_copy', imptok[:, :, :], pm[:, 0:128].rearrange('p (a b) -> p a b', a=4), reads=[pmk], writes=['imptok'])
                        P.op('vector', 'tensor_tensor', imptok[:, :, :], imptok[:, :, :], fm_mul[:, qi * 4:(qi + 1) * 4, :], ALU.mult,
                             reads=['imptok', 'fm_mul'], writes=['imptok'])
                        P.op('vector', 'tensor_tensor', imptok[:, :, :], imptok[:, :, :], fm_add[:, qi * 4:(qi + 1) * 4, :], ALU.add,
                             reads=['imptok', 'fm_add'], writes=['imptok'])
                        for s4 in range(4):
                            P.op('vector', 'max', mx8[:, :], imptok[:, s4, :], reads=['imptok'], writes=['mx8'])
                            P.op('vector', 'match_replace', impw[:, s4, :], mx8[:, :], imptok[:, s4, :], -3e9, reads=['imptok', 'mx8'], writes=[('impw', s4)])
                            P.op('vector', 'max', mx8[:, :], impw[:, s4, :], reads=[('impw', s4)], writes=['mx8'])
                            P.op('vector', 'tensor_reduce', thr[:, :], mx8[:, :], AX.X, ALU.min, reads=['mx8'], writes=['thr'])
                            P.op('vector', 'tensor_scalar', selb[:, s4, :], imptok[:, s4, :], thr[:, 0:1], None, ALU.is_ge,
                                 reads=['imptok', 'thr'], writes=[('selb', s4)])
                            P.op('vector', 'tensor_scalar', selb[:, s4, :], selb[:, s4, :], 30000.0, -30000.0, ALU.mult, ALU.add,
                                 reads=[('selb', s4)], writes=[('selb', s4)])

                    def topk2():
                        pm, pmk = mbank
                        for s4 in range(4):
                            P.tr(pm[0:32, s4 * 128:(s4 + 1) * 128], selb[:, s4, :], ident_f[:, :], [('selb', s4), 'ident_f'], [pmk], signal=(s4 == 3))
                        P.op('vector', 'tensor_copy', biasT[:, :], pm[0:32, :], reads=[pmk], writes=['biasT'])
                        if 'biasT' in G['dbg_t']:
                            P.dma('sync', G['dbg_t']['biasT'][hk, :, q0:q0 + 512], biasT[:, :], ['biasT'], [])

                    heads = {}
                    LOOK = 2
                    issued = 0

                    def issue_upto(lim):
                        nonlocal issued
                        while issued < min(lim, len(tiles)):
                            if tiles[issued]['br'] == 1 and not sel_ready[0]:
                                if not topk1_done[0]:
                                    break
                                topk2()
                                sel_ready[0] = True
                            stage1(tiles[issued])
                            issued += 1
                    topk1_done = [False]
                    sel_ready = [False]
                    issue_upto(LOOK)
                    for n in range(len(tiles)):
                        t = tiles[n]
                        key = (t['br'], t['hl'])
                        if t['n'] == 0:
                            heads[key] = t
                        issue_upto(n + 1 + LOOK)
                        if issued <= n:
                            issue_upto(n + 1)
                        stage2(t, heads[key])
                        if n == min(ntile_cmp + 1, first_sel - 1):
                            topk()
                            topk1_done[0] = True
                P.flush()


def phase_gdn(P, nc, G):
    banks = G['banks']
    tb = G['tbanks']
    bi = [0, 0]

    def nb():
        b = banks[bi[0] % 6]
        bi[0] += 1
        return b

    def ntb():
        b = tb[bi[1] % 2]
        bi[1] += 1
        return b

    def b4(ps):
        return ps[:, :].rearrange('p (a b) -> p a b', a=4)

    def t8(ps):
        return ps[:, :].rearrange('p (a b) -> p a b', a=8)

    ident_f = G['ident_f']
    ident_bf = G['ident_bf']
    BIGK = ['gq', 'gk']
    with ExitStack() as es:
        def ld(name, shape, dt, src, q='sync'):
            t = _mk(es, nc, name, shape, dt)
            P.dma(q, t[:], src, [], [name])
            return t
        tri = ld('tri', [128, 128], F32, G['tri'][:, :])
        sellast = ld('sellast', [128, 128], F32, G['sellast'][:, :], 'scalar')
        sel63 = ld('sel63', [128, 128], F32, G['sel63'][:, :])
        sel127 = ld('sel127', [128, 128], F32, G['sel127'][:, :], 'scalar')
        mblow = ld('mblow', [128, 128], F32, G['mblow'][:, :])
        strict = ld('strict', [128, 128], F32, G['strictm'][:, :], 'scalar')
        ones_f = ld('ones_f', [128, 128], F32, G['ones_f'][:, :])
        ones_b = ld('ones_b', [128, 128], BF16, G['ones_bf'][:, :], 'scalar')
        alr = ld('alr', [128, 128], F32, G['alog_rep'][:, :], 'scalar')
        dtr = ld('dtr', [128, 128], F32, G['dtb_rep'][:, :])
        nwb = ld('nwb', [128, 1024], F32, G['nw_rep'][0:1, :].partition_broadcast(128), 'scalar')
        epsb = G['epsb']
        ktok_d = G['ktok_d']
        vtok2_d = G['vtok2_d']

        ab = ld('ab', [128, 16, 16], F32, G['dnab_d'].ap().rearrange('(tt p) c -> p tt c', p=128))
        names = ['beta', 'gg', 'gc', 'glsel', 'egc', 'ekd', 'negbeta', 'bgc', 'tmpa', 'tmpb']
        sc = {n: _mk(es, nc, 'sc_' + n, [128, 128], F32) for n in names}
        egl2 = _mk(es, nc, 'egl2', [128, 16, 2, 8], F32)

        def v3(t):
            return t[:, :].rearrange('p (a b) -> p a b', a=16)
        P.op('scalar', 'activation', v3(sc['beta']), ab[:, :, 8:16], AF.Sigmoid, reads=['ab'], writes=['beta'])
        P.op('vector', 'tensor_tensor', v3(sc['tmpa']), ab[:, :, 0:8], v3(dtr), ALU.add, reads=['ab', 'dtr'], writes=['tmpa'])
        P.op('scalar', 'activation', sc['tmpa'][:, :], sc['tmpa'][:, :], AF.Exp, reads=['tmpa'], writes=['tmpa'])
        P.op('vector', 'tensor_scalar', sc['tmpa'][:, :], sc['tmpa'][:, :], 1.0, None, ALU.add, reads=['tmpa'], writes=['tmpa'])
        P.op('scalar', 'activation', sc['tmpa'][:, :], sc['tmpa'][:, :], AF.Ln, reads=['tmpa'], writes=['tmpa'])
        P.op('scalar', 'activation', sc['tmpb'][:, :], alr[:, :], AF.Exp, reads=['alr'], writes=['tmpb'])
        P.op('vector', 'scalar_tensor_tensor', sc['gg'][:, :], sc['tmpa'][:, :], -1.0, sc['tmpb'][:, :], ALU.mult, ALU.mult,
             reads=['tmpa', 'tmpb'], writes=['gg'])
        ps, pk = nb()
        P.mm(ps[:, 0:128], tri[:, :], sc['gg'][:, :], True, True, ['tri', 'gg'], [pk])
        P.op('vector', 'tensor_copy', sc['gc'][:, :], ps[:, 0:128], reads=[pk], writes=['gc'])
        ps, pk = nb()
        P.mm(ps[:, 0:128], sellast[:, :], sc['gc'][:, :], True, True, ['sellast', 'gc'], [pk])
        P.op('vector', 'tensor_tensor', sc['tmpa'][:, :], ps[:, 0:128], sc['gc'][:, :], ALU.subtract, reads=[pk, 'gc'], writes=['tmpa'])
        P.op('scalar', 'activation', sc['ekd'][:, :], sc['tmpa'][:, :], AF.Exp, reads=['tmpa'], writes=['ekd'])
        for ci, selm in enumerate((sel63, sel127)):
            ps, pk = nb()
            P.mm(ps[:, 0:128], selm[:, :], sc['gc'][:, :], True, True, ['sel63', 'sel127', 'gc'], [pk])
            P.op('scalar', 'activation', egl2[:, :, ci, :], ps[:, 0:128].rearrange('p (a b) -> p a b', a=16), AF.Exp,
                 reads=[pk], writes=[('egl2', ci)])
        P.op('scalar', 'activation', sc['egc'][:, :], sc['gc'][:, :], AF.Exp, reads=['gc'], writes=['egc'])
        P.op('vector', 'tensor_scalar', sc['negbeta'][:, :], sc['beta'][:, :], -1.0, None, ALU.mult, reads=['beta'], writes=['negbeta'])
        P.op('vector', 'tensor_tensor', sc['bgc'][:, :], sc['beta'][:, :], sc['egc'][:, :], ALU.mult, reads=['beta', 'egc'], writes=['bgc'])
        if 'gdn_sc' in G['dbg_t']:
            for i_, n_ in enumerate(('beta', 'gg', 'gc', 'ekd')):
                P.dma('sync', G['dbg_t']['gdn_sc'][i_].rearrange('(tt p) h -> p tt h', p=128), v3(sc[n_]), [n_], [])

        gcT_s = _mk(es, nc, 'gcT_s', [8, S], F32)
        gc3 = v3(sc['gc']); egc3 = v3(sc['egc']); nb3 = v3(sc['negbeta']); bgc3 = v3(sc['bgc'])
        beta3 = v3(sc['beta']); ekd3 = v3(sc['ekd'])
        for q4 in range(4):
            ps, pk = nb()
            for j in range(4):
                tt = q4 * 4 + j
                P.tr(ps[0:8, j * 128:(j + 1) * 128], gc3[:, tt, :], ident_f[:, :], ['gc', 'ident_f'], [pk], signal=(j == 3))
            P.op('vector', 'tensor_copy', gcT_s[:, q4 * 512:(q4 + 1) * 512], ps[0:8, :], reads=[pk], writes=[('gcT_s', q4)])
        P.dma('sync', G['gcT_d'].ap().rearrange('tt o (h t) -> h (tt o) t', h=8), gcT_s[:, :].rearrange('h (tt t) -> h tt t', tt=16),
              [('gcT_s', q4) for q4 in range(4)], ['gcT_d'])

        def T3(name, dt):
            return _mk(es, nc, name, [128, 8, 128], dt)
        decay = T3('decay', F32)
        NM = [[T3('N%d' % i, BF16), T3('M%d' % i, BF16)] for i in range(2)]
        PTf = T3('PTf', F32); PTb = T3('PTb', BF16)
        Nf = T3('Nf', F32)
        attn = T3('attn', BF16)
        vb = T3('vb', BF16); kbg = T3('kbg', BF16)
        vn = T3('vn', BF16)
        tmpo = T3('tmpo', F32); oall = T3('oall', F32); o2 = T3('o2', F32); onb = T3('onb', BF16)
        ostg = T3('ostg', BF16)
        attnT2 = [T3('attnT%d' % i, BF16) for i in range(2)]
        kdec2 = [T3('kdec%d' % i, BF16) for i in range(2)]
        uf2 = [T3('uf%d' % i, F32) for i in range(2)]
        wT2 = [T3('wT%d' % i, BF16) for i in range(2)]
        ktok2 = [T3('ktok%d' % i, BF16) for i in range(2)]
        vtok2 = [T3('vtok%d' % i, BF16) for i in range(2)]
        zs2 = [_mk(es, nc, 'zs%d' % i, [128, 1024], F32) for i in range(3)]
        grow2 = [_mk(es, nc, 'grow%d' % i, [128, 1024], F32) for i in range(2)]
        qTt3 = [T3('qTt%d' % i, BF16) for i in range(3)]
        kTt2 = [T3('kTt%d' % i, BF16) for i in range(2)]
        negmask = ld('negmask', [128, 128], F32, G['negmask'][:, :])
        Sf = T3('Sf', F32); Sb = T3('Sb', BF16)
        ssq = _mk(es, nc, 'gssq', [128, 8], F32)
        P.op('vector', 'memset', Sf[:, :, :], 0.0, reads=[], writes=[('Sf', 0), ('Sf', 1)])
        P.op('vector', 'memset', Sb[:, :, :], 0.0, reads=[], writes=['Sb'])

        def bc_h(ap2):
            return ap2.unsqueeze(2).to_broadcast([128, 8, 128])

        def bc_m(ap2):
            return ap2.unsqueeze(1).to_broadcast([128, 8, 128])

        def loads(tt):
            pb = tt % 2
            tl = slice(tt * 128, (tt + 1) * 128)
            P.dma('sync', ktok2[pb][:, :, :], ktok_d[:, tl, :].rearrange('h p d -> p h d'), [], [('ktok', pb)])
            P.dma('scalar', vtok2[pb][:, :, :], vtok2_d[:, tl, :].rearrange('h p d -> p h d'), [], [('vtok', pb)])
            P.dma('sync', zs2[tt % 3][:, :], G['dnz_d'][tl, :], [], [('zs', tt % 3)])
            P.dma('scalar', grow2[pb][:, :], G['gcT_d'][tt, 0:1, :].partition_broadcast(128), ['gcT_d'], [('grow', pb)])
            P.dma('sync', qTt3[tt % 3][:, :, :], G['gqT_d'][:, :, tl].rearrange('h p t -> p h t'), [], [('qTt', tt % 3)])
            P.dma('scalar', kTt2[pb][:, :, :], G['gkT_d'][:, :, tl].rearrange('h p t -> p h t'), [], [('kTt', pb)])

        def prep(tt):
            pb = tt % 2
            tl = slice(tt * 128, (tt + 1) * 128)
            ktok, vtok, grow = ktok2[pb], vtok2[pb], grow2[pb]
            qTt, kTt = qTt3[tt % 3], kTt2[pb]
            attnT, kdec, uf, wT = attnT2[pb], kdec2[pb], uf2[pb], wT2[pb]
            if tt + 1 < 16:
                loads(tt + 1)
            P.op('vector', 'tensor_tensor', decay[:, :, :], bc_h(gc3[:, tt, :]), grow[:, :].rearrange('p (a b) -> p a b', a=8), ALU.subtract,
                 reads=['gc', ('grow', pb)], writes=['decay'])
            P.op('vector', 'tensor_tensor', decay[:, :, :], decay[:, :, :], bc_m(negmask[:, :]), ALU.add, reads=['decay', 'negmask'], writes=['decay'])
            P.op('scalar', 'activation', decay[:, :, :], decay[:, :, :], AF.Exp, reads=['decay'], writes=['decay'])
            yield
            for hb in range(2):
                ps, pk = nb()
                for hl in range(4):
                    h = hb * 4 + hl
                    P.mm(ps[:, hl * 128:(hl + 1) * 128], kTt[:, h, :], kTt[:, h, :], True, True, [('kTt', pb)], [pk])
                P.op('vector', 'tensor_tensor', Nf[:, hb * 4:hb * 4 + 4, :], b4(ps), decay[:, hb * 4:hb * 4 + 4, :], ALU.mult,
                     reads=[pk, 'decay'], writes=[('Nf', hb)])
                ps2, pk2 = nb()
                for hl in range(4):
                    h = hb * 4 + hl
                    P.mm(ps2[:, hl * 128:(hl + 1) * 128], qTt[:, h, :], kTt[:, h, :], True, True, [('qTt', tt % 3), ('kTt', pb)], [pk2])
                P.op('vector', 'tensor_tensor', attn[:, hb * 4:hb * 4 + 4, :], b4(ps2), decay[:, hb * 4:hb * 4 + 4, :], ALU.mult,
                     reads=[pk2, 'decay'], writes=[('attn', hb)])
                yield
            P.op('gpsimd', 'tensor_tensor', Nf[:, :, :], Nf[:, :, :], bc_h(nb3[:, tt, :]), ALU.mult,
                 reads=[('Nf', 0), ('Nf', 1), 'negbeta'], writes=[('Nf', 0), ('Nf', 1)])
            N1, M1 = NM[0]
            P.op('gpsimd', 'tensor_tensor', N1[:, :, :], Nf[:, :, :], bc_m(strict[:, :]), ALU.mult,
                 reads=[('Nf', 0), ('Nf', 1), 'strict'], writes=[('N', 0, 0), ('N', 0, 1)])
            P.op('gpsimd', 'tensor_tensor', vb[:, :, :], vtok[:, :, :], bc_h(beta3[:, tt, :]), ALU.mult, reads=[('vtok', pb), 'beta'], writes=['vb'])
            P.op('gpsimd', 'tensor_tensor', kbg[:, :, :], ktok[:, :, :], bc_h(bgc3[:, tt, :]), ALU.mult, reads=[('ktok', pb), 'bgc'], writes=['kbg'])
            P.op('gpsimd', 'tensor_tensor', kdec[:, :, :], ktok[:, :, :], bc_h(ekd3[:, tt, :]), ALU.mult, reads=[('ktok', pb), 'ekd'], writes=[('kdec', pb)])
            yield
            pt, ptk = ntb()
            for h in range(8):
                P.tr(pt[:, h * 128:(h + 1) * 128], N1[:, h, :], ident_bf[:, :], [('N', 0, 0), ('N', 0, 1), 'ident_bf'], [ptk], signal=(h == 7))
            P.op('vector', 'tensor_copy', M1[:, :, :], t8(pt), reads=[ptk], writes=[('M', 0, 0), ('M', 0, 1)])
            P.op('vector', 'tensor_tensor', PTf[:, :, :], t8(pt), bc_m(ident_f[:, :]), ALU.add, reads=[ptk, 'ident_f'], writes=[('PTf', 0), ('PTf', 1)])
            P.op('scalar', 'copy', PTb[:, :, :], PTf[:, :, :], reads=[('PTf', 0), ('PTf', 1)], writes=['PTb'])
            yield
            pt, ptk = ntb()
            for h in range(8):
                P.tr(pt[:, h * 128:(h + 1) * 128], attn[:, h, :], ident_bf[:, :], [('attn', 0), ('attn', 1), 'ident_bf'], [ptk], signal=(h == 7))
            P.op('scalar', 'copy', attnT[:, :, :], t8(pt), reads=[ptk], writes=[('attnT', pb)])
            yield
            cur = 0
            for k in range(1, 6):
                N1, M1 = NM[cur]
                N2, M2 = NM[1 - cur]
                for hb in range(2):
                    ps, pk = nb()
                    for hl in range(4):
                        h = hb * 4 + hl
                        P.mm(ps[:, hl * 128:(hl + 1) * 128], M1[:, h, :], N1[:, h, :], True, True, [('N', cur, hb), ('M', cur, hb)], [pk])
                    P.op('scalar' if hb == 0 else 'vector', 'copy' if hb == 0 else 'tensor_copy', N2[:, hb * 4:hb * 4 + 4, :], b4(ps),
                         reads=[pk], writes=[('N', 1 - cur, hb)])
                    if k < 5:
                        ps2, pk2 = nb()
                        for hl in range(4):
                            h = hb * 4 + hl
                            P.mm(ps2[:, hl * 128:(hl + 1) * 128], N1[:, h, :], M1[:, h, :], True, True, [('N', cur, hb), ('M', cur, hb)], [pk2])
                        P.op('vector' if hb == 0 else 'scalar', 'tensor_copy' if hb == 0 else 'copy', M2[:, hb * 4:hb * 4 + 4, :], b4(ps2),
                             reads=[pk2], writes=[('M', 1 - cur, hb)])
                    yield
                for hb in range(2):
                    ps, pk = nb()
                    for hl in range(4):
                        h = hb * 4 + hl
                        P.mm(ps[:, hl * 128:(hl + 1) * 128], N2[:, h, :], PTb[:, h, :], True, True, [('N', 1 - cur, hb), 'PTb'], [pk])
                    P.op('vector', 'tensor_tensor', PTf[:, hb * 4:hb * 4 + 4, :], b4(ps), PTf[:, hb * 4:hb * 4 + 4, :], ALU.add,
                         reads=[pk, ('PTf', hb)], writes=[('PTf', hb)])
                P.op('scalar', 'copy', PTb[:, :, :], PTf[:, :, :], reads=[('PTf', 0), ('PTf', 1)], writes=['PTb'])
                cur = 1 - cur
                yield
            for hb in range(2):
                ps, pk = nb()
                for hl in range(4):
                    h = hb * 4 + hl
                    P.mm(ps[:, hl * 128:(hl + 1) * 128], PTb[:, h, :], vb[:, h, :], True, True, ['PTb', 'vb'], [pk])
                P.op('scalar', 'copy', uf[:, hb * 4:hb * 4 + 4, :], b4(ps), reads=[pk], writes=[('uf', pb, hb)])
                ps2, pk2 = nb()
                for hl in range(4):
                    h = hb * 4 + hl
                    P.mm(ps2[:, hl * 128:(hl + 1) * 128], kbg[:, h, :], PTb[:, h, :], True, True, ['PTb', 'kbg'], [pk2])
                P.op('vector', 'tensor_copy', wT[:, hb * 4:hb * 4 + 4, :], b4(ps2), reads=[pk2], writes=[('wT', pb, hb)])
                yield

        def scan(tt):
            pb = tt % 2
            tl = slice(tt * 128, (tt + 1) * 128)
            attnT, kdec, uf, wT, zs = attnT2[pb], kdec2[pb], uf2[pb], wT2[pb], zs2[tt % 3]
            qTt = qTt3[tt % 3]
            for c in range(2):
                rows = slice(64 * c, 64 * c + 64)
                for hb in range(2):
                    hs = slice(hb * 4, hb * 4 + 4)
                    ps, pk = nb()
                    for hl in range(4):
                        h = hb * 4 + hl
                        P.mm(ps[:, hl * 128:(hl + 1) * 128], wT[:, h, :], Sb[:, h, :], True, True, [('wT', pb, hb), 'Sb'], [pk])
                    P.op('vector', 'tensor_tensor', vn[rows, hs, :], uf[rows, hs, :], b4(ps)[rows, :, :], ALU.subtract,
                         reads=[pk, ('uf', pb, hb)], writes=[('vn', hb)])
                    psq, pkq = nb()
                    for hl in range(4):
                        h = hb * 4 + hl
                        P.mm(psq[:, hl * 128:(hl + 1) * 128], qTt[:, h, :], Sb[:, h, :], True, True, [('qTt', tt % 3), 'Sb'], [pkq])
                    P.op('vector', 'tensor_tensor', tmpo[rows, hs, :], b4(psq)[rows, :, :], bc_h(egc3[:, tt, :])[rows, hs, :], ALU.mult,
                         reads=[pkq, 'egc'], writes=[('tmpo', hb)])
                    yield
                for hb in range(2):
                    hs = slice(hb * 4, hb * 4 + 4)
                    psa, pka = nb()
                    for hl in range(4):
                        h = hb * 4 + hl
                        P.mm(psa[:, hl * 128:(hl + 1) * 128], attnT[rows, h, :], vn[rows, h, :], True, True, [('attnT', pb), ('vn', hb)], [pka])
                    P.op('vector', 'tensor_tensor', oall[rows, hs, :], b4(psa)[rows, :, :], tmpo[rows, hs, :], ALU.add,
                         reads=[pka, ('tmpo', hb)], writes=[('oall', hb, c)])
                P.op('gpsimd', 'tensor_tensor', Sf[:, :, :], Sf[:, :, :], bc_h(egl2[:, tt, c, :]), ALU.mult,
                     reads=[('Sf', 0), ('Sf', 1), ('egl2', c)], writes=[('Sf', 0), ('Sf', 1)])
                yield
                for hb in range(2):
                    hs = slice(hb * 4, hb * 4 + 4)
                    psk, pkk = nb()
                    for hl in range(4):
                        h = hb * 4 + hl
                        P.mm(psk[:, hl * 128:(hl + 1) * 128], kdec[rows, h, :], vn[rows, h, :], True, True, [('kdec', pb), ('vn', hb)], [pkk])
                    P.op('vector', 'tensor_tensor', Sf[:, hs, :], b4(psk), Sf[:, hs, :], ALU.add, reads=[pkk, ('Sf', hb)], writes=[('Sf', hb)])
                P.op('scalar', 'copy', Sb[:, :, :], Sf[:, :, :], reads=[('Sf', 0), ('Sf', 1)], writes=['Sb'])
                yield
            okeys = [('oall', hb, c) for hb in range(2) for c in range(2)]
            P.op('gpsimd', 'tensor_tensor', o2[:, :, :], oall[:, :, :], oall[:, :, :], ALU.mult, reads=okeys, writes=['o2'])
            P.op('vector', 'tensor_reduce', ssq[:, :], o2[:, :, :], AX.X, ALU.add, reads=['o2'], writes=['gssq'])
            P.op('scalar', 'activation', ssq[:, :], ssq[:, :], AF.Sqrt, bias=epsb[:, 0:1], scale=1.0 / 128, reads=['gssq', 'epsb'], writes=['gssq'])
            P.op('vector', 'reciprocal', ssq[:, :], ssq[:, :], reads=['gssq'], writes=['gssq'])
            yield
            P.op('vector', 'tensor_tensor', o2[:, :, :], oall[:, :, :], bc_h(ssq[:, :]), ALU.mult, reads=okeys + ['gssq'], writes=['o2'])
            P.op('gpsimd', 'tensor_tensor', o2[:, :, :], o2[:, :, :], nwb[:, :].rearrange('p (a b) -> p a b', a=8), ALU.mult,
                 reads=['o2', 'nwb'], writes=['o2'])
            P.op('vector', 'tensor_tensor', onb[:, :, :], o2[:, :, :], zs[:, :].rearrange('p (a b) -> p a b', a=8), ALU.mult,
                 reads=['o2', ('zs', tt % 3)], writes=['onb'])
            yield
            pt, ptk = ntb()
            for h in range(8):
                P.tr(pt[:, h * 128:(h + 1) * 128], onb[:, h, :], ident_bf[:, :], ['onb', 'ident_bf'], [ptk], signal=(h == 7))
            P.op('scalar', 'copy', ostg[:, :, :], t8(pt), reads=[ptk], writes=['ostg'])
            P.dma('sync', G['odnT_d'][:, :, tl].rearrange('h p t -> p h t'), ostg[:, :, :], ['ostg'], [])
            yield

        loads(0)
        run_streams([prep(0)])
        for tt in range(16):
            run_streams([prep(tt + 1) if tt + 1 < 16 else None, scan(tt)])
        P.flush()


def norm_transpose(P, nc, es1, G, src_d, w_d, dstT, rstd_pre=None):
    tb = G['tbanks']
    ident_bf = G['ident_bf']
    epsb = G['epsb']
    w1b = _mk(es1, nc, 'nw_b', [128, D], F32)
    P.dma('scalar', w1b[:], w_d[0:1, :].partition_broadcast(128), [], ['nw_b'])
    xrot = _rot(es1, nc, 'nxt', 2, [128, D], F32)
    hbrot = _rot(es1, nc, 'nhb', 2, [128, D], BF16)
    junk = _mk(es1, nc, 'njunk', [128, D], BF16)
    ssq = _mk(es1, nc, 'nssq', [128, 16], F32)
    rstd = rstd_pre if rstd_pre is not None else _mk(es1, nc, 'nrstd', [128, 16], F32)
    for tt in range(16):
        xt, xk = xrot.next()
        hb, hk = hbrot.next()
        P.dma('sync', xt[:], src_d[tt * 128:(tt + 1) * 128, :], [], [xk])
        if rstd_pre is None:
            P.op('scalar', 'activation', junk[:], xt[:], AF.Square, accum_out=ssq[:, tt:tt + 1], reads=[xk], writes=['njunk', ('nssq', tt)])
            P.op('scalar', 'activation', rstd[:, tt:tt + 1], ssq[:, tt:tt + 1], AF.Sqrt, bias=epsb[:, 0:1], scale=1.0 / D,
                 reads=[('nssq', tt), 'epsb'], writes=[('nrstd', tt)])
            P.op('vector', 'reciprocal', rstd[:, tt:tt + 1], rstd[:, tt:tt + 1], reads=[('nrstd', tt)], writes=[('nrstd', tt)])
        P.op('vector', 'scalar_tensor_tensor', hb[:], xt[:], rstd[:, tt:tt + 1], w1b[:], ALU.mult, ALU.mult,
             reads=[xk, ('nrstd', tt), 'nrstd_all', 'nw_b'], writes=[hk])
        for half in range(2):
            pt, pk = tb[half]
            for j in range(8):
                kc = half * 8 + j
                P.tr(pt[:, j * 128:(j + 1) * 128], hb[:, kc * 128:(kc + 1) * 128], ident_bf[:], [hk, 'ident_bf'], [pk], signal=(j == 7))
            dst = dstT[:, half * 8:(half + 1) * 8, tt * 128:(tt + 1) * 128]
            src = pt[:, :].rearrange('p (j c) -> p j c', j=8)
            if half == 0:
                P.op('scalar', 'copy', dst, src, reads=[pk], writes=[('nT', tt, half)])
            else:
                P.op('vector', 'tensor_copy', dst, src, reads=[pk], writes=[('nT', tt, half)])


def phase_tail(P, nc, G):
    banks = G['banks']
    bi = [0]

    def nb():
        b = banks[bi[0] % 6]
        bi[0] += 1
        return b
    qi = [0]

    def oq():
        qi[0] += 1
        return ('sync', 'scalar')[qi[0] % 2]
    epsb = G['epsb']
    x1_d, x2_d, actT_d, mgT_d = G['x1_d'], G['x2_d'], G['actT_d'], G['mgT_d']
    with ExitStack() as es:
        ssq1 = _mk(es, nc, 'ssq1', [128, 16, 4], F32)
        ssq2 = _mk(es, nc, 'ssq2', [128, 16, 4], F32)
        rstd2 = _mk(es, nc, 'rstd2', [128, 16], F32)
        rstdf = _mk(es, nc, 'rstdf', [128, 16], F32)
        with ExitStack() as e1:
            mixedT = _mk(e1, nc, 'mixedT', [128, 16, S], BF16)
            with ExitStack() as e2:
                oa = _mk(e2, nc, 'onsaT_s', [128, 8, S], BF16)
                ob = _mk(e2, nc, 'odnT_s', [128, 8, S], BF16)
                P.dma('sync', oa[:, :, :], G['onsaT_d'].ap().rearrange('h p t -> p h t'), [], ['oa'])
                P.dma('scalar', ob[:, :, :], G['odnT_d'].ap().rearrange('h p t -> p h t'), [], ['ob'])
                wr = [_rot(e2, nc, 'wup%d' % i, 2, [128, 8, 512], BF16) for i in range(2)]
                grot = _rot(e2, nc, 'mg', 4, [128, 512], BF16)
                trot = _rot(e2, nc, 'mt', 4, [128, 512], F32)
                Wn = G['w_up_nsa'].ap().rearrange('(kc p) n -> p kc n', p=128)
                Wd = G['w_up_dn'].ap().rearrange('(kc p) n -> p kc n', p=128)
                for cb in range(4):
                    c0 = cb * 512
                    w1, w1k = wr[0].next()
                    w2, w2k = wr[1].next()
                    P.dma('gpsimd', w1[:, :, :], Wn[:, :, c0:c0 + 512], [], [w1k])
                    P.dma('gpsimd', w2[:, :, :], Wd[:, :, c0:c0 + 512], [], [w2k])
                    for m in range(4):
                        n0 = c0 + m * 128
                        for t4 in range(4):
                            t0 = t4 * 512
                            g1, g1k = grot.next()
                            g2, g2k = grot.next()
                            P.dma('sync', g1[:, :], mgT_d[n0:n0 + 128, t0:t0 + 512], [], [g1k])
                            P.dma('sync', g2[:, :], mgT_d[2048 + n0:2048 + n0 + 128, t0:t0 + 512], [], [g2k])
                            p1, p1k = nb()
                            for kc in range(8):
                                P.mm(p1[:, :], w1[:, kc, m * 128:(m + 1) * 128], oa[:, kc, t0:t0 + 512], kc == 0, kc == 7, [w1k, 'oa'], [p1k])
                            p2, p2k = nb()
                            for kc in range(8):
                                P.mm(p2[:, :], w2[:, kc, m * 128:(m + 1) * 128], ob[:, kc, t0:t0 + 512], kc == 0, kc == 7, [w2k, 'ob'], [p2k])
                            ta, tak = trot.next()
                            tb_, tbk = trot.next()
                            P.op('scalar', 'activation', g1[:, :], g1[:, :], AF.Sigmoid, reads=[g1k], writes=[g1k])
                            P.op('scalar', 'activation', g2[:, :], g2[:, :], AF.Sigmoid, reads=[g2k], writes=[g2k])
                            P.op('vector', 'tensor_tensor', ta[:, :], p1[:, :], g1[:, :], ALU.mult, reads=[p1k, g1k], writes=[tak])
                            P.op('vector', 'tensor_tensor', tb_[:, :], p2[:, :], g2[:, :], ALU.mult, reads=[p2k, g2k], writes=[tbk])
                            P.op('vector', 'tensor_tensor', mixedT[:, n0 // 128, t0:t0 + 512], ta[:, :], tb_[:, :], ALU.add,
                                 reads=[tak, tbk], writes=[('mixedT', n0 // 128, t4)])
                P.flush()
            if 'mixedT' in G['dbg_t']:
                P.dma('sync', G['dbg_t']['mixedT'].ap().rearrange('(kc p) t -> p kc t', p=128), mixedT[:, :, :], [], [])
                P.flush()
            with ExitStack() as e2:
                xr = _rot(e2, nc, 'xres', 3, [128, 512], F32)
                orr = _rot(e2, nc, 'x1o', 3, [128, 512], F32)
                junk = _mk(e2, nc, 'junk1', [128, 512], BF16)

                def epi_res(src_d, dst_d, ssq):
                    def epi(ps, pk, info):
                        t0, c0 = info['t0'], info['c0']
                        xt, xk = xr.next()
                        o, ok = orr.next()
                        P.dma('sync', xt[:, :], src_d[t0:t0 + 128, c0:c0 + 512], [], [xk])
                        P.op('vector', 'tensor_tensor', o[:, :], ps, xt[:, :], ALU.add, reads=[pk, xk], writes=[ok])
                        P.op('scalar', 'activation', junk[:, :], o[:, :], AF.Square, accum_out=ssq[:, t0 // 128, c0 // 512:c0 // 512 + 1],
                             reads=[ok], writes=['junk1', ('ssq', t0 // 128, c0 // 512)])
                        P.dma('scalar', dst_d[t0:t0 + 128, c0:c0 + 512], o[:, :], [ok], [])
                    return epi
                Wo = G['w_o'].ap().rearrange('(kc p) n -> p kc n', p=128)
                blocks = [(c0, 512, 'tok', epi_res(G['x'], x1_d, ssq1)) for c0 in range(0, D, 512)]
                dense(P, nc, e2, 'wo', mixedT, 'mixedT_all', 16, Wo, blocks, banks)
                P.flush()

        def finish_rstd(ssq, rstd):
            P.op('vector', 'tensor_reduce', rstd[:, :], ssq[:, :, :], AX.X, ALU.add, reads=[], writes=['nrstd_all'])
            P.op('scalar', 'activation', rstd[:, :], rstd[:, :], AF.Sqrt, bias=epsb[:, 0:1], scale=1.0 / D, reads=['nrstd_all', 'epsb'], writes=['nrstd_all'])
            P.op('vector', 'reciprocal', rstd[:, :], rstd[:, :], reads=['nrstd_all'], writes=['nrstd_all'])
        e_ffn = es
        KF = DFF // 128
        wdr = _rot(es, nc, 'wdn', 2, [128, KF, 512], BF16)
        Wdn = G['w_ffn_down'].ap().rearrange('(kc p) n -> p kc n', p=128)
        wd_pre = []
        with ExitStack() as e1:
            h2T = _mk(e1, nc, 'h2T', [128, 16, S], BF16)
            with ExitStack() as e2:
                finish_rstd(ssq1, rstd2)
                norm_transpose(P, nc, e2, G, x1_d, G['norm2_w'], h2T, rstd_pre=rstd2)
                P.flush()
            for cb in range(2):
                wt, wk = wdr.next()
                for part in range(4):
                    P.dma('gpsimd', wt[:, part * 11:(part + 1) * 11, :], Wdn[:, part * 11:(part + 1) * 11, cb * 512:(cb + 1) * 512], [], [(wk, part)])
                wd_pre.append((wt, wk))
            with ExitStack() as e2:
                wr = [_rot(e2, nc, 'wgu%d' % i, 2, [128, 16, 256], BF16) for i in range(2)]
                sgr = _rot(e2, nc, 'sg', 3, [128, 512], F32)
                acr = _rot(e2, nc, 'acb', 3, [128, 512], BF16)
                Wg = G['w_ffn_gate'].ap().rearrange('(kc p) n -> p kc n', p=128)
                Wu = G['w_ffn_up'].ap().rearrange('(kc p) n -> p kc n', p=128)
                for fb2 in range(DFF // 256):
                    c0 = fb2 * 256
                    w1, w1k = wr[0].next()
                    w2, w2k = wr[1].next()
                    P.dma('gpsimd', w1[:, :, :], Wg[:, :, c0:c0 + 256], [], [w1k])
                    P.dma('gpsimd', w2[:, :, :], Wu[:, :, c0:c0 + 256], [], [w2k])
                    for m in range(2):
                        fb = fb2 * 2 + m
                        for t4 in range(4):
                            t0 = t4 * 512
                            p1, p1k = nb()
                            for kc in range(16):
                                P.mm(p1[:, :], w1[:, kc, m * 128:(m + 1) * 128], h2T[:, kc, t0:t0 + 512], kc == 0, kc == 15, [w1k, 'h2T'], [p1k])
                            p2, p2k = nb()
                            for kc in range(16):
                                P.mm(p2[:, :], w2[:, kc, m * 128:(m + 1) * 128], h2T[:, kc, t0:t0 + 512], kc == 0, kc == 15, [w2k, 'h2T'], [p2k])
                            sg, sgk = sgr.next()
                            ac, ack = acr.next()
                            P.op('scalar', 'activation', sg[:, :], p1[:, :], AF.Silu, reads=[p1k], writes=[sgk])
                            P.op('vector', 'tensor_tensor', ac[:, :], p2[:, :], sg[:, :], ALU.mult, reads=[p2k, sgk], writes=[ack])
                            P.dma('scalar', actT_d[t4 * 4:(t4 + 1) * 4, :, fb, :].rearrange('a p t -> p a t'),
                                  ac[:, :].rearrange('p (a t) -> p a t', a=4), [ack], [])
                P.flush()
        if True:
            e1 = e_ffn
            atr = _rot(e1, nc, 'actt', 3, [128, KF, 128], BF16)
            xr = _rot(e1, nc, 'x1res', 3, [128, 512], F32)
            orr = _rot(e1, nc, 'x2o', 3, [128, 512], F32)
            junk = _mk(e1, nc, 'junk2', [128, 512], BF16)
            wfb = _mk(e1, nc, 'wfb', [128, D], F32)
            P.dma('sync', wfb[:], G['norm_f_w'][0:1, :].partition_broadcast(128), [], ['wfb'])
            x2rot = _rot(e1, nc, 'x2t', 2, [128, 1536], F32)
            outrot = _rot(e1, nc, 'outt', 2, [128, D], F32)
            for cb in range(4):
                c0 = cb * 512
                if cb < 2:
                    wt, wk = wd_pre[cb]
                else:
                    wt, wk = wdr.next()
                    for part in range(4):
                        P.dma('gpsimd', wt[:, part * 11:(part + 1) * 11, :], Wdn[:, part * 11:(part + 1) * 11, c0:c0 + 512], [], [(wk, part)])
                wks = [(wk, part) for part in range(4)]
                pend = []
                at0, ak0 = atr.next()
                P.dma('sync', at0[:, :, :], actT_d[0, :, :, :], [], [ak0])
                pend.append((at0, ak0))
                for tt in range(16):
                    t0 = tt * 128
                    if tt + 1 < 16:
                        at1, ak1 = atr.next()
                        P.dma('sync', at1[:, :, :], actT_d[tt + 1, :, :, :], [], [ak1])
                        pend.append((at1, ak1))
                    at, ak = pend.pop(0)
                    ps, pk = nb()
                    for kc in range(KF):
                        P.mm(ps[:, :], at[:, kc, :], wt[:, kc, :], kc == 0, kc == KF - 1, wks + [ak], [pk])
                    xt, xk = xr.next()
                    o, ok = orr.next()
                    P.dma('sync', xt[:, :], x1_d[t0:t0 + 128, c0:c0 + 512], [], [xk])
                    P.op('vector', 'tensor_tensor', o[:, :], ps[:, :], xt[:, :], ALU.add, reads=[pk, xk], writes=[ok])
                    P.op('scalar', 'activation', junk[:, :], o[:, :], AF.Square, accum_out=ssq2[:, tt, cb:cb + 1],
                         reads=[ok], writes=['junk2', ('ssq2', tt, cb)])
                    if cb < 3:
                        P.dma('scalar', x2_d[t0:t0 + 128, c0:c0 + 512], o[:, :], [ok], [('x2', tt, cb)])
                    else:
                        P.op('vector', 'tensor_reduce', rstdf[:, tt:tt + 1], ssq2[:, tt, :], AX.X, ALU.add,
                             reads=[('ssq2', tt, c_) for c_ in range(4)], writes=[('rstdf', tt)])
                        P.op('scalar', 'activation', rstdf[:, tt:tt + 1], rstdf[:, tt:tt + 1], AF.Sqrt, bias=epsb[:, 0:1], scale=1.0 / D,
                             reads=[('rstdf', tt), 'epsb'], writes=[('rstdf', tt)])
                        P.op('vector', 'reciprocal', rstdf[:, tt:tt + 1], rstdf[:, tt:tt + 1], reads=[('rstdf', tt)], writes=[('rstdf', tt)])
                        x2t, x2k = x2rot.next()
                        ot, otk = outrot.next()
                        P.dma('sync', x2t[:, 0:1536], x2_d[t0:t0 + 128, 0:1536], [('x2', tt, c_) for c_ in range(3)], [x2k])
                        P.op('vector', 'scalar_tensor_tensor', ot[:, 0:1536], x2t[:, 0:1536], rstdf[:, tt:tt + 1], wfb[:, 0:1536], ALU.mult, ALU.mult,
                             reads=[x2k, ('rstdf', tt), 'wfb'], writes=[(otk, 0)])
                        P.op('vector', 'scalar_tensor_tensor', ot[:, 1536:2048], o[:, :], rstdf[:, tt:tt + 1], wfb[:, 1536:2048], ALU.mult, ALU.mult,
                             reads=[ok, ('rstdf', tt), 'wfb'], writes=[(otk, 1)])
                        P.dma('scalar', G['out_d'][t0:t0 + 128, :], ot[:, :], [(otk, 0), (otk, 1)], [])
            P.flush()
        if False:
            finish_rstd(ssq2, rstdf)
            wfb = _mk(e1, nc, 'wfb', [128, D], F32)
            P.dma('scalar', wfb[:], G['norm_f_w'][0:1, :].partition_broadcast(128), [], ['wfb'])
            xr = _rot(e1, nc, 'x2t', 2, [128, D], F32)
            orr = _rot(e1, nc, 'outt', 2, [128, D], F32)
            for tt in range(16):
                xt, xk = xr.next()
                o, ok = orr.next()
                P.dma('sync', xt[:, :], x2_d[tt * 128:(tt + 1) * 128, :], [], [xk])
                P.op('vector', 'scalar_tensor_tensor', o[:, :], xt[:, :], rstdf[:, tt:tt + 1], wfb[:, :], ALU.mult, ALU.mult,
                     reads=[xk, 'nrstd_all', 'wfb'], writes=[ok])
                P.dma('scalar', G['out_d'][tt * 128:(tt + 1) * 128, :], o[:, :], [ok], [])
            P.flush()


def build(dbg=None, stop_after=99):
    nc = bass.Bass('TRN2', target_bir_lowering=False)
    P = Prog(nc)
    dbg = dbg or []

    def din(name, shape, dt=F32):
        return nc.dram_tensor(name, list(shape), dt, kind='ExternalInput')

    def dscr(name, shape, dt):
        return nc.dram_tensor(name, list(shape), dt)

    x = din('x', [S, D])
    norm1_w = din('norm1_w', [1, D])
    w_in = din('w_in', [D, NIN])
    ropec = din('ropec', [128, S])
    ropes = din('ropes', [128, S])
    ident_bf_d = din('ident_bf', [128, 128], BF16)
    ident_f_d = din('ident_f', [128, 128])
    out_d = nc.dram_tensor('out', [S, D], F32, kind='ExternalOutput')
    G = {}
    for nme, shape, dt in (('maskC', [128, S], BF16), ('ov', [128, 32], BF16), ('E', [32, S], BF16),
                           ('masks8', [128, 8, 512], BF16), ('mb8', [128, 8, 512], BF16), ('maskCb', [128, S], BF16), ('tinyrow', [1, 128], BF16), ('onesrow', [1, 512], BF16), ('sel24', [24, 24 * 128], F32),
                           ('fm_mul', [128, 16, 32], F32), ('fm_add', [128, 16, 32], F32), ('ones_bf', [128, 128], BF16),
                           ('cmp_w1_k', [4096, 128], F32), ('cmp_w2_k', [128, 128], F32),
                           ('cmp_w1_v', [4096, 128], F32), ('cmp_w2_v', [128, 128], F32),
                           ('peT_k', [128, 32], F32), ('peT_v', [128, 32], F32),
                           ('tri', [128, 128], F32), ('sellast', [128, 128], F32), ('sel63', [128, 128], F32),
                           ('sel127', [128, 128], F32), ('mblow', [128, 128], F32), ('strictm', [128, 128], F32),
                           ('ones_f', [128, 128], F32), ('negmask', [128, 128], F32), ('convw', [128, 24, 4], F32), ('alog_rep', [128, 128], F32),
                           ('dtb_rep', [128, 128], F32), ('nw_rep', [1, 1024], F32),
                           ('w_up_nsa', [1024, D], F32), ('w_up_dn', [1024, D], F32), ('w_o', [D, D], F32),
                           ('norm2_w', [1, D], F32), ('w_ffn_gate', [D, DFF], F32), ('w_ffn_up', [D, DFF], F32),
                           ('w_ffn_down', [DFF, D], F32), ('norm_f_w', [1, D], F32)):
        G[nme] = din(nme, shape, dt)

    qT_d = dscr('qT_d', [8, 128, S], BF16)
    kvT_d = dscr('kvT_d', [6, 2, 128, S], BF16)
    vtok_d = dscr('vtok_d', [2, S, 256], BF16)
    gT_d = dscr('gT_d', [24, S], F32)
    dnqkvT_d = dscr('dnqkvT_d', [3072, S], F32)
    dnz_d = dscr('dnz_d', [S, 1024], F32)
    dnab_d = dscr('dnab_d', [S, 16], F32)
    mgT_d = dscr('mgT_d', [4096, S], BF16)
    onsaT_d = dscr('onsaT_d', [8, 128, S], BF16)
    odnT_d = dscr('odnT_d', [8, 128, S], BF16)
    ktok_d = dscr('ktok_d', [8, S, 128], BF16)
    gcT_d = dscr('gcT_d', [16, 1, 1024], F32)
    gqT_d = dscr('gqT_d', [8, 128, S], BF16)
    gkT_d = dscr('gkT_d', [8, 128, S], BF16)
    x1_d = dscr('x1_d', [S, D], F32)
    x2_d = dscr('x2_d', [S, D], F32)
    actT_d = dscr('actT_d', [16, 128, DFF // 128, 128], BF16)
    vtok2_d = dscr('vtok2_d', [8, S, 128], BF16)

    dbg_t = {}
    for nme, shape, dt in dbg:
        dbg_t[nme] = nc.dram_tensor('dbg_' + nme, list(shape), dt, kind='ExternalOutput')

    banks = []
    for b in range(6):
        t = nc.alloc_psum_tensor('psb%d' % b, [128, 512], F32)
        banks.append((t, ('ps', b)))
    tbanks = []
    for b in range(2):
        t = nc.alloc_psum_tensor('pst%d' % b, [128, 1024], BF16)
        tbanks.append((t, ('pst', b)))

    with ExitStack() as gs:
        ident_bf = _mk(gs, nc, 'ident_bf_s', [128, 128], BF16)
        ident_f = _mk(gs, nc, 'ident_f_s', [128, 128], F32)
        P.dma('sync', ident_bf[:], ident_bf_d[:, :], [], ['ident_bf'])
        P.dma('sync', ident_f[:], ident_f_d[:, :], [], ['ident_f'])
        epsb = _mk(gs, nc, 'epsb', [128, 1], F32)
        P.add('vector', lambda e: e.memset(epsb[:], EPS), [], ['epsb'])

        G.update(banks=banks, tbanks=tbanks, ident_f=ident_f, ident_bf=ident_bf, dbg_t=dbg_t, qT_d=qT_d, kvT_d=kvT_d,
                 vtok_d=vtok_d, gT_d=gT_d, onsaT_d=onsaT_d, odnT_d=odnT_d, ktok_d=ktok_d, vtok2_d=vtok2_d,
                 dnqkvT_d=dnqkvT_d, dnz_d=dnz_d, dnab_d=dnab_d, epsb=epsb, gcT_d=gcT_d, gqT_d=gqT_d, gkT_d=gkT_d)
        with ExitStack() as es:
            hT = _mk(es, nc, 'hT', [128, 16, S], BF16)
            with ExitStack() as es1:
                w1b = _mk(es1, nc, 'w1b', [128, D], F32)
                P.dma('scalar', w1b[:], norm1_w[0:1, :].partition_broadcast(128), [], ['w1b'])
                xrot = _rot(es1, nc, 'xt', 2, [128, D], F32)
                hbrot = _rot(es1, nc, 'hb', 2, [128, D], BF16)
                junk = _mk(es1, nc, 'junk', [128, D], BF16)
                ssq = _mk(es1, nc, 'ssq', [128, 16], F32)
                rstd = _mk(es1, nc, 'rstd', [128, 16], F32)
                for tt in range(16):
                    xt, xk = xrot.next()
                    hb, hk = hbrot.next()
                    P.dma('sync', xt[:], x[tt * 128:(tt + 1) * 128, :], [], [xk])
                    P.add('scalar', lambda e, xt=xt, tt=tt: e.activation(
                        junk[:], xt[:], AF.Square, accum_out=ssq[:, tt:tt + 1]),
                        reads=[xk], writes=['junk', ('ssq', tt)])
                    P.add('scalar', lambda e, tt=tt: e.activation(
                        rstd[:, tt:tt + 1], ssq[:, tt:tt + 1], AF.Sqrt, bias=epsb[:, 0:1], scale=1.0 / D),
                        reads=[('ssq', tt), 'epsb'], writes=[('rstd', tt)])
                    P.add('vector', lambda e, tt=tt: e.reciprocal(rstd[:, tt:tt + 1], rstd[:, tt:tt + 1]),
                        reads=[('rstd', tt)], writes=[('rstd', tt)])
                    P.add('vector', lambda e, xt=xt, hb=hb, tt=tt: e.scalar_tensor_tensor(
                        hb[:], xt[:], rstd[:, tt:tt + 1], w1b[:], ALU.mult, ALU.mult),
                        reads=[xk, ('rstd', tt), 'w1b'], writes=[hk])
                    for half in range(2):
                        pt, pk = tbanks[half]
                        for j in range(8):
                            kc = half * 8 + j
                            P.tr(pt[:, j * 128:(j + 1) * 128], hb[:, kc * 128:(kc + 1) * 128], ident_bf[:],
                                 reads=[hk, 'ident_bf'], writes=[pk], signal=(j == 7))
                        eng = 'scalar' if half == 0 else 'vector'
                        dst = hT[:, half * 8:(half + 1) * 8, tt * 128:(tt + 1) * 128]
                        src = pt[:, :].rearrange('p (j c) -> p j c', j=8)
                        if eng == 'scalar':
                            P.add('scalar', lambda e, dst=dst, src=src: e.copy(dst, src),
                                  reads=[pk], writes=[('hT', tt, half)])
                        else:
                            P.add('vector', lambda e, dst=dst, src=src: e.tensor_copy(dst, src),
                                  reads=[pk], writes=[('hT', tt, half)])
                P.flush()
            if 'hT' in dbg_t:
                P.dma('sync', dbg_t['hT'].ap().rearrange('(kc p) t -> p kc t', p=128), hT[:], [], [])
                P.flush()

            if stop_after >= 2:
                with ExitStack() as es2:
                    stf = _rot(es2, nc, 'stf', 4, [128, 512], F32)
                    stb = _rot(es2, nc, 'stb', 4, [128, 512], BF16)
                    wrot_in = _rot(es2, nc, 'inproj_w', 3, [128, 16, 512], BF16)
                    er = ExitStack()
                    cc = _mk(er, nc, 'cc', [128, S], F32)
                    ss = _mk(er, nc, 'ss', [128, S], F32)
                    P.dma('sync', cc[:], ropec[:, :], [], ['cc'])
                    P.dma('scalar', ss[:], ropes[:, :], [], ['ss'])
                    stf2 = _rot(er, nc, 'stf2', 3, [128, 512], F32)
                    oq = ['sync', 'scalar']
                    oqi = [0]

                    def outq():
                        oqi[0] += 1
                        return oq[oqi[0] % 2]

                    def epi_rope(dst_fn):
                        def epi(ps, pk, info):
                            t0 = info['t0']
                            a, ak = stf.next()
                            b, bk = stf2.next()
                            o, ok = stb.next()
                            P.add('vector', lambda e: e.tensor_tensor(a[:], ps, cc[:, t0:t0 + 512], ALU.mult),
                                  reads=[pk, 'cc'], writes=[ak])
                            P.add('vector', lambda e: e.tensor_tensor(b[0:64, :], ps[64:128, :], ss[0:64, t0:t0 + 512], ALU.mult),
                                  reads=[pk, 'ss'], writes=[bk])
                            P.add('vector', lambda e: e.tensor_tensor(b[64:128, :], ps[0:64, :], ss[64:128, t0:t0 + 512], ALU.mult),
                                  reads=[pk, 'ss'], writes=[(bk, 1)])
                            P.add('vector', lambda e: e.tensor_tensor(o[:], a[:], b[:], ALU.add),
                                  reads=[ak, bk, (bk, 1)], writes=[ok])
                            P.dma(outq(), dst_fn(info), o[:], reads=[ok], writes=[])
                        return epi

                    def epi_copy_feat(dst_fn, dt, func=None, key_fn=None):
                        def epi(ps, pk, info):
                            mw = info['mw']
                            if dt == BF16:
                                o, ok = stb.next()
                            else:
                                o, ok = stf.next()
                            if func is None:
                                P.add('scalar', lambda e: e.copy(o[0:mw, :], ps), reads=[pk], writes=[ok])
                            else:
                                P.add('scalar', lambda e: e.activation(o[0:mw, :], ps, func), reads=[pk], writes=[ok])
                            P.dma(outq(), dst_fn(info), o[0:mw, :], reads=[ok], writes=([key_fn(info)] if key_fn else []))
                        return epi

                    def epi_copy_tok(dst_fn, dt, func=None):
                        def epi(ps, pk, info):
                            nw = info['nw']
                            if dt == BF16:
                                o, ok = stb.next()
                            else:
                                o, ok = stf.next()
                            if func is None:
                                P.add('scalar', lambda e: e.copy(o[:, 0:nw], ps), reads=[pk], writes=[ok])
                            else:
                                P.add('scalar', lambda e: e.activation(o[:, 0:nw], ps, func), reads=[pk], writes=[ok])
                            P.dma(outq(), dst_fn(info), o[:, 0:nw], reads=[ok], writes=[])
                        return epi

                    blocks_q = []
                    for c0 in range(0, 1024, 512):
                        blocks_q.append((c0, 512, 'feat', epi_rope(
                            lambda info: qT_d[info['c0'] // 128, :, info['t0']:info['t0'] + 512])))
                    for i in range(6):
                        base = 1024 + i * 256
                        if i in (0, 2, 4):
                            blocks_q.append((base, 256, 'feat', epi_rope(
                                lambda info, i=i, base=base: kvT_d[i, (info['c0'] - base) // 128, :, info['t0']:info['t0'] + 512])))
                        elif i == 1:
                            blocks_q.append((base, 256, 'feat', epi_copy_feat(
                                lambda info, i=i, base=base: kvT_d[i, (info['c0'] - base) // 128, :, info['t0']:info['t0'] + 512], BF16)))
                        else:
                            blocks_q.append((base, 256, 'tok', epi_copy_tok(
                                lambda info, i=i: vtok_d[(i - 3) // 2, info['t0']:info['t0'] + 128, :], BF16)))
                    blocks_q.append((2560, 24, 'feat', epi_copy_feat(
                        lambda info: gT_d[0:24, info['t0']:info['t0'] + 512], F32, AF.Sigmoid)))
                    blocks_dn = []
                    for c0 in range(2584, 5656, 512):
                        blocks_dn.append((c0, 512, 'feat', epi_copy_feat(
                            lambda info: dnqkvT_d[info['c0'] - 2584:info['c0'] - 2584 + 128, info['t0']:info['t0'] + 512], F32,
                            key_fn=lambda info: ('dnqkvT', (info['c0'] - 2584) // 128, info['t0'] // 512))))
                    blocks_m = []
                    blocks_mg = []
                    for c0 in range(6696, 10792, 512):
                        blocks_mg.append((c0, 512, 'feat', epi_copy_feat(
                            lambda info: mgT_d[info['c0'] - 6696:info['c0'] - 6696 + 128, info['t0']:info['t0'] + 512], BF16)))
                    for c0 in range(5656, 6680, 512):
                        blocks_m.append((c0, 512, 'tok', epi_copy_tok(
                            lambda info: dnz_d[info['t0']:info['t0'] + 128, info['c0'] - 5656:info['c0'] - 5656 + 512], F32, AF.Silu)))
                    blocks_m.append((6680, 16, 'tok', epi_copy_tok(
                        lambda info: dnab_d[info['t0']:info['t0'] + 128, :], F32)))
                    blocks_m = blocks_m + blocks_mg
                    Wv = w_in.ap().rearrange('(kc p) n -> p kc n', p=128)
                    dense(P, nc, es2, 'inproj', hT, 'hTall', 16, Wv, blocks_dn + blocks_q, banks, wrot=wrot_in)
                    P.flush()
                    er.close()
                    with ExitStack() as ea:
                        gen = dense_gen(P, nc, es2, 'inproj', hT, 'hTall', 16, Wv, blocks_m, banks, wrot=wrot_in)
                        run_streams([gen, gdn_stepA_gen(P, nc, ea, G)])
                        P.flush()

        import os as _os
        if stop_after >= 3 and not _os.environ.get('SKIP_NSA'):
            phase_nsa(P, nc, G)
        if stop_after >= 4 and not _os.environ.get('SKIP_GDN'):
            phase_gdn(P, nc, G)
        G.update(x1_d=x1_d, x2_d=x2_d, actT_d=actT_d, mgT_d=mgT_d, x=x, out_d=out_d)
        if stop_after >= 5:
            phase_tail(P, nc, G)

        for nme, src in (('qT', qT_d), ('kvT', kvT_d), ('vtok', vtok_d), ('gT', gT_d), ('dnqkvT', dnqkvT_d),
                         ('dnz', dnz_d), ('dnab', dnab_d), ('mgT', mgT_d), ('onsaT', onsaT_d), ('odnT', odnT_d), ('x1', x1_d), ('x2', x2_d)):
            if nme in dbg_t:
                P.dma('sync', dbg_t[nme].ap(), src.ap(), [], [])
        P.flush()
        print('PROG stats', P.stats)
    return nc


_CACHE = {}


def make_shared_inputs(inputs):
    m = dict(host_consts())
    g = lambda k: np.ascontiguousarray(np.asarray(inputs[k], dtype=np.float32))
    m['norm1_w'] = g('norm1_w').reshape(1, D)
    m['w_in'] = g('w_in').reshape(D, NIN)
    for nm in ('k', 'v'):
        m['cmp_w1_' + nm] = g('cmp_w1_' + nm).reshape(4096, 128)
        m['cmp_w2_' + nm] = g('cmp_w2_' + nm).reshape(128, 128)
        m['peT_' + nm] = np.ascontiguousarray(g('cmp_pe_' + nm).reshape(32, 128).T)
    m['convw'] = np.ascontiguousarray(g('conv_w').reshape(4, 24, 128).transpose(2, 1, 0))
    m['alog_rep'] = np.ascontiguousarray(np.broadcast_to(np.tile(g('a_log').reshape(8), 16)[None, :], (128, 128)))
    m['dtb_rep'] = np.ascontiguousarray(np.broadcast_to(np.tile(g('dt_bias').reshape(8), 16)[None, :], (128, 128)))
    m['nw_rep'] = np.ascontiguousarray(np.tile(g('dn_norm_w').reshape(128), 8)[None, :])
    m['w_up_nsa'] = g('w_up_nsa').reshape(1024, D)
    m['w_up_dn'] = g('w_up_dn').reshape(1024, D)
    m['w_o'] = g('w_o').reshape(D, D)
    m['norm2_w'] = g('norm2_w').reshape(1, D)
    m['w_ffn_gate'] = g('w_ffn_gate').reshape(D, DFF)
    m['w_ffn_up'] = g('w_ffn_up').reshape(D, DFF)
    m['w_ffn_down'] = g('w_ffn_down').reshape(DFF, D)
    m['norm_f_w'] = g('norm_f_w').reshape(1, D)
    return m


def kernel(**inputs):
    if 'nc' not in _CACHE:
        _CACHE['nc'] = build()
    nc = _CACHE['nc']
    shared = make_shared_inputs(inputs)
    x = np.asarray(inputs['x'], dtype=np.float32)
    B = x.shape[0]
    in_maps = []
    for b in range(B):
        m = dict(shared)
        m['x'] = np.ascontiguousarray(x[b])
        in_maps.append(m)
    res = run_bass_kernel_spmd(nc, in_maps, core_ids=list(range(B)))
    return np.stack([np.asarray(r['out'], dtype=np.float32) for r in res.results], axis=0)
```

```python
import math
from contextlib import ExitStack

import numpy as np
import ml_dtypes
import concourse.bass as bass
import concourse.mybir as mybir
from concourse.bass_utils import run_bass_kernel_spmd

F32 = mybir.dt.float32
BF16 = mybir.dt.bfloat16
AF = mybir.ActivationFunctionType
ALU = mybir.AluOpType
AX = mybir.AxisListType

S = 2048
D = 2048
NIN = 10792
DFF = 5632
EPS = 1e-6
SAME_ENGINE_SYNC = True


class Prog:
    ENGS = ('tensor', 'vector', 'scalar', 'gpsimd', 'sync')
    DMAQ = ('sync', 'scalar', 'gpsimd')
    NDS = 6

    def __init__(self, nc):
        self.nc = nc
        self.ops = {e: [] for e in self.ENGS}
        self.state = {}
        self.esem = {e: nc.alloc_semaphore('s_' + e) for e in self.ENGS}
        self.dsem = {e: [nc.alloc_semaphore('d_%s_%d' % (e, i)) for i in range(self.NDS)] for e in self.DMAQ}
        self.dcnt = {e: 0 for e in self.DMAQ}
        self.dlast = {}
        self.base = {e: 0 for e in self.ENGS}
        self.known_c = {e: {} for e in self.ENGS}
        self.known_d = {e: {} for e in self.ENGS}
        self.stats = {}

    def add(self, eng, fn, reads=(), writes=(), signal=True, dma=False):
        idx = len(self.ops[eng])
        deps = []
        for k in reads:
            st = self.state.get(k)
            if st is not None and st['w'] is not None:
                deps.append(st['w'])
        for k in writes:
            st = self.state.get(k)
            if st is not None:
                if st['w'] is not None:
                    deps.append(st['w'])
                deps.extend(st['rc'].values())
                deps.extend(st['rd'])
        slot = None
        if dma:
            n = self.dcnt[eng]
            self.dcnt[eng] += 1
            slot = n % self.NDS
            me = ('d', (eng, slot), 16 * (n // self.NDS + 1))
            prev = self.dlast.get((eng, slot))
            if prev is not None:
                deps.append(prev)
            self.dlast[(eng, slot)] = me
        else:
            me = ('c', eng, idx)
        self.ops[eng].append(dict(fn=fn, deps=deps, signal=signal, dma=dma, slot=slot))
        for k in reads:
            st = self.state.setdefault(k, dict(w=None, rc={}, rd=[]))
            if dma:
                st['rd'].append(me)
            else:
                st['rc'][eng] = me
        for k in writes:
            self.state[k] = dict(w=me, rc={}, rd=[])

    def mm(self, out, lhsT, rhs, start, stop, reads, writes, signal=None):
        self.add('tensor', lambda e: e.matmul(out, lhsT, rhs, start=start, stop=stop),
                 reads, writes, signal=stop if signal is None else signal)

    def tr(self, out, in_, ident, reads, writes, signal=True):
        self.add('tensor', lambda e: e.transpose(out, in_, ident), reads, writes, signal=signal)

    def op(self, eng, meth, *args, reads=(), writes=(), **kw):
        self.add(eng, lambda e: getattr(e, meth)(*args, **kw), reads, writes)

    def dma(self, q, out, in_, reads, writes):
        self.add(q, lambda e: e.dma_start(out=out, in_=in_), reads, writes, dma=True)

    def barrier(self):
        deps = []
        for e in self.ENGS:
            for j in range(len(self.ops[e]) - 1, -1, -1):
                op = self.ops[e][j]
                if op['fn'] is not None and not op['dma']:
                    assert op['signal'], 'last op on %s before barrier must signal' % e
                    deps.append(('c', e, j))
                    break
        deps.extend(self.dlast.values())
        for e in self.ENGS:
            self.ops[e].append(dict(fn=None, deps=list(deps), signal=False, dma=False, slot=None))
        self.state = {}

    def flush(self):
        self.barrier()
        ops = self.ops
        cnt = {}
        nxt = {}
        for e in self.ENGS:
            c = self.base[e]
            cl = []
            for op in ops[e]:
                if op['fn'] is not None and not op['dma'] and op['signal']:
                    c += 1
                cl.append(c)
            cnt[e] = cl
            nl = [None] * len(ops[e])
            nx = None
            for j in range(len(ops[e]) - 1, -1, -1):
                op = ops[e][j]
                if op['fn'] is not None and not op['dma'] and op['signal']:
                    nx = cl[j]
                nl[j] = nx
            nxt[e] = nl
        self._nxt = nxt
        with self.nc.Block() as block:
            @block.tensor
            def _(e):
                self.emit('tensor', e)

            @block.vector
            def _(e):
                self.emit('vector', e)

            @block.scalar
            def _(e):
                self.emit('scalar', e)

            @block.gpsimd
            def _(e):
                self.emit('gpsimd', e)

            @block.sync
            def _(e):
                self.emit('sync', e)
        for e in self.ENGS:
            if cnt[e]:
                self.base[e] = cnt[e][-1]
            self.ops[e] = []

    def emit(self, ename, eng):
        ops = self.ops
        nxt = self._nxt
        known_c = self.known_c[ename]
        known_d = self.known_d[ename]
        n_wait = 0
        for i, op in enumerate(ops[ename]):
            for d in op['deps']:
                if d[0] == 'c':
                    _, f, j = d
                    if f == ename:
                        if ename == 'tensor' or not SAME_ENGINE_SYNC or j >= i:
                            continue
                        if not ops[f][j]['signal']:
                            continue
                    need = nxt[f][j]
                    assert need is not None, 'dependency on %s op %d never signals' % (f, j)
                    if known_c.get(f, 0) >= need:
                        continue
                    eng.wait_ge(self.esem[f], need)
                    known_c[f] = need
                    n_wait += 1
                else:
                    _, key, val = d
                    if known_d.get(key, 0) >= val:
                        continue
                    eng.wait_ge(self.dsem[key[0]][key[1]], val)
                    known_d[key] = val
                    n_wait += 1
            if op['fn'] is None:
                continue
            ins = op['fn'](eng)
            if op['dma']:
                ins.then_inc(self.dsem[ename][op['slot']], 16)
            elif op['signal']:
                ins.then_inc(self.esem[ename], 1)
        self.stats[ename] = self.stats.get(ename, 0) + len(ops[ename])
        self.stats[ename + '_w'] = self.stats.get(ename + '_w', 0) + n_wait


class Rot:
    def __init__(self, tiles, name):
        self.tiles = tiles
        self.name = name
        self.i = 0

    def next(self):
        j = self.i % len(self.tiles)
        self.i += 1
        return self.tiles[j], (self.name, j)


_UID = [0]


def _mk(es, nc, name, shape, dt):
    _UID[0] += 1
    return es.enter_context(nc.sbuf_tensor('%s_u%d' % (name, _UID[0]), shape, dt))


def _rot(es, nc, name, n, shape, dt):
    return Rot([_mk(es, nc, '%s%d' % (name, i), shape, dt) for i in range(n)], name)


def host_consts():
    c = {}
    inv = 1.0 / (10000.0 ** (np.arange(0, 128, 2, dtype=np.float32) / 128.0))
    ang = np.arange(S, dtype=np.float32)[:, None] * inv[None, :].astype(np.float32)
    ang = ang.astype(np.float32)
    cos = np.cos(ang).astype(np.float32).T
    sin = np.sin(ang).astype(np.float32).T
    c['ropec'] = np.ascontiguousarray(np.concatenate([cos, cos], 0))
    c['ropes'] = np.ascontiguousarray(np.concatenate([-sin, sin], 0))
    c['ident_bf'] = np.eye(128, dtype=np.float32).astype(ml_dtypes.bfloat16)
    c['ident_f'] = np.eye(128, dtype=np.float32)
    bf = ml_dtypes.bfloat16
    cc_ = np.arange(128)[:, None]
    qq = np.arange(S)[None, :]
    c['maskC'] = ((16 * cc_ + 31 <= qq) & (cc_ < 127)).astype(np.float32).astype(bf)
    cs = np.arange(128)[:, None] * 16
    ssb = np.arange(32)[None, :] * 64
    ov = np.clip(np.minimum(cs + 32, ssb + 64) - np.maximum(cs, ssb), 0, None) / 32.0
    ov[127] = 0
    c['ov'] = ov.astype(np.float32).astype(bf)
    c['E'] = (np.arange(S)[None, :] // 64 == np.arange(32)[:, None]).astype(np.float32).astype(bf)
    k_ = np.arange(128)[:, None]
    q_ = np.arange(512)[None, :]
    m8 = np.zeros((128, 8, 512), np.float32)
    for r in range(-4, 4):
        diff = q_ - (128 * r + k_)
        m8[:, r + 4, :] = ((diff >= 0) & (diff < 512))
    c['masks8'] = m8.astype(bf)
    c['mb8'] = ((m8 - 1.0) * 30000.0).astype(bf)
    c['maskCb'] = ((c['maskC'].astype(np.float32) - 1.0) * 30000.0).astype(bf)
    c['tinyrow'] = np.full((1, 128), 1e-30, np.float32).astype(bf)
    c['onesrow'] = np.ones((1, 512), np.float32).astype(bf)
    sel = np.zeros((24, 24, 128), np.float32)
    for i in range(24):
        sel[i, i, :] = 1
    c['sel24'] = sel.reshape(24, 24 * 128)
    t_ = np.arange(S)[:, None]
    j_ = np.arange(32)[None, :]
    blk = t_ // 64
    forced = (j_ == 0) | (j_ == blk) | (j_ == blk - 1)
    fut = j_ > blk
    fm_mul = (~(forced | fut)).astype(np.float32)
    fm_add = np.where(forced, 1e9, np.where(fut, -1e9, 0.0)).astype(np.float32)
    c['fm_mul'] = np.ascontiguousarray(fm_mul.reshape(16, 128, 32).transpose(1, 0, 2))
    c['fm_add'] = np.ascontiguousarray(fm_add.reshape(16, 128, 32).transpose(1, 0, 2))
    c['ones_bf'] = np.ones((128, 128), np.float32).astype(bf)
    p_ = np.arange(128)[:, None]
    f_ = np.arange(128)[None, :]
    same = (p_ // 64) == (f_ // 64)
    c['tri'] = ((p_ <= f_) & same).astype(np.float32)
    c['sellast'] = (p_ == 64 * (f_ // 64) + 63).astype(np.float32)
    c['sel63'] = np.broadcast_to(p_ == 63, (128, 128)).astype(np.float32).copy()
    c['sel127'] = np.broadcast_to(p_ == 127, (128, 128)).astype(np.float32).copy()
    c['mblow'] = np.where((f_ <= p_) & same, 0.0, 1e5).astype(np.float32)
    c['strictm'] = ((f_ < p_) & same).astype(np.float32)
    c['ones_f'] = np.ones((128, 128), np.float32)
    c['negmask'] = np.where((f_ <= p_) & same, 0.0, -1e5).astype(np.float32)
    return c


def dense_gen(P, nc, es, name, aT, aT_key, KC, Wv, blocks, banks, T=S, wbufs=3, wcols=512, wrot=None):
    if wrot is None:
        wrot = _rot(es, nc, name + '_w', wbufs, [128, KC, wcols], BF16)
    bi = [0]

    def nextbank():
        b = banks[bi[0] % len(banks)]
        bi[0] += 1
        return b

    for (c0, ncols, mode, epi) in blocks:
        wt, wk = wrot.next()
        P.dma('gpsimd', wt[:, :, 0:ncols], Wv[:, :, c0:c0 + ncols], reads=[], writes=[wk])
        if mode == 'feat':
            for m0 in range(0, ncols, 128):
                mw = min(128, ncols - m0)
                for t0 in range(0, T, 512):
                    ps, pk = nextbank()
                    for kc in range(KC):
                        P.mm(ps[0:mw, 0:512], wt[:, kc, m0:m0 + mw], aT[:, kc, t0:t0 + 512],
                             start=(kc == 0), stop=(kc == KC - 1), reads=[wk, aT_key], writes=[pk])
                    epi(ps[0:mw, 0:512], pk, dict(c0=c0 + m0, mw=mw, t0=t0))
                    yield
        else:
            for t0 in range(0, T, 128):
                ps, pk = nextbank()
                for kc in range(KC):
                    P.mm(ps[:, 0:ncols], aT[:, kc, t0:t0 + 128], wt[:, kc, 0:ncols],
                         start=(kc == 0), stop=(kc == KC - 1), reads=[wk, aT_key], writes=[pk])
                epi(ps[:, 0:ncols], pk, dict(c0=c0, nw=ncols, t0=t0))
                yield


def dense(*a, **kw):
    for _ in dense_gen(*a, **kw):
        pass


def run_streams(gens):
    gens = [g for g in gens if g is not None]
    while gens:
        for g in list(gens):
            try:
                next(g)
            except StopIteration:
                gens.remove(g)


def gdn_stepA_gen(P, nc, ea, G):
    tb = G['tbanks']
    sbank = (tb[0][0][:, :].bitcast(F32), tb[0][1])
    tbank = tb[1]
    ident_bf = G['ident_bf']
    epsb = G['epsb']

    def t8(ps):
        return ps[:, :].rearrange('p (a b) -> p a b', a=8)
    convw = _mk(ea, nc, 'convw', [128, 24, 4], F32)
    P.dma('sync', convw[:], G['convw'][:, :, :], [], ['convw'])
    ones_b = _mk(ea, nc, 'ones_bA', [128, 128], BF16)
    P.dma('sync', ones_b[:], G['ones_bf'][:, :], [], ['ones_bA'])
    xprot = _rot(ea, nc, 'xp', 2, [128, S + 3], F32)
    accrot = _rot(ea, nc, 'acc', 2, [128, S], F32)
    yrot = _rot(ea, nc, 'yy', 2, [128, S], F32)
    sq = _mk(ea, nc, 'sq', [128, S], BF16)
    outrot = _rot(ea, nc, 'qkT', 2, [128, S], BF16)
    rnrot = _rot(ea, nc, 'rn', 2, [128, 512], F32)
    st = _mk(ea, nc, 'tokst', [128, 16, 128], BF16)
    for i in range(2):
        P.op('vector', 'memset', xprot.tiles[i][:, 0:3], 0.0, reads=[], writes=[('xp', i, 'pad')])

    def stageX(c):
        xp, xk = xprot.next()
        acc, ak = accrot.next()
        P.dma('sync', xp[:, 3:S + 3], G['dnqkvT_d'][c * 128:(c + 1) * 128, :], [('dnqkvT', c, t4) for t4 in range(4)], [xk])
        P.op('vector', 'tensor_scalar', acc[:], xp[:, 0:S], convw[:, c, 0:1], None, ALU.mult, reads=[xk, xk + ('pad',), 'convw'], writes=[ak])
        yield
        for j in range(1, 4):
            P.op('vector', 'scalar_tensor_tensor', acc[:], xp[:, j:j + S], convw[:, c, j:j + 1], acc[:], ALU.mult, ALU.add,
                 reads=[xk, xk + ('pad',), 'convw', ak], writes=[ak])
            yield
        return_val[c] = (acc, ak)

    return_val = {}

    def stageY(c):
        which, h = c // 8, c % 8
        acc, ak = return_val[c]
        y, yk = yrot.next()
        P.op('scalar', 'activation', y[:], acc[:], AF.Silu, reads=[ak], writes=[yk])
        yield
        if which < 2:
            P.op('scalar', 'activation', sq[:], y[:], AF.Square, reads=[yk], writes=['sq'])
            ot, otk = outrot.next()
            for t4 in range(4):
                ps, pk = sbank
                P.mm(ps[:, :], ones_b[:, :], sq[:, t4 * 512:(t4 + 1) * 512], True, True, ['ones_bA', 'sq'], [pk])
                rn, rk = rnrot.next()
                P.op('scalar', 'activation', rn[:], ps[:, :], AF.Ln, bias=epsb[:, 0:1], reads=[pk, 'epsb'], writes=[rk])
                P.op('scalar', 'activation', rn[:], rn[:], AF.Exp, scale=-0.5, reads=[rk], writes=[rk])
                P.op('vector', 'scalar_tensor_tensor', ot[:, t4 * 512:(t4 + 1) * 512], y[:, t4 * 512:(t4 + 1) * 512],
                     (128 ** -0.5) if which == 0 else 1.0, rn[:], ALU.mult, ALU.mult, reads=[yk, rk], writes=[(otk, t4)])
                yield
            dstd = G['gqT_d'] if which == 0 else G['gkT_d']
            P.dma('scalar', dstd[h, :, :], ot[:, :], [(otk, t4) for t4 in range(4)], [])
            srcT = ot
            srck = [(otk, t4) for t4 in range(4)]
        else:
            P.op('scalar', 'copy', sq[:], y[:], reads=[yk], writes=['sq'])
            srcT = sq
            srck = ['sq']
        if which >= 1:
            for half in range(2):
                pt, pk = tbank
                for j in range(8):
                    tt = half * 8 + j
                    P.tr(pt[:, j * 128:(j + 1) * 128], srcT[:, tt * 128:(tt + 1) * 128], ident_bf[:, :],
                         srck + ['ident_bf'], [pk], signal=(j == 7))
                P.op('vector' if half == 0 else 'scalar', 'tensor_copy' if half == 0 else 'copy',
                     st[:, half * 8:(half + 1) * 8, :], t8(pt), reads=[pk], writes=[('tokst', half)])
                yield
            dstd = G['ktok_d'] if which == 1 else G['vtok2_d']
            P.dma('scalar', dstd[h].rearrange('(tt p) d -> p tt d', p=128), st[:, :, :], [('tokst', 0), ('tokst', 1)], [])
        yield

    for _ in stageX(0):
        yield
    for c in range(24):
        gx = stageX(c + 1) if c + 1 < 24 else iter(())
        gy = stageY(c)
        alive = [gx, gy]
        while alive:
            for g in list(alive):
                try:
                    next(g)
                except StopIteration:
                    alive.remove(g)
            yield


def phase_nsa(P, nc, G):
    banks = G['banks']
    tb = G['tbanks']
    SC = 128 ** -0.5
    sbank = [banks[0], banks[1], banks[2]]
    obank = [banks[3], banks[4]]
    ubank = [banks[5], (tb[0][0][:, :].bitcast(F32), tb[0][1])]
    gbank = (tb[1][0][:, :].bitcast(F32), tb[1][1])
    mbank = gbank
    cnt = dict(s=0, o=0)
    ident_f = G['ident_f']
    ident_bf = G['ident_bf']
    with ExitStack() as es:
        def ld(name, shape, dt, src, q='sync'):
            t = _mk(es, nc, name, shape, dt)
            P.dma(q, t[:], src, [], [name])
            return t
        maskCb = ld('maskCb', [128, S], BF16, G['maskCb'][:, :])
        mb8 = ld('mb8', [128, 8, 512], BF16, G['mb8'][:, :, :], 'scalar')
        ovt = ld('ovt', [128, 32], BF16, G['ov'][:, :])
        Et = ld('Et', [32, S], BF16, G['E'][:, :], 'scalar')
        ones = ld('ones', [128, 128], BF16, G['ones_bf'][:, :])
        tinyr = ld('tinyr', [1, 128], BF16, G['tinyrow'][:, :])
        onesr = ld('onesr', [1, 512], BF16, G['onesrow'][:, :])
        fm_mul = ld('fm_mul', [128, 16, 32], F32, G['fm_mul'][:, :, :], 'scalar')
        fm_add = ld('fm_add', [128, 16, 32], F32, G['fm_add'][:, :, :])
        W1 = []
        W2 = []
        peT = []
        for i, nm in enumerate(('k', 'v')):
            w1 = _mk(es, nc, 'cw1' + nm, [128, 32, 128], BF16)
            P.dma('gpsimd', w1[:], G['cmp_w1_' + nm].ap().rearrange('(l d) f -> d l f', d=128), [], ['cw1' + nm])
            w2 = _mk(es, nc, 'cw2' + nm, [128, 128], BF16)
            P.dma('gpsimd', w2[:], G['cmp_w2_' + nm][:, :], [], ['cw2' + nm])
            pt = _mk(es, nc, 'cpe' + nm, [128, 32], BF16)
            P.dma('gpsimd', pt[:], G['peT_' + nm][:, :], [], ['cpe' + nm])
            W1.append(w1); W2.append(w2); peT.append(pt)
        qtile = [_rot(es, nc, 'qt%d' % hl, 2, [128, 512], BF16) for hl in range(4)]
        ptrot = _rot(es, nc, 'pT', 4, [128, 512], BF16)
        rsrot = _rot(es, nc, 'rs', 2, [128, 512], F32)
        posrot = _rot(es, nc, 'pos', 2, [128, 512], F32)
        gbrot = _rot(es, nc, 'gbs', 4, [128, 512], F32)
        tgrot = _rot(es, nc, 'tg', 2, [128, 512], F32)
        tmrot = _rot(es, nc, 'tmpo', 2, [128, 512], F32)
        oacc = [_mk(es, nc, 'oacc%d' % i, [128, 512], F32) for i in range(4)]
        obf = _rot(es, nc, 'obf', 2, [128, 512], BF16)
        impacc = _mk(es, nc, 'impacc', [32, 512], F32)
        imptok = _mk(es, nc, 'imptok', [128, 4, 32], F32)
        impw = _mk(es, nc, 'impw', [128, 4, 32], F32)
        mx8 = _mk(es, nc, 'mx8', [128, 8], F32)
        thr = _mk(es, nc, 'thr', [128, 1], F32)
        selb = _mk(es, nc, 'selb', [128, 4, 32], F32)
        biasT = _mk(es, nc, 'biasT', [32, 512], BF16)

        for hk in range(2):
            with ExitStack() as eh:
                def ldh(name, shape, src, q):
                    t = _mk(eh, nc, name, shape, BF16)
                    P.dma(q, t[:], src, [], [name])
                    return t
                kcx = ldh('kcx', [128, S], G['kvT_d'][0, hk, :, :], 'sync')
                vcx = ldh('vcx', [128, S], G['kvT_d'][1, hk, :, :], 'scalar')
                ksT = ldh('ksT', [128, S], G['kvT_d'][2, hk, :, :], 'sync')
                kwT = ldh('kwT', [128, S], G['kvT_d'][4, hk, :, :], 'scalar')
                vs = ldh('vs', [128, 16, 128], G['vtok_d'][0].rearrange('(tt p) c -> p tt c', p=128)[:, :, hk * 128:(hk + 1) * 128], 'sync')
                vw = ldh('vw', [128, 16, 128], G['vtok_d'][1].rearrange('(tt p) c -> p tt c', p=128)[:, :, hk * 128:(hk + 1) * 128], 'scalar')
                kcT = _mk(eh, nc, 'kcT', [128, 128], BF16)
                vc = _mk(eh, nc, 'vc', [128, 128], BF16)
                cb = _mk(eh, nc, 'cb', [128, 1], F32)
                cu = _mk(eh, nc, 'cu', [128, 128], F32)
                ct = _mk(eh, nc, 'ct', [128, 128], F32)
                cg = _mk(eh, nc, 'cg', [128, 128], BF16)
                for i, (xT, xk) in enumerate(((kcx, 'kcx'), (vcx, 'vcx'))):
                    nm = 'kv'[i]
                    ps, pk = mbank
                    for l in range(32):
                        P.mm(ps[:, 0:127], W1[i][:, l, :], xT[:, l:l + 16 * 126 + 1:16], start=(l == 0), stop=(l == 31),
                             reads=['cw1' + nm, xk], writes=[pk])
                    ps2, pk2 = sbank[0]
                    for l in range(32):
                        P.mm(ps2[:, 0:1], W1[i][:, l, :], peT[i][:, l:l + 1], start=(l == 0), stop=(l == 31),
                             reads=['cw1' + nm, 'cpe' + nm], writes=[pk2])
                    P.op('vector', 'tensor_copy', cb[:], ps2[:, 0:1], reads=[pk2], writes=['cb'])
                    P.op('scalar', 'activation', cu[:, 0:127], ps[:, 0:127], AF.Identity, bias=cb[:, 0:1], reads=[pk, 'cb'], writes=['cu'])
                    P.op('vector', 'tensor_tensor', ct[:, 0:127], cu[:, 0:127], cu[:, 0:127], ALU.mult, reads=['cu'], writes=['ct'])
                    P.op('vector', 'tensor_scalar', ct[:, 0:127], ct[:, 0:127], 0.044715, 1.0, ALU.mult, ALU.add, reads=['ct'], writes=['ct'])
                    P.op('vector', 'tensor_tensor', ct[:, 0:127], ct[:, 0:127], cu[:, 0:127], ALU.mult, reads=['ct', 'cu'], writes=['ct'])
                    P.op('scalar', 'activation', ct[:, 0:127], ct[:, 0:127], AF.Tanh, scale=0.7978845608028654, reads=['ct'], writes=['ct'])
                    P.op('vector', 'scalar_tensor_tensor', ct[:, 0:127], ct[:, 0:127], 1.0, cu[:, 0:127], ALU.add, ALU.mult,
                         reads=['ct', 'cu'], writes=['ct'])
                    P.op('vector', 'tensor_scalar', cg[:, 0:127], ct[:, 0:127], 0.5, None, ALU.mult, reads=['ct'], writes=['cg'])
                    if i == 0:
                        P.mm(ps[:, 0:127], W2[0][:, :], cg[:, 0:127], True, True, ['cw2k', 'cg'], [pk])
                        P.op('vector', 'tensor_copy', kcT[:, 0:127], ps[:, 0:127], reads=[pk], writes=['kcT'])
                    else:
                        P.mm(ps[0:127, 0:128], cg[:, 0:127], W2[1][:, :], True, True, ['cw2v', 'cg'], [pk])
                        P.op('vector', 'tensor_copy', vc[0:127, :], ps[0:127, 0:128], reads=[pk], writes=['vc'])

                def load_q(qi_):
                    res = []
                    for hl in range(4):
                        h = hk * 4 + hl
                        t_, k_ = qtile[hl].next()
                        P.dma('sync', t_[:], G['qT_d'][h, :, qi_ * 512:(qi_ + 1) * 512], [], [k_])
                        res.append((t_, k_))
                    return res
                qnext = load_q(0)
                for qi in range(4):
                    q0 = qi * 512
                    qh = qnext
                    if qi + 1 < 4:
                        qnext = load_q(qi + 1)

                    tiles = []
                    for hl in range(4):
                        tiles.append(dict(br=0, hl=hl, ki=0, n=0, last=True))
                    ntile_cmp = 4
                    for hl in range(4):
                        kis = list(range(max(0, 4 * qi - 4), 4 * qi + 4))
                        for n_, ki in enumerate(kis):
                            tiles.append(dict(br=2, hl=hl, ki=ki, n=n_, last=(n_ == len(kis) - 1)))
                    for hl in range(4):
                        kis = list(range(0, 4 * qi + 4))
                        for n_, ki in enumerate(kis):
                            tiles.append(dict(br=1, hl=hl, ki=ki, n=n_, last=(n_ == len(kis) - 1)))
                    first_sel = next(i for i, t in enumerate(tiles) if t['br'] == 1)

                    def stage1(t):
                        hl, br, ki = t['hl'], t['br'], t['ki']
                        qt, qk = qh[hl]
                        pS, pSk = sbank[cnt['s'] % 3]; cnt['s'] += 1
                        t['pS'] = (pS, pSk)
                        if br == 0:
                            P.mm(pS[0:127, :], kcT[:, 0:127], qt[:, :], True, False, ['kcT', qk], [pSk], signal=False)
                            P.mm(pS[0:127, :], ident_bf[0:127, 0:127], maskCb[0:127, q0:q0 + 512], False, True, ['ident_bf', 'maskCb'], [pSk])
                        elif br == 1:
                            r = ki - 4 * qi
                            P.mm(pS[:, :], ksT[:, ki * 128:(ki + 1) * 128], qt[:, :], True, False, ['ksT', qk], [pSk], signal=False)
                            if r >= 0:
                                P.mm(pS[:, :], ident_bf[:, :], mb8[:, r + 4, :], False, False, ['ident_bf', 'mb8'], [pSk], signal=False)
                            P.mm(pS[:, :], Et[0:32, ki * 128:(ki + 1) * 128], biasT[0:32, :], False, True, ['Et', 'biasT'], [pSk])
                        else:
                            r = ki - 4 * qi
                            P.mm(pS[:, :], kwT[:, ki * 128:(ki + 1) * 128], qt[:, :], True, False, ['kwT', qk], [pSk], signal=False)
                            P.mm(pS[:, :], ident_bf[:, :], mb8[:, r + 4, :], False, True, ['ident_bf', 'mb8'], [pSk])
                        if t['n'] == 0:
                            t['acc'] = (obank[cnt['o'] % 2], ubank[cnt['o'] % 2]); cnt['o'] += 1
                            h = hk * 4 + hl
                            gidx = h * 3 + br
                            gb, gbk = gbrot.next()
                            P.dma('sync', gb[:, :], G['gT_d'][gidx:gidx + 1, q0:q0 + 512].partition_broadcast(128), [], [gbk])
                            t['gb'] = (gb, gbk)

                    def stage2(t, head_t):
                        hl, br, ki = t['hl'], t['br'], t['ki']
                        pS, pSk = t['pS']
                        (po, pok), (pu, puk) = head_t['acc']
                        np_ = 127 if br == 0 else 128
                        pT, pTk = ptrot.next()
                        P.op('scalar', 'activation', pT[0:np_, :], pS[0:np_, :], AF.Exp, scale=SC, reads=[pSk], writes=[pTk])
                        first, last = t['n'] == 0, t['last']
                        if br == 0:
                            vt, vk = vc[0:127, :], 'vc'
                        elif br == 1:
                            vt, vk = vs[:, ki, :], 'vs'
                        else:
                            vt, vk = vw[:, ki, :], 'vw'
                        P.mm(po[:, :], vt, pT[0:np_, :], first, last, [vk, pTk], [pok])
                        if br == 0:
                            P.mm(pu[:, :], ones[0:127, :], pT[0:127, :], True, False, ['ones', pTk], [puk], signal=False)
                            P.mm(pu[:, :], tinyr[0:1, :], onesr[0:1, :], False, True, ['tinyr', 'onesr'], [puk])
                            pi_, pik = mbank
                            P.mm(pi_[0:32, :], ovt[0:127, :], pT[0:127, :], True, True, ['ovt', pTk], [pik])
                        else:
                            P.mm(pu[:, :], ones[:, :], pT[:, :], first, last, ['ones', pTk], [puk])
                        if not last:
                            return
                        gb, gbk = head_t['gb']
                        tg, tk = tgrot.next()
                        pos, posk = posrot.next()
                        P.op('vector', 'tensor_copy', pos[:], po[:, :], reads=[pok], writes=[posk])
                        def fin():
                            finish_rest(br, hl, pu, puk, pos, posk, gb, gbk, tg, tk, pi_ if br == 0 else None, pik if br == 0 else None)
                        if br == 0:
                            fin()
                        else:
                            pending.append([2, fin])

                    def finish_rest(br, hl, pu, puk, pos, posk, gb, gbk, tg, tk, pi_, pik):
                        rs, rk = rsrot.next()
                        P.op('scalar', 'activation', rs[:], pu[:, :], AF.Ln, reads=[puk], writes=[rk])
                        P.op('scalar', 'activation', rs[:], rs[:], AF.Exp, scale=-1.0, reads=[rk], writes=[rk])
                        if br == 0:
                            if hl == 0:
                                P.op('vector', 'tensor_tensor', impacc[:, :], pi_[0:32, :], rs[0:32, :], ALU.mult, reads=[pik, rk], writes=['impacc'])
                            else:
                                it, itk = tmrot.next()
                                P.op('vector', 'tensor_tensor', it[0:32, :], pi_[0:32, :], rs[0:32, :], ALU.mult, reads=[pik, rk], writes=[itk])
                                P.op('gpsimd', 'tensor_tensor', impacc[:, :], impacc[:, :], it[0:32, :], ALU.add, reads=[itk, 'impacc'], writes=['impacc'])
                        P.op('vector', 'tensor_tensor', tg[:], rs[:], gb[:, :], ALU.mult, reads=[rk, gbk], writes=[tk])
                        if br == 0:
                            P.op('vector', 'tensor_tensor', oacc[hl][:], pos[:], tg[:], ALU.mult, reads=[posk, tk], writes=[('oacc', hl)])
                        else:
                            tm, tmk = tmrot.next()
                            P.op('vector', 'tensor_tensor', tm[:], pos[:], tg[:], ALU.mult, reads=[posk, tk], writes=[tmk])
                            P.op('gpsimd', 'tensor_tensor', oacc[hl][:], oacc[hl][:], tm[:], ALU.add, reads=[tmk, ('oacc', hl)], writes=[('oacc', hl)])
                        if br == 1:
                            h = hk * 4 + hl
                            ob, obk = obf.next()
                            P.op('gpsimd', 'tensor_copy', ob[:], oacc[hl][:], reads=[('oacc', hl)], writes=[obk])
                            P.dma('sync', G['onsaT_d'][h, :, q0:q0 + 512], ob[:], [obk], [])

                    def topk():
                        pm, pmk = mbank
                        for s4 in range(4):
                            P.tr(pm[:, s4 * 32:(s4 + 1) * 32], impacc[0:32, s4 * 128:(s4 + 1) * 128], ident_f[0:32, 0:32],
                                 ['impacc', 'ident_f'], [pmk], signal=(s4 == 3))
                        P.op('vector', 'tensor_copy', imptok[:, :, :], pm[:, 0:128].rearrange('p (a b) -> p a b', a=4), reads=[pmk], writes=['imptok'])
                        P.op('vector', 'tensor_tensor', imptok[:, :, :], imptok[:, :, :], fm_mul[:, qi * 4:(qi + 1) * 4, :], ALU.mult,
                             reads=['imptok', 'fm_mul'], writes=['imptok'])
                        P.op('vector', 'tensor_tensor', imptok[:, :, :], imptok[:, :, :], fm_add[:, qi * 4:(qi + 1) * 4, :], ALU.add,
                             reads=['imptok', 'fm_add'], writes=['imptok'])
                        for s4 in range(4):
                            P.op('vector', 'max', mx8[:, :], imptok[:, s4, :], reads=['imptok'], writes=['mx8'])
                            P.op('vector', 'match_replace', impw[:, s4, :], mx8[:, :], imptok[:, s4, :], -3e9, reads=['imptok', 'mx8'], writes=[('impw', s4)])
                            P.op('vector', 'max', mx8[:, :], impw[:, s4, :], reads=[('impw', s4)], writes=['mx8'])
                            P.op('vector', 'tensor_reduce', thr[:, :], mx8[:, :], AX.X, ALU.min, reads=['mx8'], writes=['thr'])
                            P.op('vector', 'tensor_scalar', selb[:, s4, :], imptok[:, s4, :], thr[:, 0:1], None, ALU.is_ge,
                                 reads=['imptok', 'thr'], writes=[('selb', s4)])
                            P.op('vector', 'tensor_scalar', selb[:, s4, :], selb[:, s4, :], 30000.0, -30000.0, ALU.mult, ALU.add,
                                 reads=[('selb', s4)], writes=[('selb', s4)])

                    def topk2():
                        pm, pmk = mbank
                        for s4 in range(4):
                            P.tr(pm[0:32, s4 * 128:(s4 + 1) * 128], selb[:, s4, :], ident_f[:, :], [('selb', s4), 'ident_f'], [pmk], signal=(s4 == 3))
                        P.op('vector', 'tensor_copy', biasT[:, :], pm[0:32, :], reads=[pmk], writes=['biasT'])
                        if 'biasT' in G['dbg_t']:
                            P.dma('sync', G['dbg_t']['biasT'][hk, :, q0:q0 + 512], biasT[:, :], ['biasT'], [])

                    heads = {}
                    LOOK = 2
                    issued = 0

                    def issue_upto(lim):
                        nonlocal issued
                        while issued < min(lim, len(tiles)):
                            if tiles[issued]['br'] == 1 and not sel_ready[0]:
                                if not topk1_done[0]:
                                    break
                                topk2()
                                sel_ready[0] = True
                            stage1(tiles[issued])
                            issued += 1
                    topk1_done = [False]
                    pending = []
                    sel_ready = [False]
                    issue_upto(LOOK)
                    for n in range(len(tiles)):
                        t = tiles[n]
                        key = (t['br'], t['hl'])
                        if t['n'] == 0:
                            heads[key] = t
                        issue_upto(n + 1 + LOOK)
                        if issued <= n:
                            issue_upto(n + 1)
                        stage2(t, heads[key])
                        for pf in list(pending):
                            pf[0] -= 1
                            if pf[0] < 0:
                                pending.remove(pf)
                                pf[1]()
                        if n == min(ntile_cmp + 1, first_sel - 1):
                            topk()
                            topk1_done[0] = True
                    for pf in pending:
                        pf[1]()
                    pending = []
                P.flush()


def phase_gdn(P, nc, G):
    banks = G['banks']
    tb = G['tbanks']
    bi = [0, 0]

    def nb():
        b = banks[bi[0] % 6]
        bi[0] += 1
        return b

    def ntb():
        b = tb[bi[1] % 2]
        bi[1] += 1
        return b

    def b4(ps):
        return ps[:, :].rearrange('p (a b) -> p a b', a=4)

    def t8(ps):
        return ps[:, :].rearrange('p (a b) -> p a b', a=8)

    ident_f = G['ident_f']
    ident_bf = G['ident_bf']
    BIGK = ['gq', 'gk']
    with ExitStack() as es:
        def ld(name, shape, dt, src, q='sync'):
            t = _mk(es, nc, name, shape, dt)
            P.dma(q, t[:], src, [], [name])
            return t
        tri = ld('tri', [128, 128], F32, G['tri'][:, :])
        sellast = ld('sellast', [128, 128], F32, G['sellast'][:, :], 'scalar')
        sel63 = ld('sel63', [128, 128], F32, G['sel63'][:, :])
        sel127 = ld('sel127', [128, 128], F32, G['sel127'][:, :], 'scalar')
        mblow = ld('mblow', [128, 128], F32, G['mblow'][:, :])
        strict = ld('strict', [128, 128], F32, G['strictm'][:, :], 'scalar')
        ones_f = ld('ones_f', [128, 128], F32, G['ones_f'][:, :])
        ones_b = ld('ones_b', [128, 128], BF16, G['ones_bf'][:, :], 'scalar')
        alr = ld('alr', [128, 128], F32, G['alog_rep'][:, :], 'scalar')
        dtr = ld('dtr', [128, 128], F32, G['dtb_rep'][:, :])
        nwb = ld('nwb', [128, 1024], F32, G['nw_rep'][0:1, :].partition_broadcast(128), 'scalar')
        epsb = G['epsb']
        ktok_d = G['ktok_d']
        vtok2_d = G['vtok2_d']

        ab = ld('ab', [128, 16, 16], F32, G['dnab_d'].ap().rearrange('(tt p) c -> p tt c', p=128))
        names = ['beta', 'gg', 'gc', 'glsel', 'egc', 'ekd', 'negbeta', 'bgc', 'tmpa', 'tmpb']
        sc = {n: _mk(es, nc, 'sc_' + n, [128, 128], F32) for n in names}
        egl2 = _mk(es, nc, 'egl2', [128, 16, 2, 8], F32)

        def v3(t):
            return t[:, :].rearrange('p (a b) -> p a b', a=16)
        P.op('scalar', 'activation', v3(sc['beta']), ab[:, :, 8:16], AF.Sigmoid, reads=['ab'], writes=['beta'])
        P.op('vector', 'tensor_tensor', v3(sc['tmpa']), ab[:, :, 0:8], v3(dtr), ALU.add, reads=['ab', 'dtr'], writes=['tmpa'])
        P.op('scalar', 'activation', sc['tmpa'][:, :], sc['tmpa'][:, :], AF.Exp, reads=['tmpa'], writes=['tmpa'])
        P.op('vector', 'tensor_scalar', sc['tmpa'][:, :], sc['tmpa'][:, :], 1.0, None, ALU.add, reads=['tmpa'], writes=['tmpa'])
        P.op('scalar', 'activation', sc['tmpa'][:, :], sc['tmpa'][:, :], AF.Ln, reads=['tmpa'], writes=['tmpa'])
        P.op('scalar', 'activation', sc['tmpb'][:, :], alr[:, :], AF.Exp, reads=['alr'], writes=['tmpb'])
        P.op('vector', 'scalar_tensor_tensor', sc['gg'][:, :], sc['tmpa'][:, :], -1.0, sc['tmpb'][:, :], ALU.mult, ALU.mult,
             reads=['tmpa', 'tmpb'], writes=['gg'])
        ps, pk = nb()
        P.mm(ps[:, 0:128], tri[:, :], sc['gg'][:, :], True, True, ['tri', 'gg'], [pk])
        P.op('vector', 'tensor_copy', sc['gc'][:, :], ps[:, 0:128], reads=[pk], writes=['gc'])
        ps, pk = nb()
        P.mm(ps[:, 0:128], sellast[:, :], sc['gc'][:, :], True, True, ['sellast', 'gc'], [pk])
        P.op('vector', 'tensor_tensor', sc['tmpa'][:, :], ps[:, 0:128], sc['gc'][:, :], ALU.subtract, reads=[pk, 'gc'], writes=['tmpa'])
        P.op('scalar', 'activation', sc['ekd'][:, :], sc['tmpa'][:, :], AF.Exp, reads=['tmpa'], writes=['ekd'])
        for ci, selm in enumerate((sel63, sel127)):
            ps, pk = nb()
            P.mm(ps[:, 0:128], selm[:, :], sc['gc'][:, :], True, True, ['sel63', 'sel127', 'gc'], [pk])
            P.op('scalar', 'activation', egl2[:, :, ci, :], ps[:, 0:128].rearrange('p (a b) -> p a b', a=16), AF.Exp,
                 reads=[pk], writes=[('egl2', ci)])
        P.op('scalar', 'activation', sc['egc'][:, :], sc['gc'][:, :], AF.Exp, reads=['gc'], writes=['egc'])
        P.op('vector', 'tensor_scalar', sc['negbeta'][:, :], sc['beta'][:, :], -1.0, None, ALU.mult, reads=['beta'], writes=['negbeta'])
        P.op('vector', 'tensor_tensor', sc['bgc'][:, :], sc['beta'][:, :], sc['egc'][:, :], ALU.mult, reads=['beta', 'egc'], writes=['bgc'])
        if 'gdn_sc' in G['dbg_t']:
            for i_, n_ in enumerate(('beta', 'gg', 'gc', 'ekd')):
                P.dma('sync', G['dbg_t']['gdn_sc'][i_].rearrange('(tt p) h -> p tt h', p=128), v3(sc[n_]), [n_], [])

        gcT_s = _mk(es, nc, 'gcT_s', [8, S], F32)
        gc3 = v3(sc['gc']); egc3 = v3(sc['egc']); nb3 = v3(sc['negbeta']); bgc3 = v3(sc['bgc'])
        beta3 = v3(sc['beta']); ekd3 = v3(sc['ekd'])
        for q4 in range(4):
            ps, pk = nb()
            for j in range(4):
                tt = q4 * 4 + j
                P.tr(ps[0:8, j * 128:(j + 1) * 128], gc3[:, tt, :], ident_f[:, :], ['gc', 'ident_f'], [pk], signal=(j == 3))
            P.op('vector', 'tensor_copy', gcT_s[:, q4 * 512:(q4 + 1) * 512], ps[0:8, :], reads=[pk], writes=[('gcT_s', q4)])
        P.dma('sync', G['gcT_d'].ap().rearrange('tt o (h t) -> h (tt o) t', h=8), gcT_s[:, :].rearrange('h (tt t) -> h tt t', tt=16),
              [('gcT_s', q4) for q4 in range(4)], ['gcT_d'])

        def T3(name, dt):
            return _mk(es, nc, name, [128, 8, 128], dt)
        decay = T3('decay', F32)
        NM = [[T3('N%d' % i, BF16), T3('M%d' % i, BF16)] for i in range(2)]
        PTf = T3('PTf', F32); PTb = T3('PTb', BF16)
        Nf = T3('Nf', F32)
        attn = T3('attn', BF16)
        vb = T3('vb', BF16); kbg = T3('kbg', BF16)
        vn = T3('vn', BF16)
        tmpo = T3('tmpo', F32); oall = T3('oall', F32); o2 = T3('o2', F32); onb = T3('onb', BF16)
        ostg = T3('ostg', BF16)
        attnT2 = [T3('attnT%d' % i, BF16) for i in range(2)]
        kdec2 = [T3('kdec%d' % i, BF16) for i in range(2)]
        uf2 = [T3('uf%d' % i, F32) for i in range(2)]
        wT2 = [T3('wT%d' % i, BF16) for i in range(2)]
        ktok2 = [T3('ktok%d' % i, BF16) for i in range(2)]
        vtok2 = [T3('vtok%d' % i, BF16) for i in range(2)]
        zs2 = [_mk(es, nc, 'zs%d' % i, [128, 1024], F32) for i in range(3)]
        grow2 = [_mk(es, nc, 'grow%d' % i, [128, 1024], F32) for i in range(2)]
        qTt3 = [T3('qTt%d' % i, BF16) for i in range(3)]
        kTt2 = [T3('kTt%d' % i, BF16) for i in range(2)]
        negmask = ld('negmask', [128, 128], F32, G['negmask'][:, :])
        Sf = T3('Sf', F32); Sb = T3('Sb', BF16)
        ssq = _mk(es, nc, 'gssq', [128, 8], F32)
        P.op('vector', 'memset', Sf[:, :, :], 0.0, reads=[], writes=[('Sf', 0), ('Sf', 1)])
        P.op('vector', 'memset', Sb[:, :, :], 0.0, reads=[], writes=['Sb'])

        def bc_h(ap2):
            return ap2.unsqueeze(2).to_broadcast([128, 8, 128])

        def bc_m(ap2):
            return ap2.unsqueeze(1).to_broadcast([128, 8, 128])

        def loads(tt):
            pb = tt % 2
            tl = slice(tt * 128, (tt + 1) * 128)
            P.dma('sync', ktok2[pb][:, :, :], ktok_d[:, tl, :].rearrange('h p d -> p h d'), [], [('ktok', pb)])
            P.dma('scalar', vtok2[pb][:, :, :], vtok2_d[:, tl, :].rearrange('h p d -> p h d'), [], [('vtok', pb)])
            P.dma('sync', zs2[tt % 3][:, :], G['dnz_d'][tl, :], [], [('zs', tt % 3)])
            P.dma('scalar', grow2[pb][:, :], G['gcT_d'][tt, 0:1, :].partition_broadcast(128), ['gcT_d'], [('grow', pb)])
            P.dma('sync', qTt3[tt % 3][:, :, :], G['gqT_d'][:, :, tl].rearrange('h p t -> p h t'), [], [('qTt', tt % 3)])
            P.dma('scalar', kTt2[pb][:, :, :], G['gkT_d'][:, :, tl].rearrange('h p t -> p h t'), [], [('kTt', pb)])

        def prep(tt):
            pb = tt % 2
            tl = slice(tt * 128, (tt + 1) * 128)
            ktok, vtok, grow = ktok2[pb], vtok2[pb], grow2[pb]
            qTt, kTt = qTt3[tt % 3], kTt2[pb]
            attnT, kdec, uf, wT = attnT2[pb], kdec2[pb], uf2[pb], wT2[pb]
            if tt + 1 < 16:
                loads(tt + 1)
            P.op('vector', 'tensor_tensor', decay[:, :, :], bc_h(gc3[:, tt, :]), grow[:, :].rearrange('p (a b) -> p a b', a=8), ALU.subtract,
                 reads=['gc', ('grow', pb)], writes=['decay'])
            P.op('vector', 'tensor_tensor', decay[:, :, :], decay[:, :, :], bc_m(negmask[:, :]), ALU.add, reads=['decay', 'negmask'], writes=['decay'])
            P.op('scalar', 'activation', decay[:, :, :], decay[:, :, :], AF.Exp, reads=['decay'], writes=['decay'])
            yield
            for hb in range(2):
                ps, pk = nb()
                for hl in range(4):
                    h = hb * 4 + hl
                    P.mm(ps[:, hl * 128:(hl + 1) * 128], kTt[:, h, :], kTt[:, h, :], True, True, [('kTt', pb)], [pk])
                P.op('vector', 'tensor_tensor', Nf[:, hb * 4:hb * 4 + 4, :], b4(ps), decay[:, hb * 4:hb * 4 + 4, :], ALU.mult,
                     reads=[pk, 'decay'], writes=[('Nf', hb)])
                ps2, pk2 = nb()
                for hl in range(4):
                    h = hb * 4 + hl
                    P.mm(ps2[:, hl * 128:(hl + 1) * 128], qTt[:, h, :], kTt[:, h, :], True, True, [('qTt', tt % 3), ('kTt', pb)], [pk2])
                P.op('vector', 'tensor_tensor', attn[:, hb * 4:hb * 4 + 4, :], b4(ps2), decay[:, hb * 4:hb * 4 + 4, :], ALU.mult,
                     reads=[pk2, 'decay'], writes=[('attn', hb)])
                yield
            P.op('gpsimd', 'tensor_tensor', Nf[:, :, :], Nf[:, :, :], bc_h(nb3[:, tt, :]), ALU.mult,
                 reads=[('Nf', 0), ('Nf', 1), 'negbeta'], writes=[('Nf', 0), ('Nf', 1)])
            N1, M1 = NM[0]
            P.op('gpsimd', 'tensor_tensor', N1[:, :, :], Nf[:, :, :], bc_m(strict[:, :]), ALU.mult,
                 reads=[('Nf', 0), ('Nf', 1), 'strict'], writes=[('N', 0, 0), ('N', 0, 1)])
            P.op('gpsimd', 'tensor_tensor', vb[:, :, :], vtok[:, :, :], bc_h(beta3[:, tt, :]), ALU.mult, reads=[('vtok', pb), 'beta'], writes=['vb'])
            P.op('gpsimd', 'tensor_tensor', kbg[:, :, :], ktok[:, :, :], bc_h(bgc3[:, tt, :]), ALU.mult, reads=[('ktok', pb), 'bgc'], writes=['kbg'])
            P.op('gpsimd', 'tensor_tensor', kdec[:, :, :], ktok[:, :, :], bc_h(ekd3[:, tt, :]), ALU.mult, reads=[('ktok', pb), 'ekd'], writes=[('kdec', pb)])
            yield
            pt, ptk = ntb()
            for h in range(8):
                P.tr(pt[:, h * 128:(h + 1) * 128], N1[:, h, :], ident_bf[:, :], [('N', 0, 0), ('N', 0, 1), 'ident_bf'], [ptk], signal=(h == 7))
            P.op('vector', 'tensor_copy', M1[:, :, :], t8(pt), reads=[ptk], writes=[('M', 0, 0), ('M', 0, 1)])
            P.op('vector', 'tensor_tensor', PTf[:, :, :], t8(pt), bc_m(ident_f[:, :]), ALU.add, reads=[ptk, 'ident_f'], writes=[('PTf', 0), ('PTf', 1)])
            P.op('scalar', 'copy', PTb[:, :, :], PTf[:, :, :], reads=[('PTf', 0), ('PTf', 1)], writes=['PTb'])
            yield
            pt, ptk = ntb()
            for h in range(8):
                P.tr(pt[:, h * 128:(h + 1) * 128], attn[:, h, :], ident_bf[:, :], [('attn', 0), ('attn', 1), 'ident_bf'], [ptk], signal=(h == 7))
            P.op('scalar', 'copy', attnT[:, :, :], t8(pt), reads=[ptk], writes=[('attnT', pb)])
            yield
            cur = 0
            for k in range(1, 6):
                N1, M1 = NM[cur]
                N2, M2 = NM[1 - cur]
                for hb in range(2):
                    ps, pk = nb()
                    for hl in range(4):
                        h = hb * 4 + hl
                        P.mm(ps[:, hl * 128:(hl + 1) * 128], M1[:, h, :], N1[:, h, :], True, True, [('N', cur, hb), ('M', cur, hb)], [pk])
                    P.op('scalar' if hb == 0 else 'vector', 'copy' if hb == 0 else 'tensor_copy', N2[:, hb * 4:hb * 4 + 4, :], b4(ps),
                         reads=[pk], writes=[('N', 1 - cur, hb)])
                    if k < 5:
                        ps2, pk2 = nb()
                        for hl in range(4):
                            h = hb * 4 + hl
                            P.mm(ps2[:, hl * 128:(hl + 1) * 128], N1[:, h, :], M1[:, h, :], True, True, [('N', cur, hb), ('M', cur, hb)], [pk2])
                        P.op('vector' if hb == 0 else 'scalar', 'tensor_copy' if hb == 0 else 'copy', M2[:, hb * 4:hb * 4 + 4, :], b4(ps2),
                             reads=[pk2], writes=[('M', 1 - cur, hb)])
                    yield
                for hb in range(2):
                    ps, pk = nb()
                    for hl in range(4):
                        h = hb * 4 + hl
                        P.mm(ps[:, hl * 128:(hl + 1) * 128], N2[:, h, :], PTb[:, h, :], True, True, [('N', 1 - cur, hb), 'PTb'], [pk])
                    P.op('vector', 'tensor_tensor', PTf[:, hb * 4:hb * 4 + 4, :], b4(ps), PTf[:, hb * 4:hb * 4 + 4, :], ALU.add,
                         reads=[pk, ('PTf', hb)], writes=[('PTf', hb)])
                P.op('scalar', 'copy', PTb[:, :, :], PTf[:, :, :], reads=[('PTf', 0), ('PTf', 1)], writes=['PTb'])
                cur = 1 - cur
                yield
            for hb in range(2):
                ps, pk = nb()
                for hl in range(4):
                    h = hb * 4 + hl
                    P.mm(ps[:, hl * 128:(hl + 1) * 128], PTb[:, h, :], vb[:, h, :], True, True, ['PTb', 'vb'], [pk])
                P.op('scalar', 'copy', uf[:, hb * 4:hb * 4 + 4, :], b4(ps), reads=[pk], writes=[('uf', pb, hb)])
                ps2, pk2 = nb()
                for hl in range(4):
                    h = hb * 4 + hl
                    P.mm(ps2[:, hl * 128:(hl + 1) * 128], kbg[:, h, :], PTb[:, h, :], True, True, ['PTb', 'kbg'], [pk2])
                P.op('vector', 'tensor_copy', wT[:, hb * 4:hb * 4 + 4, :], b4(ps2), reads=[pk2], writes=[('wT', pb, hb)])
                yield

        def scan(tt):
            pb = tt % 2
            tl = slice(tt * 128, (tt + 1) * 128)
            attnT, kdec, uf, wT, zs = attnT2[pb], kdec2[pb], uf2[pb], wT2[pb], zs2[tt % 3]
            qTt = qTt3[tt % 3]
            for c in range(2):
                rows = slice(64 * c, 64 * c + 64)
                for hb in range(2):
                    hs = slice(hb * 4, hb * 4 + 4)
                    ps, pk = nb()
                    for hl in range(4):
                        h = hb * 4 + hl
                        P.mm(ps[:, hl * 128:(hl + 1) * 128], wT[:, h, :], Sb[:, h, :], True, True, [('wT', pb, hb), 'Sb'], [pk])
                    P.op('vector', 'tensor_tensor', vn[rows, hs, :], uf[rows, hs, :], b4(ps)[rows, :, :], ALU.subtract,
                         reads=[pk, ('uf', pb, hb)], writes=[('vn', hb)])
                    psq, pkq = nb()
                    for hl in range(4):
                        h = hb * 4 + hl
                        P.mm(psq[:, hl * 128:(hl + 1) * 128], qTt[:, h, :], Sb[:, h, :], True, True, [('qTt', tt % 3), 'Sb'], [pkq])
                    P.op('vector', 'tensor_tensor', tmpo[rows, hs, :], b4(psq)[rows, :, :], bc_h(egc3[:, tt, :])[rows, hs, :], ALU.mult,
                         reads=[pkq, 'egc'], writes=[('tmpo', hb)])
                    yield
                for hb in range(2):
                    hs = slice(hb * 4, hb * 4 + 4)
                    psa, pka = nb()
                    for hl in range(4):
                        h = hb * 4 + hl
                        P.mm(psa[:, hl * 128:(hl + 1) * 128], attnT[rows, h, :], vn[rows, h, :], True, True, [('attnT', pb), ('vn', hb)], [pka])
                    P.op('vector', 'tensor_tensor', oall[rows, hs, :], b4(psa)[rows, :, :], tmpo[rows, hs, :], ALU.add,
                         reads=[pka, ('tmpo', hb)], writes=[('oall', hb, c)])
                P.op('gpsimd', 'tensor_tensor', Sf[:, :, :], Sf[:, :, :], bc_h(egl2[:, tt, c, :]), ALU.mult,
                     reads=[('Sf', 0), ('Sf', 1), ('egl2', c)], writes=[('Sf', 0), ('Sf', 1)])
                yield
                for hb in range(2):
                    hs = slice(hb * 4, hb * 4 + 4)
                    psk, pkk = nb()
                    for hl in range(4):
                        h = hb * 4 + hl
                        P.mm(psk[:, hl * 128:(hl + 1) * 128], kdec[rows, h, :], vn[rows, h, :], True, True, [('kdec', pb), ('vn', hb)], [pkk])
                    P.op('vector', 'tensor_tensor', Sf[:, hs, :], b4(psk), Sf[:, hs, :], ALU.add, reads=[pkk, ('Sf', hb)], writes=[('Sf', hb)])
                P.op('scalar', 'copy', Sb[:, :, :], Sf[:, :, :], reads=[('Sf', 0), ('Sf', 1)], writes=['Sb'])
                yield
            okeys = [('oall', hb, c) for hb in range(2) for c in range(2)]
            P.op('gpsimd', 'tensor_tensor', o2[:, :, :], oall[:, :, :], oall[:, :, :], ALU.mult, reads=okeys, writes=['o2'])
            P.op('vector', 'tensor_reduce', ssq[:, :], o2[:, :, :], AX.X, ALU.add, reads=['o2'], writes=['gssq'])
            P.op('scalar', 'activation', ssq[:, :], ssq[:, :], AF.Sqrt, bias=epsb[:, 0:1], scale=1.0 / 128, reads=['gssq', 'epsb'], writes=['gssq'])
            P.op('vector', 'reciprocal', ssq[:, :], ssq[:, :], reads=['gssq'], writes=['gssq'])
            yield
            P.op('vector', 'tensor_tensor', o2[:, :, :], oall[:, :, :], bc_h(ssq[:, :]), ALU.mult, reads=okeys + ['gssq'], writes=['o2'])
            P.op('gpsimd', 'tensor_tensor', o2[:, :, :], o2[:, :, :], nwb[:, :].rearrange('p (a b) -> p a b', a=8), ALU.mult,
                 reads=['o2', 'nwb'], writes=['o2'])
            P.op('vector', 'tensor_tensor', onb[:, :, :], o2[:, :, :], zs[:, :].rearrange('p (a b) -> p a b', a=8), ALU.mult,
                 reads=['o2', ('zs', tt % 3)], writes=['onb'])
            yield
            pt, ptk = ntb()
            for h in range(8):
                P.tr(pt[:, h * 128:(h + 1) * 128], onb[:, h, :], ident_bf[:, :], ['onb', 'ident_bf'], [ptk], signal=(h == 7))
            P.op('scalar', 'copy', ostg[:, :, :], t8(pt), reads=[ptk], writes=['ostg'])
            P.dma('sync', G['odnT_d'][:, :, tl].rearrange('h p t -> p h t'), ostg[:, :, :], ['ostg'], [])
            yield

        loads(0)
        run_streams([prep(0)])
        for tt in range(16):
            run_streams([prep(tt + 1) if tt + 1 < 16 else None, scan(tt)])
        P.flush()


def norm_transpose(P, nc, es1, G, src_d, w_d, dstT, rstd_pre=None):
    tb = G['tbanks']
    ident_bf = G['ident_bf']
    epsb = G['epsb']
    w1b = _mk(es1, nc, 'nw_b', [128, D], F32)
    P.dma('scalar', w1b[:], w_d[0:1, :].partition_broadcast(128), [], ['nw_b'])
    xrot = _rot(es1, nc, 'nxt', 2, [128, D], F32)
    hbrot = _rot(es1, nc, 'nhb', 2, [128, D], BF16)
    junk = _mk(es1, nc, 'njunk', [128, D], BF16)
    ssq = _mk(es1, nc, 'nssq', [128, 16], F32)
    rstd = rstd_pre if rstd_pre is not None else _mk(es1, nc, 'nrstd', [128, 16], F32)
    for tt in range(16):
        xt, xk = xrot.next()
        hb, hk = hbrot.next()
        P.dma('sync', xt[:], src_d[tt * 128:(tt + 1) * 128, :], [], [xk])
        if rstd_pre is None:
            P.op('scalar', 'activation', junk[:], xt[:], AF.Square, accum_out=ssq[:, tt:tt + 1], reads=[xk], writes=['njunk', ('nssq', tt)])
            P.op('scalar', 'activation', rstd[:, tt:tt + 1], ssq[:, tt:tt + 1], AF.Sqrt, bias=epsb[:, 0:1], scale=1.0 / D,
                 reads=[('nssq', tt), 'epsb'], writes=[('nrstd', tt)])
            P.op('vector', 'reciprocal', rstd[:, tt:tt + 1], rstd[:, tt:tt + 1], reads=[('nrstd', tt)], writes=[('nrstd', tt)])
        P.op('vector', 'scalar_tensor_tensor', hb[:], xt[:], rstd[:, tt:tt + 1], w1b[:], ALU.mult, ALU.mult,
             reads=[xk, ('nrstd', tt), 'nrstd_all', 'nw_b'], writes=[hk])
        for half in range(2):
            pt, pk = tb[half]
            for j in range(8):
                kc = half * 8 + j
                P.tr(pt[:, j * 128:(j + 1) * 128], hb[:, kc * 128:(kc + 1) * 128], ident_bf[:], [hk, 'ident_bf'], [pk], signal=(j == 7))
            dst = dstT[:, half * 8:(half + 1) * 8, tt * 128:(tt + 1) * 128]
            src = pt[:, :].rearrange('p (j c) -> p j c', j=8)
            if half == 0:
                P.op('scalar', 'copy', dst, src, reads=[pk], writes=[('nT', tt, half)])
            else:
                P.op('vector', 'tensor_copy', dst, src, reads=[pk], writes=[('nT', tt, half)])


def phase_tail(P, nc, G):
    banks = G['banks']
    bi = [0]

    def nb():
        b = banks[bi[0] % 6]
        bi[0] += 1
        return b
    qi = [0]

    def oq():
        qi[0] += 1
        return ('sync', 'scalar')[qi[0] % 2]
    epsb = G['epsb']
    x1_d, x2_d, actT_d, mgT_d = G['x1_d'], G['x2_d'], G['actT_d'], G['mgT_d']
    with ExitStack() as es:
        ssq1 = _mk(es, nc, 'ssq1', [128, 16, 4], F32)
        ssq2 = _mk(es, nc, 'ssq2', [128, 16, 4], F32)
        rstd2 = _mk(es, nc, 'rstd2', [128, 16], F32)
        rstdf = _mk(es, nc, 'rstdf', [128, 16], F32)
        with ExitStack() as e1:
            mixedT = _mk(e1, nc, 'mixedT', [128, 16, S], BF16)
            with ExitStack() as e2:
                oa = _mk(e2, nc, 'onsaT_s', [128, 8, S], BF16)
                ob = _mk(e2, nc, 'odnT_s', [128, 8, S], BF16)
                P.dma('sync', oa[:, :, :], G['onsaT_d'].ap().rearrange('h p t -> p h t'), [], ['oa'])
                P.dma('scalar', ob[:, :, :], G['odnT_d'].ap().rearrange('h p t -> p h t'), [], ['ob'])
                wr = [_rot(e2, nc, 'wup%d' % i, 2, [128, 8, 512], BF16) for i in range(2)]
                grot = _rot(e2, nc, 'mg', 4, [128, 512], BF16)
                trot = _rot(e2, nc, 'mt', 4, [128, 512], F32)
                Wn = G['w_up_nsa'].ap().rearrange('(kc p) n -> p kc n', p=128)
                Wd = G['w_up_dn'].ap().rearrange('(kc p) n -> p kc n', p=128)
                for cb in range(4):
                    c0 = cb * 512
                    w1, w1k = wr[0].next()
                    w2, w2k = wr[1].next()
                    P.dma('gpsimd', w1[:, :, :], Wn[:, :, c0:c0 + 512], [], [w1k])
                    P.dma('gpsimd', w2[:, :, :], Wd[:, :, c0:c0 + 512], [], [w2k])
                    for m in range(4):
                        n0 = c0 + m * 128
                        for t4 in range(4):
                            t0 = t4 * 512
                            g1, g1k = grot.next()
                            g2, g2k = grot.next()
                            P.dma('sync', g1[:, :], mgT_d[n0:n0 + 128, t0:t0 + 512], [], [g1k])
                            P.dma('sync', g2[:, :], mgT_d[2048 + n0:2048 + n0 + 128, t0:t0 + 512], [], [g2k])
                            p1, p1k = nb()
                            for kc in range(8):
                                P.mm(p1[:, :], w1[:, kc, m * 128:(m + 1) * 128], oa[:, kc, t0:t0 + 512], kc == 0, kc == 7, [w1k, 'oa'], [p1k])
                            p2, p2k = nb()
                            for kc in range(8):
                                P.mm(p2[:, :], w2[:, kc, m * 128:(m + 1) * 128], ob[:, kc, t0:t0 + 512], kc == 0, kc == 7, [w2k, 'ob'], [p2k])
                            ta, tak = trot.next()
                            tb_, tbk = trot.next()
                            P.op('scalar', 'activation', g1[:, :], g1[:, :], AF.Sigmoid, reads=[g1k], writes=[g1k])
                            P.op('scalar', 'activation', g2[:, :], g2[:, :], AF.Sigmoid, reads=[g2k], writes=[g2k])
                            P.op('vector', 'tensor_tensor', ta[:, :], p1[:, :], g1[:, :], ALU.mult, reads=[p1k, g1k], writes=[tak])
                            P.op('vector', 'tensor_tensor', tb_[:, :], p2[:, :], g2[:, :], ALU.mult, reads=[p2k, g2k], writes=[tbk])
                            P.op('vector', 'tensor_tensor', mixedT[:, n0 // 128, t0:t0 + 512], ta[:, :], tb_[:, :], ALU.add,
                                 reads=[tak, tbk], writes=[('mixedT', n0 // 128, t4)])
                P.flush()
            if 'mixedT' in G['dbg_t']:
                P.dma('sync', G['dbg_t']['mixedT'].ap().rearrange('(kc p) t -> p kc t', p=128), mixedT[:, :, :], [], [])
                P.flush()
            with ExitStack() as e2:
                xr = _rot(e2, nc, 'xres', 3, [128, 512], F32)
                orr = _rot(e2, nc, 'x1o', 3, [128, 512], F32)
                junk = _mk(e2, nc, 'junk1', [128, 512], BF16)

                def epi_res(src_d, dst_d, ssq):
                    def epi(ps, pk, info):
                        t0, c0 = info['t0'], info['c0']
                        xt, xk = xr.next()
                        o, ok = orr.next()
                        P.dma('sync', xt[:, :], src_d[t0:t0 + 128, c0:c0 + 512], [], [xk])
                        P.op('vector', 'tensor_tensor', o[:, :], ps, xt[:, :], ALU.add, reads=[pk, xk], writes=[ok])
                        P.op('scalar', 'activation', junk[:, :], o[:, :], AF.Square, accum_out=ssq[:, t0 // 128, c0 // 512:c0 // 512 + 1],
                             reads=[ok], writes=['junk1', ('ssq', t0 // 128, c0 // 512)])
                        P.dma('scalar', dst_d[t0:t0 + 128, c0:c0 + 512], o[:, :], [ok], [])
                    return epi
                Wo = G['w_o'].ap().rearrange('(kc p) n -> p kc n', p=128)
                blocks = [(c0, 512, 'tok', epi_res(G['x'], x1_d, ssq1)) for c0 in range(0, D, 512)]
                dense(P, nc, e2, 'wo', mixedT, 'mixedT_all', 16, Wo, blocks, banks)
                P.flush()

        def finish_rstd(ssq, rstd):
            P.op('vector', 'tensor_reduce', rstd[:, :], ssq[:, :, :], AX.X, ALU.add, reads=[], writes=['nrstd_all'])
            P.op('scalar', 'activation', rstd[:, :], rstd[:, :], AF.Sqrt, bias=epsb[:, 0:1], scale=1.0 / D, reads=['nrstd_all', 'epsb'], writes=['nrstd_all'])
            P.op('vector', 'reciprocal', rstd[:, :], rstd[:, :], reads=['nrstd_all'], writes=['nrstd_all'])
        e_ffn = es
        KF = DFF // 128
        wdr = _rot(es, nc, 'wdn', 2, [128, KF, 512], BF16)
        Wdn = G['w_ffn_down'].ap().rearrange('(kc p) n -> p kc n', p=128)
        wd_pre = []
        with ExitStack() as e1:
            h2T = _mk(e1, nc, 'h2T', [128, 16, S], BF16)
            with ExitStack() as e2:
                finish_rstd(ssq1, rstd2)
                norm_transpose(P, nc, e2, G, x1_d, G['norm2_w'], h2T, rstd_pre=rstd2)
                P.flush()
            for cb in range(2):
                wt, wk = wdr.next()
                for part in range(4):
                    P.dma('gpsimd', wt[:, part * 11:(part + 1) * 11, :], Wdn[:, part * 11:(part + 1) * 11, cb * 512:(cb + 1) * 512], [], [(wk, part)])
                wd_pre.append((wt, wk))
            with ExitStack() as e2:
                wr = [_rot(e2, nc, 'wgu%d' % i, 2, [128, 16, 256], BF16) for i in range(2)]
                sgr = _rot(e2, nc, 'sg', 3, [128, 512], F32)
                acr = _rot(e2, nc, 'acb', 3, [128, 512], BF16)
                Wg = G['w_ffn_gate'].ap().rearrange('(kc p) n -> p kc n', p=128)
                Wu = G['w_ffn_up'].ap().rearrange('(kc p) n -> p kc n', p=128)
                for fb2 in range(DFF // 256):
                    c0 = fb2 * 256
                    w1, w1k = wr[0].next()
                    w2, w2k = wr[1].next()
                    P.dma('gpsimd', w1[:, :, :], Wg[:, :, c0:c0 + 256], [], [w1k])
                    P.dma('gpsimd', w2[:, :, :], Wu[:, :, c0:c0 + 256], [], [w2k])
                    for m in range(2):
                        fb = fb2 * 2 + m
                        for t4 in range(4):
                            t0 = t4 * 512
                            p1, p1k = nb()
                            for kc in range(16):
                                P.mm(p1[:, :], w1[:, kc, m * 128:(m + 1) * 128], h2T[:, kc, t0:t0 + 512], kc == 0, kc == 15, [w1k, 'h2T'], [p1k])
                            p2, p2k = nb()
                            for kc in range(16):
                                P.mm(p2[:, :], w2[:, kc, m * 128:(m + 1) * 128], h2T[:, kc, t0:t0 + 512], kc == 0, kc == 15, [w2k, 'h2T'], [p2k])
                            sg, sgk = sgr.next()
                            ac, ack = acr.next()
                            P.op('scalar', 'activation', sg[:, :], p1[:, :], AF.Silu, reads=[p1k], writes=[sgk])
                            P.op('vector', 'tensor_tensor', ac[:, :], p2[:, :], sg[:, :], ALU.mult, reads=[p2k, sgk], writes=[ack])
                            P.dma('scalar', actT_d[t4 * 4:(t4 + 1) * 4, :, fb, :].rearrange('a p t -> p a t'),
                                  ac[:, :].rearrange('p (a t) -> p a t', a=4), [ack], [])
                P.flush()
        if True:
            e1 = e_ffn
            atr = _rot(e1, nc, 'actt', 3, [128, KF, 128], BF16)
            xr = _rot(e1, nc, 'x1res', 3, [128, 512], F32)
            orr = _rot(e1, nc, 'x2o', 3, [128, 512], F32)
            junk = _mk(e1, nc, 'junk2', [128, 512], BF16)
            wfb = _mk(e1, nc, 'wfb', [128, D], F32)
            P.dma('sync', wfb[:], G['norm_f_w'][0:1, :].partition_broadcast(128), [], ['wfb'])
            x2rot = _rot(e1, nc, 'x2t', 2, [128, 1536], F32)
            outrot = _rot(e1, nc, 'outt', 2, [128, D], F32)
            for cb in range(4):
                c0 = cb * 512
                if cb < 2:
                    wt, wk = wd_pre[cb]
                else:
                    wt, wk = wdr.next()
                    for part in range(4):
                        P.dma('gpsimd', wt[:, part * 11:(part + 1) * 11, :], Wdn[:, part * 11:(part + 1) * 11, c0:c0 + 512], [], [(wk, part)])
                wks = [(wk, part) for part in range(4)]
                pend = []
                at0, ak0 = atr.next()
                P.dma('sync', at0[:, :, :], actT_d[0, :, :, :], [], [ak0])
                pend.append((at0, ak0))
                for tt in range(16):
                    t0 = tt * 128
                    if tt + 1 < 16:
                        at1, ak1 = atr.next()
                        P.dma('sync', at1[:, :, :], actT_d[tt + 1, :, :, :], [], [ak1])
                        pend.append((at1, ak1))
                    at, ak = pend.pop(0)
                    ps, pk = nb()
                    for kc in range(KF):
                        P.mm(ps[:, :], at[:, kc, :], wt[:, kc, :], kc == 0, kc == KF - 1, wks + [ak], [pk])
                    xt, xk = xr.next()
                    o, ok = orr.next()
                    P.dma('sync', xt[:, :], x1_d[t0:t0 + 128, c0:c0 + 512], [], [xk])
                    P.op('vector', 'tensor_tensor', o[:, :], ps[:, :], xt[:, :], ALU.add, reads=[pk, xk], writes=[ok])
                    P.op('scalar', 'activation', junk[:, :], o[:, :], AF.Square, accum_out=ssq2[:, tt, cb:cb + 1],
                         reads=[ok], writes=['junk2', ('ssq2', tt, cb)])
                    if cb < 3:
                        P.dma('scalar', x2_d[t0:t0 + 128, c0:c0 + 512], o[:, :], [ok], [('x2', tt, cb)])
                    else:
                        P.op('vector', 'tensor_reduce', rstdf[:, tt:tt + 1], ssq2[:, tt, :], AX.X, ALU.add,
                             reads=[('ssq2', tt, c_) for c_ in range(4)], writes=[('rstdf', tt)])
                        P.op('scalar', 'activation', rstdf[:, tt:tt + 1], rstdf[:, tt:tt + 1], AF.Sqrt, bias=epsb[:, 0:1], scale=1.0 / D,
                             reads=[('rstdf', tt), 'epsb'], writes=[('rstdf', tt)])
                        P.op('vector', 'reciprocal', rstdf[:, tt:tt + 1], rstdf[:, tt:tt + 1], reads=[('rstdf', tt)], writes=[('rstdf', tt)])
                        x2t, x2k = x2rot.next()
                        ot, otk = outrot.next()
                        P.dma('sync', x2t[:, 0:1536], x2_d[t0:t0 + 128, 0:1536], [('x2', tt, c_) for c_ in range(3)], [x2k])
                        P.op('vector', 'scalar_tensor_tensor', ot[:, 0:1536], x2t[:, 0:1536], rstdf[:, tt:tt + 1], wfb[:, 0:1536], ALU.mult, ALU.mult,
                             reads=[x2k, ('rstdf', tt), 'wfb'], writes=[(otk, 0)])
                        P.op('vector', 'scalar_tensor_tensor', ot[:, 1536:2048], o[:, :], rstdf[:, tt:tt + 1], wfb[:, 1536:2048], ALU.mult, ALU.mult,
                             reads=[ok, ('rstdf', tt), 'wfb'], writes=[(otk, 1)])
                        P.dma('scalar', G['out_d'][t0:t0 + 128, :], ot[:, :], [(otk, 0), (otk, 1)], [])
            P.flush()
        if False:
            finish_rstd(ssq2, rstdf)
            wfb = _mk(e1, nc, 'wfb', [128, D], F32)
            P.dma('scalar', wfb[:], G['norm_f_w'][0:1, :].partition_broadcast(128), [], ['wfb'])
            xr = _rot(e1, nc, 'x2t', 2, [128, D], F32)
            orr = _rot(e1, nc, 'outt', 2, [128, D], F32)
            for tt in range(16):
                xt, xk = xr.next()
                o, ok = orr.next()
                P.dma('sync', xt[:, :], x2_d[tt * 128:(tt + 1) * 128, :], [], [xk])
                P.op('vector', 'scalar_tensor_tensor', o[:, :], xt[:, :], rstdf[:, tt:tt + 1], wfb[:, :], ALU.mult, ALU.mult,
                     reads=[xk, 'nrstd_all', 'wfb'], writes=[ok])
                P.dma('scalar', G['out_d'][tt * 128:(tt + 1) * 128, :], o[:, :], [ok], [])
            P.flush()


def build(dbg=None, stop_after=99):
    nc = bass.Bass('TRN2', target_bir_lowering=False)
    P = Prog(nc)
    dbg = dbg or []

    def din(name, shape, dt=F32):
        return nc.dram_tensor(name, list(shape), dt, kind='ExternalInput')

    def dscr(name, shape, dt):
        return nc.dram_tensor(name, list(shape), dt)

    x = din('x', [S, D])
    norm1_w = din('norm1_w', [1, D])
    w_in = din('w_in', [D, NIN])
    ropec = din('ropec', [128, S])
    ropes = din('ropes', [128, S])
    ident_bf_d = din('ident_bf', [128, 128], BF16)
    ident_f_d = din('ident_f', [128, 128])
    out_d = nc.dram_tensor('out', [S, D], F32, kind='ExternalOutput')
    G = {}
    for nme, shape, dt in (('maskC', [128, S], BF16), ('ov', [128, 32], BF16), ('E', [32, S], BF16),
                           ('masks8', [128, 8, 512], BF16), ('mb8', [128, 8, 512], BF16), ('maskCb', [128, S], BF16), ('tinyrow', [1, 128], BF16), ('onesrow', [1, 512], BF16), ('sel24', [24, 24 * 128], F32),
                           ('fm_mul', [128, 16, 32], F32), ('fm_add', [128, 16, 32], F32), ('ones_bf', [128, 128], BF16),
                           ('cmp_w1_k', [4096, 128], F32), ('cmp_w2_k', [128, 128], F32),
                           ('cmp_w1_v', [4096, 128], F32), ('cmp_w2_v', [128, 128], F32),
                           ('peT_k', [128, 32], F32), ('peT_v', [128, 32], F32),
                           ('tri', [128, 128], F32), ('sellast', [128, 128], F32), ('sel63', [128, 128], F32),
                           ('sel127', [128, 128], F32), ('mblow', [128, 128], F32), ('strictm', [128, 128], F32),
                           ('ones_f', [128, 128], F32), ('negmask', [128, 128], F32), ('convw', [128, 24, 4], F32), ('alog_rep', [128, 128], F32),
                           ('dtb_rep', [128, 128], F32), ('nw_rep', [1, 1024], F32),
                           ('w_up_nsa', [1024, D], F32), ('w_up_dn', [1024, D], F32), ('w_o', [D, D], F32),
                           ('norm2_w', [1, D], F32), ('w_ffn_gate', [D, DFF], F32), ('w_ffn_up', [D, DFF], F32),
                           ('w_ffn_down', [DFF, D], F32), ('norm_f_w', [1, D], F32)):
        G[nme] = din(nme, shape, dt)

    qT_d = dscr('qT_d', [8, 128, S], BF16)
    kvT_d = dscr('kvT_d', [6, 2, 128, S], BF16)
    vtok_d = dscr('vtok_d', [2, S, 256], BF16)
    gT_d = dscr('gT_d', [24, S], F32)
    dnqkvT_d = dscr('dnqkvT_d', [3072, S], F32)
    dnz_d = dscr('dnz_d', [S, 1024], F32)
    dnab_d = dscr('dnab_d', [S, 16], F32)
    mgT_d = dscr('mgT_d', [4096, S], BF16)
    onsaT_d = dscr('onsaT_d', [8, 128, S], BF16)
    odnT_d = dscr('odnT_d', [8, 128, S], BF16)
    ktok_d = dscr('ktok_d', [8, S, 128], BF16)
    gcT_d = dscr('gcT_d', [16, 1, 1024], F32)
    gqT_d = dscr('gqT_d', [8, 128, S], BF16)
    gkT_d = dscr('gkT_d', [8, 128, S], BF16)
    x1_d = dscr('x1_d', [S, D], F32)
    x2_d = dscr('x2_d', [S, D], F32)
    actT_d = dscr('actT_d', [16, 128, DFF // 128, 128], BF16)
    vtok2_d = dscr('vtok2_d', [8, S, 128], BF16)

    dbg_t = {}
    for nme, shape, dt in dbg:
        dbg_t[nme] = nc.dram_tensor('dbg_' + nme, list(shape), dt, kind='ExternalOutput')

    banks = []
    for b in range(6):
        t = nc.alloc_psum_tensor('psb%d' % b, [128, 512], F32)
        banks.append((t, ('ps', b)))
    tbanks = []
    for b in range(2):
        t = nc.alloc_psum_tensor('pst%d' % b, [128, 1024], BF16)
        tbanks.append((t, ('pst', b)))

    with ExitStack() as gs:
        ident_bf = _mk(gs, nc, 'ident_bf_s', [128, 128], BF16)
        ident_f = _mk(gs, nc, 'ident_f_s', [128, 128], F32)
        P.dma('sync', ident_bf[:], ident_bf_d[:, :], [], ['ident_bf'])
        P.dma('sync', ident_f[:], ident_f_d[:, :], [], ['ident_f'])
        epsb = _mk(gs, nc, 'epsb', [128, 1], F32)
        P.add('vector', lambda e: e.memset(epsb[:], EPS), [], ['epsb'])

        G.update(banks=banks, tbanks=tbanks, ident_f=ident_f, ident_bf=ident_bf, dbg_t=dbg_t, qT_d=qT_d, kvT_d=kvT_d,
                 vtok_d=vtok_d, gT_d=gT_d, onsaT_d=onsaT_d, odnT_d=odnT_d, ktok_d=ktok_d, vtok2_d=vtok2_d,
                 dnqkvT_d=dnqkvT_d, dnz_d=dnz_d, dnab_d=dnab_d, epsb=epsb, gcT_d=gcT_d, gqT_d=gqT_d, gkT_d=gkT_d)
        with ExitStack() as es:
            hT = _mk(es, nc, 'hT', [128, 16, S], BF16)
            with ExitStack() as es1:
                w1b = _mk(es1, nc, 'w1b', [128, D], F32)
                P.dma('scalar', w1b[:], norm1_w[0:1, :].partition_broadcast(128), [], ['w1b'])
                xrot = _rot(es1, nc, 'xt', 2, [128, D], F32)
                hbrot = _rot(es1, nc, 'hb', 2, [128, D], BF16)
                junk = _mk(es1, nc, 'junk', [128, D], BF16)
                ssq = _mk(es1, nc, 'ssq', [128, 16], F32)
                rstd = _mk(es1, nc, 'rstd', [128, 16], F32)
                for tt in range(16):
                    xt, xk = xrot.next()
                    hb, hk = hbrot.next()
                    P.dma('sync', xt[:], x[tt * 128:(tt + 1) * 128, :], [], [xk])
                    P.add('scalar', lambda e, xt=xt, tt=tt: e.activation(
                        junk[:], xt[:], AF.Square, accum_out=ssq[:, tt:tt + 1]),
                        reads=[xk], writes=['junk', ('ssq', tt)])
                    P.add('scalar', lambda e, tt=tt: e.activation(
                        rstd[:, tt:tt + 1], ssq[:, tt:tt + 1], AF.Sqrt, bias=epsb[:, 0:1], scale=1.0 / D),
                        reads=[('ssq', tt), 'epsb'], writes=[('rstd', tt)])
                    P.add('vector', lambda e, tt=tt: e.reciprocal(rstd[:, tt:tt + 1], rstd[:, tt:tt + 1]),
                        reads=[('rstd', tt)], writes=[('rstd', tt)])
                    P.add('vector', lambda e, xt=xt, hb=hb, tt=tt: e.scalar_tensor_tensor(
                        hb[:], xt[:], rstd[:, tt:tt + 1], w1b[:], ALU.mult, ALU.mult),
                        reads=[xk, ('rstd', tt), 'w1b'], writes=[hk])
                    for half in range(2):
                        pt, pk = tbanks[half]
                        for j in range(8):
                            kc = half * 8 + j
                            P.tr(pt[:, j * 128:(j + 1) * 128], hb[:, kc * 128:(kc + 1) * 128], ident_bf[:],
                                 reads=[hk, 'ident_bf'], writes=[pk], signal=(j == 7))
                        eng = 'scalar' if half == 0 else 'vector'
                        dst = hT[:, half * 8:(half + 1) * 8, tt * 128:(tt + 1) * 128]
                        src = pt[:, :].rearrange('p (j c) -> p j c', j=8)
                        if eng == 'scalar':
                            P.add('scalar', lambda e, dst=dst, src=src: e.copy(dst, src),
                                  reads=[pk], writes=[('hT', tt, half)])
                        else:
                            P.add('vector', lambda e, dst=dst, src=src: e.tensor_copy(dst, src),
                                  reads=[pk], writes=[('hT', tt, half)])
                P.flush()
            if 'hT' in dbg_t:
                P.dma('sync', dbg_t['hT'].ap().rearrange('(kc p) t -> p kc t', p=128), hT[:], [], [])
                P.flush()

            if stop_after >= 2:
                with ExitStack() as es2:
                    stf = _rot(es2, nc, 'stf', 4, [128, 512], F32)
                    stb = _rot(es2, nc, 'stb', 4, [128, 512], BF16)
                    wrot_in = _rot(es2, nc, 'inproj_w', 3, [128, 16, 512], BF16)
                    er = ExitStack()
                    cc = _mk(er, nc, 'cc', [128, S], F32)
                    ss = _mk(er, nc, 'ss', [128, S], F32)
                    P.dma('sync', cc[:], ropec[:, :], [], ['cc'])
                    P.dma('scalar', ss[:], ropes[:, :], [], ['ss'])
                    stf2 = _rot(er, nc, 'stf2', 3, [128, 512], F32)
                    oq = ['sync', 'scalar']
                    oqi = [0]

                    def outq():
                        oqi[0] += 1
                        return oq[oqi[0] % 2]

                    def epi_rope(dst_fn):
                        def epi(ps, pk, info):
                            t0 = info['t0']
                            a, ak = stf.next()
                            b, bk = stf2.next()
                            o, ok = stb.next()
                            P.add('vector', lambda e: e.tensor_tensor(a[:], ps, cc[:, t0:t0 + 512], ALU.mult),
                                  reads=[pk, 'cc'], writes=[ak])
                            P.add('vector', lambda e: e.tensor_tensor(b[0:64, :], ps[64:128, :], ss[0:64, t0:t0 + 512], ALU.mult),
                                  reads=[pk, 'ss'], writes=[bk])
                            P.add('vector', lambda e: e.tensor_tensor(b[64:128, :], ps[0:64, :], ss[64:128, t0:t0 + 512], ALU.mult),
                                  reads=[pk, 'ss'], writes=[(bk, 1)])
                            P.add('vector', lambda e: e.tensor_tensor(o[:], a[:], b[:], ALU.add),
                                  reads=[ak, bk, (bk, 1)], writes=[ok])
                            P.dma(outq(), dst_fn(info), o[:], reads=[ok], writes=[])
                        return epi

                    def epi_copy_feat(dst_fn, dt, func=None, key_fn=None):
                        def epi(ps, pk, info):
                            mw = info['mw']
                            if dt == BF16:
                                o, ok = stb.next()
                            else:
                                o, ok = stf.next()
                            if func is None:
                                P.add('scalar', lambda e: e.copy(o[0:mw, :], ps), reads=[pk], writes=[ok])
                            else:
                                P.add('scalar', lambda e: e.activation(o[0:mw, :], ps, func), reads=[pk], writes=[ok])
                            P.dma(outq(), dst_fn(info), o[0:mw, :], reads=[ok], writes=([key_fn(info)] if key_fn else []))
                        return epi

                    def epi_copy_tok(dst_fn, dt, func=None):
                        def epi(ps, pk, info):
                            nw = info['nw']
                            if dt == BF16:
                                o, ok = stb.next()
                            else:
                                o, ok = stf.next()
                            if func is None:
                                P.add('scalar', lambda e: e.copy(o[:, 0:nw], ps), reads=[pk], writes=[ok])
                            else:
                                P.add('scalar', lambda e: e.activation(o[:, 0:nw], ps, func), reads=[pk], writes=[ok])
                            P.dma(outq(), dst_fn(info), o[:, 0:nw], reads=[ok], writes=[])
                        return epi

                    blocks_q = []
                    for c0 in range(0, 1024, 512):
                        blocks_q.append((c0, 512, 'feat', epi_rope(
                            lambda info: qT_d[info['c0'] // 128, :, info['t0']:info['t0'] + 512])))
                    for i in range(6):
                        base = 1024 + i * 256
                        if i in (0, 2, 4):
                            blocks_q.append((base, 256, 'feat', epi_rope(
                                lambda info, i=i, base=base: kvT_d[i, (info['c0'] - base) // 128, :, info['t0']:info['t0'] + 512])))
                        elif i == 1:
                            blocks_q.append((base, 256, 'feat', epi_copy_feat(
                                lambda info, i=i, base=base: kvT_d[i, (info['c0'] - base) // 128, :, info['t0']:info['t0'] + 512], BF16)))
                        else:
                            blocks_q.append((base, 256, 'tok', epi_copy_tok(
                                lambda info, i=i: vtok_d[(i - 3) // 2, info['t0']:info['t0'] + 128, :], BF16)))
                    blocks_q.append((2560, 24, 'feat', epi_copy_feat(
                        lambda info: gT_d[0:24, info['t0']:info['t0'] + 512], F32, AF.Sigmoid)))
                    blocks_dn = []
                    for c0 in range(2584, 5656, 512):
                        blocks_dn.append((c0, 512, 'feat', epi_copy_feat(
                            lambda info: dnqkvT_d[info['c0'] - 2584:info['c0'] - 2584 + 128, info['t0']:info['t0'] + 512], F32,
                            key_fn=lambda info: ('dnqkvT', (info['c0'] - 2584) // 128, info['t0'] // 512))))
                    blocks_m = []
                    blocks_mg = []
                    for c0 in range(6696, 10792, 512):
                        blocks_mg.append((c0, 512, 'feat', epi_copy_feat(
                            lambda info: mgT_d[info['c0'] - 6696:info['c0'] - 6696 + 128, info['t0']:info['t0'] + 512], BF16)))
                    for c0 in range(5656, 6680, 512):
                        blocks_m.append((c0, 512, 'tok', epi_copy_tok(
                            lambda info: dnz_d[info['t0']:info['t0'] + 128, info['c0'] - 5656:info['c0'] - 5656 + 512], F32, AF.Silu)))
                    blocks_m.append((6680, 16, 'tok', epi_copy_tok(
                        lambda info: dnab_d[info['t0']:info['t0'] + 128, :], F32)))
                    blocks_m = blocks_m + blocks_mg
                    Wv = w_in.ap().rearrange('(kc p) n -> p kc n', p=128)
                    dense(P, nc, es2, 'inproj', hT, 'hTall', 16, Wv, blocks_dn + blocks_q, banks, wrot=wrot_in)
                    P.flush()
                    er.close()
                    with ExitStack() as ea:
                        gen = dense_gen(P, nc, es2, 'inproj', hT, 'hTall', 16, Wv, blocks_m, banks, wrot=wrot_in)
                        run_streams([gen, gdn_stepA_gen(P, nc, ea, G)])
                        P.flush()

        import os as _os
        if stop_after >= 3 and not _os.environ.get('SKIP_NSA'):
            phase_nsa(P, nc, G)
        if stop_after >= 4 and not _os.environ.get('SKIP_GDN'):
            phase_gdn(P, nc, G)
        G.update(x1_d=x1_d, x2_d=x2_d, actT_d=actT_d, mgT_d=mgT_d, x=x, out_d=out_d)
        if stop_after >= 5:
            phase_tail(P, nc, G)

        for nme, src in (('qT', qT_d), ('kvT', kvT_d), ('vtok', vtok_d), ('gT', gT_d), ('dnqkvT', dnqkvT_d),
                         ('dnz', dnz_d), ('dnab', dnab_d), ('mgT', mgT_d), ('onsaT', onsaT_d), ('odnT', odnT_d), ('x1', x1_d), ('x2', x2_d)):
            if nme in dbg_t:
                P.dma('sync', dbg_t[nme].ap(), src.ap(), [], [])
        P.flush()
        print('PROG stats', P.stats)
    return nc


_CACHE = {}


def make_shared_inputs(inputs):
    m = dict(host_consts())
    g = lambda k: np.ascontiguousarray(np.asarray(inputs[k], dtype=np.float32))
    m['norm1_w'] = g('norm1_w').reshape(1, D)
    m['w_in'] = g('w_in').reshape(D, NIN)
    for nm in ('k', 'v'):
        m['cmp_w1_' + nm] = g('cmp_w1_' + nm).reshape(4096, 128)
        m['cmp_w2_' + nm] = g('cmp_w2_' + nm).reshape(128, 128)
        m['peT_' + nm] = np.ascontiguousarray(g('cmp_pe_' + nm).reshape(32, 128).T)
    m['convw'] = np.ascontiguousarray(g('conv_w').reshape(4, 24, 128).transpose(2, 1, 0))
    m['alog_rep'] = np.ascontiguousarray(np.broadcast_to(np.tile(g('a_log').reshape(8), 16)[None, :], (128, 128)))
    m['dtb_rep'] = np.ascontiguousarray(np.broadcast_to(np.tile(g('dt_bias').reshape(8), 16)[None, :], (128, 128)))
    m['nw_rep'] = np.ascontiguousarray(np.tile(g('dn_norm_w').reshape(128), 8)[None, :])
    m['w_up_nsa'] = g('w_up_nsa').reshape(1024, D)
    m['w_up_dn'] = g('w_up_dn').reshape(1024, D)
    m['w_o'] = g('w_o').reshape(D, D)
    m['norm2_w'] = g('norm2_w').reshape(1, D)
    m['w_ffn_gate'] = g('w_ffn_gate').reshape(D, DFF)
    m['w_ffn_up'] = g('w_ffn_up').reshape(D, DFF)
    m['w_ffn_down'] = g('w_ffn_down').reshape(DFF, D)
    m['norm_f_w'] = g('norm_f_w').reshape(1, D)
    return m


def kernel(**inputs):
    if 'nc' not in _CACHE:
        _CACHE['nc'] = build()
    nc = _CACHE['nc']
    shared = make_shared_inputs(inputs)
    x = np.asarray(inputs['x'], dtype=np.float32)
    B = x.shape[0]
    in_maps = []
    for b in range(B):
        m = dict(shared)
        m['x'] = np.ascontiguousarray(x[b])
        in_maps.append(m)
    res = run_bass_kernel_spmd(nc, in_maps, core_ids=list(range(B)))
    return np.stack([np.asarray(r['out'], dtype=np.float32) for r in res.results], axis=0)
```

```python
import math
from contextlib import ExitStack

import numpy as np
import ml_dtypes
import concourse.bass as bass
import concourse.mybir as mybir
from concourse.bass_utils import run_bass_kernel_spmd

F32 = mybir.dt.float32
BF16 = mybir.dt.bfloat16
AF = mybir.ActivationFunctionType
ALU = mybir.AluOpType
AX = mybir.AxisListType

S = 2048
D = 2048
NIN = 10792
DFF = 5632
EPS = 1e-6
SAME_ENGINE_SYNC = True


class Prog:
    ENGS = ('tensor', 'vector', 'scalar', 'gpsimd', 'sync')
    DMAQ = ('sync', 'scalar', 'gpsimd')
    NDS = 6

    def __init__(self, nc):
        self.nc = nc
        self.ops = {e: [] for e in self.ENGS}
        self.state = {}
        self.esem = {e: nc.alloc_semaphore('s_' + e) for e in self.ENGS}
        self.dsem = {e: [nc.alloc_semaphore('d_%s_%d' % (e, i)) for i in range(self.NDS)] for e in self.DMAQ}
        self.dcnt = {e: 0 for e in self.DMAQ}
        self.dlast = {}
        self.base = {e: 0 for e in self.ENGS}
        self.known_c = {e: {} for e in self.ENGS}
        self.known_d = {e: {} for e in self.ENGS}
        self.stats = {}

    def add(self, eng, fn, reads=(), writes=(), signal=True, dma=False):
        idx = len(self.ops[eng])
        deps = []
        for k in reads:
            st = self.state.get(k)
            if st is not None and st['w'] is not None:
                deps.append(st['w'])
        for k in writes:
            st = self.state.get(k)
            if st is not None:
                if st['w'] is not None:
                    deps.append(st['w'])
                deps.extend(st['rc'].values())
                deps.extend(st['rd'])
        slot = None
        if dma:
            n = self.dcnt[eng]
            self.dcnt[eng] += 1
            slot = n % self.NDS
            me = ('d', (eng, slot), 16 * (n // self.NDS + 1))
            prev = self.dlast.get((eng, slot))
            if prev is not None:
                deps.append(prev)
            self.dlast[(eng, slot)] = me
        else:
            me = ('c', eng, idx)
        self.ops[eng].append(dict(fn=fn, deps=deps, signal=signal, dma=dma, slot=slot))
        for k in reads:
            st = self.state.setdefault(k, dict(w=None, rc={}, rd=[]))
            if dma:
                st['rd'].append(me)
            else:
                st['rc'][eng] = me
        for k in writes:
            self.state[k] = dict(w=me, rc={}, rd=[])

    def mm(self, out, lhsT, rhs, start, stop, reads, writes, signal=None):
        self.add('tensor', lambda e: e.matmul(out, lhsT, rhs, start=start, stop=stop),
                 reads, writes, signal=stop if signal is None else signal)

    def tr(self, out, in_, ident, reads, writes, signal=True):
        self.add('tensor', lambda e: e.transpose(out, in_, ident), reads, writes, signal=signal)

    def op(self, eng, meth, *args, reads=(), writes=(), **kw):
        self.add(eng, lambda e: getattr(e, meth)(*args, **kw), reads, writes)

    def dma(self, q, out, in_, reads, writes):
        self.add(q, lambda e: e.dma_start(out=out, in_=in_), reads, writes, dma=True)

    def barrier(self):
        deps = []
        for e in self.ENGS:
            for j in range(len(self.ops[e]) - 1, -1, -1):
                op = self.ops[e][j]
                if op['fn'] is not None and not op['dma']:
                    assert op['signal'], 'last op on %s before barrier must signal' % e
                    deps.append(('c', e, j))
                    break
        deps.extend(self.dlast.values())
        for e in self.ENGS:
            self.ops[e].append(dict(fn=None, deps=list(deps), signal=False, dma=False, slot=None))
        self.state = {}

    def flush(self):
        self.barrier()
        ops = self.ops
        cnt = {}
        nxt = {}
        for e in self.ENGS:
            c = self.base[e]
            cl = []
            for op in ops[e]:
                if op['fn'] is not None and not op['dma'] and op['signal']:
                    c += 1
                cl.append(c)
            cnt[e] = cl
            nl = [None] * len(ops[e])
            nx = None
            for j in range(len(ops[e]) - 1, -1, -1):
                op = ops[e][j]
                if op['fn'] is not None and not op['dma'] and op['signal']:
                    nx = cl[j]
                nl[j] = nx
            nxt[e] = nl
        self._nxt = nxt
        with self.nc.Block() as block:
            @block.tensor
            def _(e):
                self.emit('tensor', e)

            @block.vector
            def _(e):
                self.emit('vector', e)

            @block.scalar
            def _(e):
                self.emit('scalar', e)

            @block.gpsimd
            def _(e):
                self.emit('gpsimd', e)

            @block.sync
            def _(e):
                self.emit('sync', e)
        for e in self.ENGS:
            if cnt[e]:
                self.base[e] = cnt[e][-1]
            self.ops[e] = []

    def emit(self, ename, eng):
        ops = self.ops
        nxt = self._nxt
        known_c = self.known_c[ename]
        known_d = self.known_d[ename]
        n_wait = 0
        for i, op in enumerate(ops[ename]):
            for d in op['deps']:
                if d[0] == 'c':
                    _, f, j = d
                    if f == ename:
                        if ename == 'tensor' or not SAME_ENGINE_SYNC or j >= i:
                            continue
                        if not ops[f][j]['signal']:
                            continue
                    need = nxt[f][j]
                    assert need is not None, 'dependency on %s op %d never signals' % (f, j)
                    if known_c.get(f, 0) >= need:
                        continue
                    eng.wait_ge(self.esem[f], need)
                    known_c[f] = need
                    n_wait += 1
                else:
                    _, key, val = d
                    if known_d.get(key, 0) >= val:
                        continue
                    eng.wait_ge(self.dsem[key[0]][key[1]], val)
                    known_d[key] = val
                    n_wait += 1
            if op['fn'] is None:
                continue
            ins = op['fn'](eng)
            if op['dma']:
                ins.then_inc(self.dsem[ename][op['slot']], 16)
            elif op['signal']:
                ins.then_inc(self.esem[ename], 1)
        self.stats[ename] = self.stats.get(ename, 0) + len(ops[ename])
        self.stats[ename + '_w'] = self.stats.get(ename + '_w', 0) + n_wait


class Rot:
    def __init__(self, tiles, name):
        self.tiles = tiles
        self.name = name
        self.i = 0

    def next(self):
        j = self.i % len(self.tiles)
        self.i += 1
        return self.tiles[j], (self.name, j)


_UID = [0]


def _mk(es, nc, name, shape, dt):
    _UID[0] += 1
    return es.enter_context(nc.sbuf_tensor('%s_u%d' % (name, _UID[0]), shape, dt))


def _rot(es, nc, name, n, shape, dt):
    return Rot([_mk(es, nc, '%s%d' % (name, i), shape, dt) for i in range(n)], name)


def host_consts():
    c = {}
    inv = 1.0 / (10000.0 ** (np.arange(0, 128, 2, dtype=np.float32) / 128.0))
    ang = np.arange(S, dtype=np.float32)[:, None] * inv[None, :].astype(np.float32)
    ang = ang.astype(np.float32)
    cos = np.cos(ang).astype(np.float32).T
    sin = np.sin(ang).astype(np.float32).T
    c['ropec'] = np.ascontiguousarray(np.concatenate([cos, cos], 0))
    c['ropes'] = np.ascontiguousarray(np.concatenate([-sin, sin], 0))
    c['ident_bf'] = np.eye(128, dtype=np.float32).astype(ml_dtypes.bfloat16)
    c['ident_f'] = np.eye(128, dtype=np.float32)
    bf = ml_dtypes.bfloat16
    cc_ = np.arange(128)[:, None]
    qq = np.arange(S)[None, :]
    c['maskC'] = ((16 * cc_ + 31 <= qq) & (cc_ < 127)).astype(np.float32).astype(bf)
    cs = np.arange(128)[:, None] * 16
    ssb = np.arange(32)[None, :] * 64
    ov = np.clip(np.minimum(cs + 32, ssb + 64) - np.maximum(cs, ssb), 0, None) / 32.0
    ov[127] = 0
    c['ov'] = ov.astype(np.float32).astype(bf)
    c['E'] = (np.arange(S)[None, :] // 64 == np.arange(32)[:, None]).astype(np.float32).astype(bf)
    k_ = np.arange(128)[:, None]
    q_ = np.arange(512)[None, :]
    m8 = np.zeros((128, 8, 512), np.float32)
    for r in range(-4, 4):
        diff = q_ - (128 * r + k_)
        m8[:, r + 4, :] = ((diff >= 0) & (diff < 512))
    c['masks8'] = m8.astype(bf)
    c['mb8'] = ((m8 - 1.0) * 30000.0).astype(bf)
    c['maskCb'] = ((c['maskC'].astype(np.float32) - 1.0) * 30000.0).astype(bf)
    c['tinyrow'] = np.full((1, 128), 1e-30, np.float32).astype(bf)
    c['onesrow'] = np.ones((1, 512), np.float32).astype(bf)
    sel = np.zeros((24, 24, 128), np.float32)
    for i in range(24):
        sel[i, i, :] = 1
    c['sel24'] = sel.reshape(24, 24 * 128)
    t_ = np.arange(S)[:, None]
    j_ = np.arange(32)[None, :]
    blk = t_ // 64
    forced = (j_ == 0) | (j_ == blk) | (j_ == blk - 1)
    fut = j_ > blk
    fm_mul = (~(forced | fut)).astype(np.float32)
    fm_add = np.where(forced, 1e9, np.where(fut, -1e9, 0.0)).astype(np.float32)
    c['fm_mul'] = np.ascontiguousarray(fm_mul.reshape(16, 128, 32).transpose(1, 0, 2))
    c['fm_add'] = np.ascontiguousarray(fm_add.reshape(16, 128, 32).transpose(1, 0, 2))
    c['ones_bf'] = np.ones((128, 128), np.float32).astype(bf)
    p_ = np.arange(128)[:, None]
    f_ = np.arange(128)[None, :]
    same = (p_ // 64) == (f_ // 64)
    c['tri'] = ((p_ <= f_) & same).astype(np.float32)
    c['sellast'] = (p_ == 64 * (f_ // 64) + 63).astype(np.float32)
    c['sel63'] = np.broadcast_to(p_ == 63, (128, 128)).astype(np.float32).copy()
    c['sel127'] = np.broadcast_to(p_ == 127, (128, 128)).astype(np.float32).copy()
    c['mblow'] = np.where((f_ <= p_) & same, 0.0, 1e5).astype(np.float32)
    c['strictm'] = ((f_ < p_) & same).astype(np.float32)
    c['ones_f'] = np.ones((128, 128), np.float32)
    c['negmask'] = np.where((f_ <= p_) & same, 0.0, -1e5).astype(np.float32)
    return c


def dense_gen(P, nc, es, name, aT, aT_key, KC, Wv, blocks, banks, T=S, wbufs=3, wcols=512, wrot=None):
    if wrot is None:
        wrot = _rot(es, nc, name + '_w', wbufs, [128, KC, wcols], BF16)
    bi = [0]

    def nextbank():
        b = banks[bi[0] % len(banks)]
        bi[0] += 1
        return b

    for (c0, ncols, mode, epi) in blocks:
        wt, wk = wrot.next()
        P.dma('gpsimd', wt[:, :, 0:ncols], Wv[:, :, c0:c0 + ncols], reads=[], writes=[wk])
        if mode == 'feat':
            for m0 in range(0, ncols, 128):
                mw = min(128, ncols - m0)
                for t0 in range(0, T, 512):
                    ps, pk = nextbank()
                    for kc in range(KC):
                        P.mm(ps[0:mw, 0:512], wt[:, kc, m0:m0 + mw], aT[:, kc, t0:t0 + 512],
                             start=(kc == 0), stop=(kc == KC - 1), reads=[wk, aT_key], writes=[pk])
                    epi(ps[0:mw, 0:512], pk, dict(c0=c0 + m0, mw=mw, t0=t0))
                    yield
        else:
            for t0 in range(0, T, 128):
                ps, pk = nextbank()
                for kc in range(KC):
                    P.mm(ps[:, 0:ncols], aT[:, kc, t0:t0 + 128], wt[:, kc, 0:ncols],
                         start=(kc == 0), stop=(kc == KC - 1), reads=[wk, aT_key], writes=[pk])
                epi(ps[:, 0:ncols], pk, dict(c0=c0, nw=ncols, t0=t0))
                yield


def dense(*a, **kw):
    for _ in dense_gen(*a, **kw):
        pass


def run_streams(gens):
    gens = [g for g in gens if g is not None]
    while gens:
        for g in list(gens):
            try:
                next(g)
            except StopIteration:
                gens.remove(g)


def gdn_stepA_gen(P, nc, ea, G):
    tb = G['tbanks']
    sbank = (tb[0][0][:, :].bitcast(F32), tb[0][1])
    tbank = tb[1]
    ident_bf = G['ident_bf']
    epsb = G['epsb']

    def t8(ps):
        return ps[:, :].rearrange('p (a b) -> p a b', a=8)
    convw = _mk(ea, nc, 'convw', [128, 24, 4], F32)
    P.dma('sync', convw[:], G['convw'][:, :, :], [], ['convw'])
    ones_b = _mk(ea, nc, 'ones_bA', [128, 128], BF16)
    P.dma('sync', ones_b[:], G['ones_bf'][:, :], [], ['ones_bA'])
    xprot = _rot(ea, nc, 'xp', 2, [128, S + 3], F32)
    accrot = _rot(ea, nc, 'acc', 2, [128, S], F32)
    yrot = _rot(ea, nc, 'yy', 2, [128, S], F32)
    sq = _mk(ea, nc, 'sq', [128, S], BF16)
    outrot = _rot(ea, nc, 'qkT', 2, [128, S], BF16)
    rnrot = _rot(ea, nc, 'rn', 2, [128, 512], F32)
    st = _mk(ea, nc, 'tokst', [128, 16, 128], BF16)
    for i in range(2):
        P.op('vector', 'memset', xprot.tiles[i][:, 0:3], 0.0, reads=[], writes=[('xp', i, 'pad')])

    def stageX(c):
        xp, xk = xprot.next()
        acc, ak = accrot.next()
        P.dma('sync', xp[:, 3:S + 3], G['dnqkvT_d'][c * 128:(c + 1) * 128, :], [('dnqkvT', c, t4) for t4 in range(4)], [xk])
        P.op('vector', 'tensor_scalar', acc[:], xp[:, 0:S], convw[:, c, 0:1], None, ALU.mult, reads=[xk, xk + ('pad',), 'convw'], writes=[ak])
        yield
        for j in range(1, 4):
            P.op('vector', 'scalar_tensor_tensor', acc[:], xp[:, j:j + S], convw[:, c, j:j + 1], acc[:], ALU.mult, ALU.add,
                 reads=[xk, xk + ('pad',), 'convw', ak], writes=[ak])
            yield
        return_val[c] = (acc, ak)

    return_val = {}

    def stageY(c):
        which, h = c // 8, c % 8
        acc, ak = return_val[c]
        y, yk = yrot.next()
        P.op('scalar', 'activation', y[:], acc[:], AF.Silu, reads=[ak], writes=[yk])
        yield
        if which < 2:
            P.op('scalar', 'activation', sq[:], y[:], AF.Square, reads=[yk], writes=['sq'])
            ot, otk = outrot.next()
            for t4 in range(4):
                ps, pk = sbank
                P.mm(ps[:, :], ones_b[:, :], sq[:, t4 * 512:(t4 + 1) * 512], True, True, ['ones_bA', 'sq'], [pk])
                rn, rk = rnrot.next()
                P.op('scalar', 'activation', rn[:], ps[:, :], AF.Ln, bias=epsb[:, 0:1], reads=[pk, 'epsb'], writes=[rk])
                P.op('scalar', 'activation', rn[:], rn[:], AF.Exp, scale=-0.5, reads=[rk], writes=[rk])
                P.op('vector', 'scalar_tensor_tensor', ot[:, t4 * 512:(t4 + 1) * 512], y[:, t4 * 512:(t4 + 1) * 512],
                     (128 ** -0.5) if which == 0 else 1.0, rn[:], ALU.mult, ALU.mult, reads=[yk, rk], writes=[(otk, t4)])
                yield
            dstd = G['gqT_d'] if which == 0 else G['gkT_d']
            P.dma('scalar', dstd[h, :, :], ot[:, :], [(otk, t4) for t4 in range(4)], [])
            srcT = ot
            srck = [(otk, t4) for t4 in range(4)]
        else:
            P.op('scalar', 'copy', sq[:], y[:], reads=[yk], writes=['sq'])
            srcT = sq
            srck = ['sq']
        if which >= 1:
            for half in range(2):
                pt, pk = tbank
                for j in range(8):
                    tt = half * 8 + j
                    P.tr(pt[:, j * 128:(j + 1) * 128], srcT[:, tt * 128:(tt + 1) * 128], ident_bf[:, :],
                         srck + ['ident_bf'], [pk], signal=(j == 7))
                P.op('vector' if half == 0 else 'scalar', 'tensor_copy' if half == 0 else 'copy',
                     st[:, half * 8:(half + 1) * 8, :], t8(pt), reads=[pk], writes=[('tokst', half)])
                yield
            dstd = G['ktok_d'] if which == 1 else G['vtok2_d']
            P.dma('scalar', dstd[h].rearrange('(tt p) d -> p tt d', p=128), st[:, :, :], [('tokst', 0), ('tokst', 1)], [])
        yield

    for _ in stageX(0):
        yield
    for c in range(24):
        gx = stageX(c + 1) if c + 1 < 24 else iter(())
        gy = stageY(c)
        alive = [gx, gy]
        while alive:
            for g in list(alive):
                try:
                    next(g)
                except StopIteration:
                    alive.remove(g)
            yield


def phase_nsa(P, nc, G):
    banks = G['banks']
    tb = G['tbanks']
    SC = 128 ** -0.5
    sbank = [banks[0], banks[1], banks[2]]
    obank = [banks[3], banks[4]]
    ubank = [banks[5], (tb[0][0][:, :].bitcast(F32), tb[0][1])]
    gbank = (tb[1][0][:, :].bitcast(F32), tb[1][1])
    mbank = gbank
    cnt = dict(s=0, o=0)
    ident_f = G['ident_f']
    ident_bf = G['ident_bf']
    with ExitStack() as es:
        def ld(name, shape, dt, src, q='sync'):
            t = _mk(es, nc, name, shape, dt)
            P.dma(q, t[:], src, [], [name])
            return t
        maskCb = ld('maskCb', [128, S], BF16, G['maskCb'][:, :])
        mb8 = ld('mb8', [128, 8, 512], BF16, G['mb8'][:, :, :], 'scalar')
        ovt = ld('ovt', [128, 32], BF16, G['ov'][:, :])
        Et = ld('Et', [32, S], BF16, G['E'][:, :], 'scalar')
        ones = ld('ones', [128, 128], BF16, G['ones_bf'][:, :])
        tinyr = ld('tinyr', [1, 128], BF16, G['tinyrow'][:, :])
        onesr = ld('onesr', [1, 512], BF16, G['onesrow'][:, :])
        fm_mul = ld('fm_mul', [128, 16, 32], F32, G['fm_mul'][:, :, :], 'scalar')
        fm_add = ld('fm_add', [128, 16, 32], F32, G['fm_add'][:, :, :])
        W1 = []
        W2 = []
        peT = []
        for i, nm in enumerate(('k', 'v')):
            w1 = _mk(es, nc, 'cw1' + nm, [128, 32, 128], BF16)
            P.dma('gpsimd', w1[:], G['cmp_w1_' + nm].ap().rearrange('(l d) f -> d l f', d=128), [], ['cw1' + nm])
            w2 = _mk(es, nc, 'cw2' + nm, [128, 128], BF16)
            P.dma('gpsimd', w2[:], G['cmp_w2_' + nm][:, :], [], ['cw2' + nm])
            pt = _mk(es, nc, 'cpe' + nm, [128, 32], BF16)
            P.dma('gpsimd', pt[:], G['peT_' + nm][:, :], [], ['cpe' + nm])
            W1.append(w1); W2.append(w2); peT.append(pt)
        qtile = [_rot(es, nc, 'qt%d' % hl, 2, [128, 512], BF16) for hl in range(4)]
        ptrot = _rot(es, nc, 'pT', 4, [128, 512], BF16)
        rsrot = _rot(es, nc, 'rs', 2, [128, 512], F32)
        posrot = _rot(es, nc, 'pos', 2, [128, 512], F32)
        gbrot = _rot(es, nc, 'gbs', 4, [128, 512], F32)
        tgrot = _rot(es, nc, 'tg', 2, [128, 512], F32)
        tmrot = _rot(es, nc, 'tmpo', 2, [128, 512], F32)
        oacc = [_mk(es, nc, 'oacc%d' % i, [128, 512], F32) for i in range(4)]
        obf = _rot(es, nc, 'obf', 2, [128, 512], BF16)
        impacc = _mk(es, nc, 'impacc', [32, 512], F32)
        imptok = _mk(es, nc, 'imptok', [128, 4, 32], F32)
        impw = _mk(es, nc, 'impw', [128, 4, 32], F32)
        mx8 = _mk(es, nc, 'mx8', [128, 8], F32)
        thr = _mk(es, nc, 'thr', [128, 1], F32)
        selb = _mk(es, nc, 'selb', [128, 4, 32], F32)
        biasT = _mk(es, nc, 'biasT', [32, 512], BF16)

        for hk in range(2):
            with ExitStack() as eh:
                def ldh(name, shape, src, q):
                    t = _mk(eh, nc, name, shape, BF16)
                    P.dma(q, t[:], src, [], [name])
                    return t
                kcx = ldh('kcx', [128, S], G['kvT_d'][0, hk, :, :], 'sync')
                vcx = ldh('vcx', [128, S], G['kvT_d'][1, hk, :, :], 'scalar')
                ksT = ldh('ksT', [128, S], G['kvT_d'][2, hk, :, :], 'sync')
                kwT = ldh('kwT', [128, S], G['kvT_d'][4, hk, :, :], 'scalar')
                vs = ldh('vs', [128, 16, 128], G['vtok_d'][0].rearrange('(tt p) c -> p tt c', p=128)[:, :, hk * 128:(hk + 1) * 128], 'sync')
                vw = ldh('vw', [128, 16, 128], G['vtok_d'][1].rearrange('(tt p) c -> p tt c', p=128)[:, :, hk * 128:(hk + 1) * 128], 'scalar')
                kcT = _mk(eh, nc, 'kcT', [128, 128], BF16)
                vc = _mk(eh, nc, 'vc', [128, 128], BF16)
                cb = _mk(eh, nc, 'cb', [128, 1], F32)
                cu = _mk(eh, nc, 'cu', [128, 128], F32)
                ct = _mk(eh, nc, 'ct', [128, 128], F32)
                cg = _mk(eh, nc, 'cg', [128, 128], BF16)
                for i, (xT, xk) in enumerate(((kcx, 'kcx'), (vcx, 'vcx'))):
                    nm = 'kv'[i]
                    ps, pk = mbank
                    for l in range(32):
                        P.mm(ps[:, 0:127], W1[i][:, l, :], xT[:, l:l + 16 * 126 + 1:16], start=(l == 0), stop=(l == 31),
                             reads=['cw1' + nm, xk], writes=[pk])
                    ps2, pk2 = sbank[0]
                    for l in range(32):
                        P.mm(ps2[:, 0:1], W1[i][:, l, :], peT[i][:, l:l + 1], start=(l == 0), stop=(l == 31),
                             reads=['cw1' + nm, 'cpe' + nm], writes=[pk2])
                    P.op('vector', 'tensor_copy', cb[:], ps2[:, 0:1], reads=[pk2], writes=['cb'])
                    P.op('scalar', 'activation', cu[:, 0:127], ps[:, 0:127], AF.Identity, bias=cb[:, 0:1], reads=[pk, 'cb'], writes=['cu'])
                    P.op('vector', 'tensor_tensor', ct[:, 0:127], cu[:, 0:127], cu[:, 0:127], ALU.mult, reads=['cu'], writes=['ct'])
                    P.op('vector', 'tensor_scalar', ct[:, 0:127], ct[:, 0:127], 0.044715, 1.0, ALU.mult, ALU.add, reads=['ct'], writes=['ct'])
                    P.op('vector', 'tensor_tensor', ct[:, 0:127], ct[:, 0:127], cu[:, 0:127], ALU.mult, reads=['ct', 'cu'], writes=['ct'])
                    P.op('scalar', 'activation', ct[:, 0:127], ct[:, 0:127], AF.Tanh, scale=0.7978845608028654, reads=['ct'], writes=['ct'])
                    P.op('vector', 'scalar_tensor_tensor', ct[:, 0:127], ct[:, 0:127], 1.0, cu[:, 0:127], ALU.add, ALU.mult,
                         reads=['ct', 'cu'], writes=['ct'])
                    P.op('vector', 'tensor_scalar', cg[:, 0:127], ct[:, 0:127], 0.5, None, ALU.mult, reads=['ct'], writes=['cg'])
                    if i == 0:
                        P.mm(ps[:, 0:127], W2[0][:, :], cg[:, 0:127], True, True, ['cw2k', 'cg'], [pk])
                        P.op('vector', 'tensor_copy', kcT[:, 0:127], ps[:, 0:127], reads=[pk], writes=['kcT'])
                    else:
                        P.mm(ps[0:127, 0:128], cg[:, 0:127], W2[1][:, :], True, True, ['cw2v', 'cg'], [pk])
                        P.op('vector', 'tensor_copy', vc[0:127, :], ps[0:127, 0:128], reads=[pk], writes=['vc'])

                def load_q(qi_):
                    res = []
                    for hl in range(4):
                        h = hk * 4 + hl
                        t_, k_ = qtile[hl].next()
                        P.dma('sync', t_[:], G['qT_d'][h, :, qi_ * 512:(qi_ + 1) * 512], [], [k_])
                        res.append((t_, k_))
                    return res
                qnext = load_q(0)
                for qi in range(4):
                    q0 = qi * 512
                    qh = qnext
                    if qi + 1 < 4:
                        qnext = load_q(qi + 1)

                    tiles = []
                    for hl in range(4):
                        tiles.append(dict(br=0, hl=hl, ki=0, n=0, last=True))
                    ntile_cmp = 4
                    for hl in range(4):
                        kis = list(range(max(0, 4 * qi - 4), 4 * qi + 4))
                        for n_, ki in enumerate(kis):
                            tiles.append(dict(br=2, hl=hl, ki=ki, n=n_, last=(n_ == len(kis) - 1)))
                    for hl in range(4):
                        kis = list(range(0, 4 * qi + 4))
                        for n_, ki in enumerate(kis):
                            tiles.append(dict(br=1, hl=hl, ki=ki, n=n_, last=(n_ == len(kis) - 1)))
                    first_sel = next(i for i, t in enumerate(tiles) if t['br'] == 1)

                    def stage1(t):
                        hl, br, ki = t['hl'], t['br'], t['ki']
                        qt, qk = qh[hl]
                        pS, pSk = sbank[cnt['s'] % 3]; cnt['s'] += 1
                        t['pS'] = (pS, pSk)
                        if br == 0:
                            P.mm(pS[0:127, :], kcT[:, 0:127], qt[:, :], True, False, ['kcT', qk], [pSk], signal=False)
                            P.mm(pS[0:127, :], ident_bf[0:127, 0:127], maskCb[0:127, q0:q0 + 512], False, True, ['ident_bf', 'maskCb'], [pSk])
                        elif br == 1:
                            r = ki - 4 * qi
                            P.mm(pS[:, :], ksT[:, ki * 128:(ki + 1) * 128], qt[:, :], True, False, ['ksT', qk], [pSk], signal=False)
                            if r >= 0:
                                P.mm(pS[:, :], ident_bf[:, :], mb8[:, r + 4, :], False, False, ['ident_bf', 'mb8'], [pSk], signal=False)
                            P.mm(pS[:, :], Et[0:32, ki * 128:(ki + 1) * 128], biasT[0:32, :], False, True, ['Et', 'biasT'], [pSk])
                        else:
                            r = ki - 4 * qi
                            P.mm(pS[:, :], kwT[:, ki * 128:(ki + 1) * 128], qt[:, :], True, False, ['kwT', qk], [pSk], signal=False)
                            P.mm(pS[:, :], ident_bf[:, :], mb8[:, r + 4, :], False, True, ['ident_bf', 'mb8'], [pSk])
                        if t['n'] == 0:
                            t['acc'] = (obank[cnt['o'] % 2], ubank[cnt['o'] % 2]); cnt['o'] += 1
                            h = hk * 4 + hl
                            gidx = h * 3 + br
                            gb, gbk = gbrot.next()
                            P.dma('sync', gb[:, :], G['gT_d'][gidx:gidx + 1, q0:q0 + 512].partition_broadcast(128), [], [gbk])
                            t['gb'] = (gb, gbk)

                    def stage2(t, head_t):
                        hl, br, ki = t['hl'], t['br'], t['ki']
                        pS, pSk = t['pS']
                        (po, pok), (pu, puk) = head_t['acc']
                        np_ = 127 if br == 0 else 128
                        pT, pTk = ptrot.next()
                        P.op('scalar', 'activation', pT[0:np_, :], pS[0:np_, :], AF.Exp, scale=SC, reads=[pSk], writes=[pTk])
                        first, last = t['n'] == 0, t['last']
                        if br == 0:
                            vt, vk = vc[0:127, :], 'vc'
                        elif br == 1:
                            vt, vk = vs[:, ki, :], 'vs'
                        else:
                            vt, vk = vw[:, ki, :], 'vw'
                        P.mm(po[:, :], vt, pT[0:np_, :], first, last, [vk, pTk], [pok])
                        if br == 0:
                            P.mm(pu[:, :], ones[0:127, :], pT[0:127, :], True, False, ['ones', pTk], [puk], signal=False)
                            P.mm(pu[:, :], tinyr[0:1, :], onesr[0:1, :], False, True, ['tinyr', 'onesr'], [puk])
                            pi_, pik = mbank
                            P.mm(pi_[0:32, :], ovt[0:127, :], pT[0:127, :], True, True, ['ovt', pTk], [pik])
                        else:
                            P.mm(pu[:, :], ones[:, :], pT[:, :], first, last, ['ones', pTk], [puk])
                        if not last:
                            return
                        gb, gbk = head_t['gb']
                        tg, tk = tgrot.next()
                        pos, posk = posrot.next()
                        P.op('vector', 'tensor_copy', pos[:], po[:, :], reads=[pok], writes=[posk])
                        def fin():
                            finish_rest(br, hl, pu, puk, pos, posk, gb, gbk, tg, tk, pi_ if br == 0 else None, pik if br == 0 else None)
                        if br == 0:
                            fin()
                        else:
                            pending.append([2, fin])

                    def finish_rest(br, hl, pu, puk, pos, posk, gb, gbk, tg, tk, pi_, pik):
                        rs, rk = rsrot.next()
                        P.op('scalar', 'activation', rs[:], pu[:, :], AF.Ln, reads=[puk], writes=[rk])
                        P.op('scalar', 'activation', rs[:], rs[:], AF.Exp, scale=-1.0, reads=[rk], writes=[rk])
                        if br == 0:
                            if hl == 0:
                                P.op('vector', 'tensor_tensor', impacc[:, :], pi_[0:32, :], rs[0:32, :], ALU.mult, reads=[pik, rk], writes=['impacc'])
                            else:
                                it, itk = tmrot.next()
                                P.op('vector', 'tensor_tensor', it[0:32, :], pi_[0:32, :], rs[0:32, :], ALU.mult, reads=[pik, rk], writes=[itk])
                                P.op('gpsimd', 'tensor_tensor', impacc[:, :], impacc[:, :], it[0:32, :], ALU.add, reads=[itk, 'impacc'], writes=['impacc'])
                        P.op('vector', 'tensor_tensor', tg[:], rs[:], gb[:, :], ALU.mult, reads=[rk, gbk], writes=[tk])
                        if br == 0:
                            P.op('vector', 'tensor_tensor', oacc[hl][:], pos[:], tg[:], ALU.mult, reads=[posk, tk], writes=[('oacc', hl)])
                        else:
                            tm, tmk = tmrot.next()
                            P.op('vector', 'tensor_tensor', tm[:], pos[:], tg[:], ALU.mult, reads=[posk, tk], writes=[tmk])
                            P.op('gpsimd', 'tensor_tensor', oacc[hl][:], oacc[hl][:], tm[:], ALU.add, reads=[tmk, ('oacc', hl)], writes=[('oacc', hl)])
                        if br == 1:
                            h = hk * 4 + hl
                            ob, obk = obf.next()
                            P.op('gpsimd', 'tensor_copy', ob[:], oacc[hl][:], reads=[('oacc', hl)], writes=[obk])
                            P.dma('sync', G['onsaT_d'][h, :, q0:q0 + 512], ob[:], [obk], [])

                    def topk():
                        pm, pmk = mbank
                        for s4 in range(4):
                            P.tr(pm[:, s4 * 32:(s4 + 1) * 32], impacc[0:32, s4 * 128:(s4 + 1) * 128], ident_f[0:32, 0:32],
                                 ['impacc', 'ident_f'], [pmk], signal=(s4 == 3))
                        P.op('vector', 'tensor_copy', imptok[:, :, :], pm[:, 0:128].rearrange('p (a b) -> p a b', a=4), reads=[pmk], writes=['imptok'])
                        P.op('vector', 'tensor_tensor', imptok[:, :, :], imptok[:, :, :], fm_mul[:, qi * 4:(qi + 1) * 4, :], ALU.mult,
                             reads=['imptok', 'fm_mul'], writes=['imptok'])
                        P.op('vector', 'tensor_tensor', imptok[:, :, :], imptok[:, :, :], fm_add[:, qi * 4:(qi + 1) * 4, :], ALU.add,
                             reads=['imptok', 'fm_add'], writes=['imptok'])
                        for s4 in range(4):
                            P.op('vector', 'max', mx8[:, :], imptok[:, s4, :], reads=['imptok'], writes=['mx8'])
                            P.op('vector', 'match_replace', impw[:, s4, :], mx8[:, :], imptok[:, s4, :], -3e9, reads=['imptok', 'mx8'], writes=[('impw', s4)])
                            P.op('vector', 'max', mx8[:, :], impw[:, s4, :], reads=[('impw', s4)], writes=['mx8'])
                            P.op('vector', 'tensor_reduce', thr[:, :], mx8[:, :], AX.X, ALU.min, reads=['mx8'], writes=['thr'])
                            P.op('vector', 'tensor_scalar', selb[:, s4, :], imptok[:, s4, :], thr[:, 0:1], None, ALU.is_ge,
                                 reads=['imptok', 'thr'], writes=[('selb', s4)])
                            P.op('vector', 'tensor_scalar', selb[:, s4, :], selb[:, s4, :], 30000.0, -30000.0, ALU.mult, ALU.add,
                                 reads=[('selb', s4)], writes=[('selb', s4)])

                    def topk2():
                        pm, pmk = mbank
                        for s4 in range(4):
                            P.tr(pm[0:32, s4 * 128:(s4 + 1) * 128], selb[:, s4, :], ident_f[:, :], [('selb', s4), 'ident_f'], [pmk], signal=(s4 == 3))
                        P.op('vector', 'tensor_copy', biasT[:, :], pm[0:32, :], reads=[pmk], writes=['biasT'])
                        if 'biasT' in G['dbg_t']:
                            P.dma('sync', G['dbg_t']['biasT'][hk, :, q0:q0 + 512], biasT[:, :], ['biasT'], [])

                    heads = {}
                    LOOK = 2
                    issued = 0

                    def issue_upto(lim):
                        nonlocal issued
                        while issued < min(lim, len(tiles)):
                            if tiles[issued]['br'] == 1 and not sel_ready[0]:
                                if not topk1_done[0]:
                                    break
                                topk2()
                                sel_ready[0] = True
                            stage1(tiles[issued])
                            issued += 1
                    topk1_done = [False]
                    pending = []
                    sel_ready = [False]
                    issue_upto(LOOK)
                    for n in range(len(tiles)):
                        t = tiles[n]
                        key = (t['br'], t['hl'])
                        if t['n'] == 0:
                            heads[key] = t
                        issue_upto(n + 1 + LOOK)
                        if issued <= n:
                            issue_upto(n + 1)
                        stage2(t, heads[key])
                        for pf in list(pending):
                            pf[0] -= 1
                            if pf[0] < 0:
                                pending.remove(pf)
                                pf[1]()
                        if n == min(ntile_cmp + 1, first_sel - 1):
                            topk()
                            topk1_done[0] = True
                    for pf in pending:
                        pf[1]()
                    pending = []
                P.flush()


def phase_gdn(P, nc, G):
    banks = G['banks']
    tb = G['tbanks']
    bi = [0, 0]

    def nb():
        b = banks[bi[0] % 4]
        bi[0] += 1
        return b

    def ntb():
        b = tb[bi[1] % 2]
        bi[1] += 1
        return b
    ptbank = [banks[4], banks[5]]

    def b4(ps):
        return ps[:, :].rearrange('p (a b) -> p a b', a=4)

    def t8(ps):
        return ps[:, :].rearrange('p (a b) -> p a b', a=8)

    ident_f = G['ident_f']
    ident_bf = G['ident_bf']
    BIGK = ['gq', 'gk']
    with ExitStack() as es:
        def ld(name, shape, dt, src, q='sync'):
            t = _mk(es, nc, name, shape, dt)
            P.dma(q, t[:], src, [], [name])
            return t
        tri = ld('tri', [128, 128], F32, G['tri'][:, :])
        sellast = ld('sellast', [128, 128], F32, G['sellast'][:, :], 'scalar')
        sel63 = ld('sel63', [128, 128], F32, G['sel63'][:, :])
        sel127 = ld('sel127', [128, 128], F32, G['sel127'][:, :], 'scalar')
        mblow = ld('mblow', [128, 128], F32, G['mblow'][:, :])
        strict = ld('strict', [128, 128], F32, G['strictm'][:, :], 'scalar')
        ones_f = ld('ones_f', [128, 128], F32, G['ones_f'][:, :])
        ones_b = ld('ones_b', [128, 128], BF16, G['ones_bf'][:, :], 'scalar')
        alr = ld('alr', [128, 128], F32, G['alog_rep'][:, :], 'scalar')
        dtr = ld('dtr', [128, 128], F32, G['dtb_rep'][:, :])
        nwb = ld('nwb', [128, 1024], F32, G['nw_rep'][0:1, :].partition_broadcast(128), 'scalar')
        epsb = G['epsb']
        ktok_d = G['ktok_d']
        vtok2_d = G['vtok2_d']

        ab = ld('ab', [128, 16, 16], F32, G['dnab_d'].ap().rearrange('(tt p) c -> p tt c', p=128))
        names = ['beta', 'gg', 'gc', 'glsel', 'egc', 'ekd', 'negbeta', 'bgc', 'tmpa', 'tmpb']
        sc = {n: _mk(es, nc, 'sc_' + n, [128, 128], F32) for n in names}
        egl2 = _mk(es, nc, 'egl2', [128, 16, 2, 8], F32)

        def v3(t):
            return t[:, :].rearrange('p (a b) -> p a b', a=16)
        P.op('scalar', 'activation', v3(sc['beta']), ab[:, :, 8:16], AF.Sigmoid, reads=['ab'], writes=['beta'])
        P.op('vector', 'tensor_tensor', v3(sc['tmpa']), ab[:, :, 0:8], v3(dtr), ALU.add, reads=['ab', 'dtr'], writes=['tmpa'])
        P.op('scalar', 'activation', sc['tmpa'][:, :], sc['tmpa'][:, :], AF.Exp, reads=['tmpa'], writes=['tmpa'])
        P.op('vector', 'tensor_scalar', sc['tmpa'][:, :], sc['tmpa'][:, :], 1.0, None, ALU.add, reads=['tmpa'], writes=['tmpa'])
        P.op('scalar', 'activation', sc['tmpa'][:, :], sc['tmpa'][:, :], AF.Ln, reads=['tmpa'], writes=['tmpa'])
        P.op('scalar', 'activation', sc['tmpb'][:, :], alr[:, :], AF.Exp, reads=['alr'], writes=['tmpb'])
        P.op('vector', 'scalar_tensor_tensor', sc['gg'][:, :], sc['tmpa'][:, :], -1.0, sc['tmpb'][:, :], ALU.mult, ALU.mult,
             reads=['tmpa', 'tmpb'], writes=['gg'])
        ps, pk = nb()
        P.mm(ps[:, 0:128], tri[:, :], sc['gg'][:, :], True, True, ['tri', 'gg'], [pk])
        P.op('vector', 'tensor_copy', sc['gc'][:, :], ps[:, 0:128], reads=[pk], writes=['gc'])
        ps, pk = nb()
        P.mm(ps[:, 0:128], sellast[:, :], sc['gc'][:, :], True, True, ['sellast', 'gc'], [pk])
        P.op('vector', 'tensor_tensor', sc['tmpa'][:, :], ps[:, 0:128], sc['gc'][:, :], ALU.subtract, reads=[pk, 'gc'], writes=['tmpa'])
        P.op('scalar', 'activation', sc['ekd'][:, :], sc['tmpa'][:, :], AF.Exp, reads=['tmpa'], writes=['ekd'])
        for ci, selm in enumerate((sel63, sel127)):
            ps, pk = nb()
            P.mm(ps[:, 0:128], selm[:, :], sc['gc'][:, :], True, True, ['sel63', 'sel127', 'gc'], [pk])
            P.op('scalar', 'activation', egl2[:, :, ci, :], ps[:, 0:128].rearrange('p (a b) -> p a b', a=16), AF.Exp,
                 reads=[pk], writes=[('egl2', ci)])
        P.op('scalar', 'activation', sc['egc'][:, :], sc['gc'][:, :], AF.Exp, reads=['gc'], writes=['egc'])
        P.op('vector', 'tensor_scalar', sc['negbeta'][:, :], sc['beta'][:, :], -1.0, None, ALU.mult, reads=['beta'], writes=['negbeta'])
        P.op('vector', 'tensor_tensor', sc['bgc'][:, :], sc['beta'][:, :], sc['egc'][:, :], ALU.mult, reads=['beta', 'egc'], writes=['bgc'])
        if 'gdn_sc' in G['dbg_t']:
            for i_, n_ in enumerate(('beta', 'gg', 'gc', 'ekd')):
                P.dma('sync', G['dbg_t']['gdn_sc'][i_].rearrange('(tt p) h -> p tt h', p=128), v3(sc[n_]), [n_], [])

        gcT_s = _mk(es, nc, 'gcT_s', [8, S], F32)
        gc3 = v3(sc['gc']); egc3 = v3(sc['egc']); nb3 = v3(sc['negbeta']); bgc3 = v3(sc['bgc'])
        beta3 = v3(sc['beta']); ekd3 = v3(sc['ekd'])
        for q4 in range(4):
            ps, pk = nb()
            for j in range(4):
                tt = q4 * 4 + j
                P.tr(ps[0:8, j * 128:(j + 1) * 128], gc3[:, tt, :], ident_f[:, :], ['gc', 'ident_f'], [pk], signal=(j == 3))
            P.op('vector', 'tensor_copy', gcT_s[:, q4 * 512:(q4 + 1) * 512], ps[0:8, :], reads=[pk], writes=[('gcT_s', q4)])
        P.dma('sync', G['gcT_d'].ap().rearrange('tt o (h t) -> h (tt o) t', h=8), gcT_s[:, :].rearrange('h (tt t) -> h tt t', tt=16),
              [('gcT_s', q4) for q4 in range(4)], ['gcT_d'])

        def T3(name, dt):
            return _mk(es, nc, name, [128, 8, 128], dt)
        decay = T3('decay', F32)
        NM = [[T3('N%d' % i, BF16), T3('M%d' % i, BF16)] for i in range(2)]
        PTf = T3('PTf', F32); PTb = T3('PTb', BF16)
        Nf = T3('Nf', F32)
        attn = T3('attn', BF16)
        vb = T3('vb', BF16); kbg = T3('kbg', BF16)
        vn = T3('vn', BF16)
        tmpo = T3('tmpo', F32); oall = T3('oall', F32); o2 = T3('o2', F32); onb = T3('onb', BF16)
        ostg = T3('ostg', BF16)
        attnT2 = [T3('attnT%d' % i, BF16) for i in range(2)]
        kdec2 = [T3('kdec%d' % i, BF16) for i in range(2)]
        uf2 = [T3('uf%d' % i, F32) for i in range(2)]
        wT2 = [T3('wT%d' % i, BF16) for i in range(2)]
        ktok2 = [T3('ktok%d' % i, BF16) for i in range(2)]
        vtok2 = [T3('vtok%d' % i, BF16) for i in range(2)]
        zs2 = [_mk(es, nc, 'zs%d' % i, [128, 1024], F32) for i in range(3)]
        grow2 = [_mk(es, nc, 'grow%d' % i, [128, 1024], F32) for i in range(2)]
        qTt3 = [T3('qTt%d' % i, BF16) for i in range(3)]
        kTt2 = [T3('kTt%d' % i, BF16) for i in range(2)]
        negmask = ld('negmask', [128, 128], F32, G['negmask'][:, :])
        Sf = T3('Sf', F32); Sb = T3('Sb', BF16)
        ssq = _mk(es, nc, 'gssq', [128, 8], F32)
        P.op('vector', 'memset', Sf[:, :, :], 0.0, reads=[], writes=[('Sf', 0), ('Sf', 1)])
        P.op('vector', 'memset', Sb[:, :, :], 0.0, reads=[], writes=['Sb'])

        def bc_h(ap2):
            return ap2.unsqueeze(2).to_broadcast([128, 8, 128])

        def bc_m(ap2):
            return ap2.unsqueeze(1).to_broadcast([128, 8, 128])

        def loads(tt):
            pb = tt % 2
            tl = slice(tt * 128, (tt + 1) * 128)
            P.dma('sync', ktok2[pb][:, :, :], ktok_d[:, tl, :].rearrange('h p d -> p h d'), [], [('ktok', pb)])
            P.dma('scalar', vtok2[pb][:, :, :], vtok2_d[:, tl, :].rearrange('h p d -> p h d'), [], [('vtok', pb)])
            P.dma('sync', zs2[tt % 3][:, :], G['dnz_d'][tl, :], [], [('zs', tt % 3)])
            P.dma('scalar', grow2[pb][:, :], G['gcT_d'][tt, 0:1, :].partition_broadcast(128), ['gcT_d'], [('grow', pb)])
            P.dma('sync', qTt3[tt % 3][:, :, :], G['gqT_d'][:, :, tl].rearrange('h p t -> p h t'), [], [('qTt', tt % 3)])
            P.dma('scalar', kTt2[pb][:, :, :], G['gkT_d'][:, :, tl].rearrange('h p t -> p h t'), [], [('kTt', pb)])

        def prep(tt):
            pb = tt % 2
            tl = slice(tt * 128, (tt + 1) * 128)
            ktok, vtok, grow = ktok2[pb], vtok2[pb], grow2[pb]
            qTt, kTt = qTt3[tt % 3], kTt2[pb]
            attnT, kdec, uf, wT = attnT2[pb], kdec2[pb], uf2[pb], wT2[pb]
            if tt + 1 < 16:
                loads(tt + 1)
            P.op('gpsimd', 'tensor_tensor', decay[:, :, :], bc_h(gc3[:, tt, :]), grow[:, :].rearrange('p (a b) -> p a b', a=8), ALU.subtract,
                 reads=['gc', ('grow', pb)], writes=['decay'])
            P.op('gpsimd', 'tensor_tensor', decay[:, :, :], decay[:, :, :], bc_m(negmask[:, :]), ALU.add, reads=['decay', 'negmask'], writes=['decay'])
            P.op('scalar', 'activation', decay[:, :, :], decay[:, :, :], AF.Exp, reads=['decay'], writes=['decay'])
            yield
            for hb in range(2):
                ps, pk = nb()
                for hl in range(4):
                    h = hb * 4 + hl
                    P.mm(ps[:, hl * 128:(hl + 1) * 128], kTt[:, h, :], kTt[:, h, :], True, True, [('kTt', pb)], [pk])
                P.op('vector', 'tensor_tensor', Nf[:, hb * 4:hb * 4 + 4, :], b4(ps), decay[:, hb * 4:hb * 4 + 4, :], ALU.mult,
                     reads=[pk, 'decay'], writes=[('Nf', hb)])
                ps2, pk2 = nb()
                for hl in range(4):
                    h = hb * 4 + hl
                    P.mm(ps2[:, hl * 128:(hl + 1) * 128], qTt[:, h, :], kTt[:, h, :], True, True, [('qTt', tt % 3), ('kTt', pb)], [pk2])
                P.op('vector', 'tensor_tensor', attn[:, hb * 4:hb * 4 + 4, :], b4(ps2), decay[:, hb * 4:hb * 4 + 4, :], ALU.mult,
                     reads=[pk2, 'decay'], writes=[('attn', hb)])
                yield
            P.op('gpsimd', 'tensor_tensor', Nf[:, :, :], Nf[:, :, :], bc_h(nb3[:, tt, :]), ALU.mult,
                 reads=[('Nf', 0), ('Nf', 1), 'negbeta'], writes=[('Nf', 0), ('Nf', 1)])
            N1, M1 = NM[0]
            P.op('gpsimd', 'tensor_tensor', N1[:, :, :], Nf[:, :, :], bc_m(strict[:, :]), ALU.mult,
                 reads=[('Nf', 0), ('Nf', 1), 'strict'], writes=[('N', 0, 0), ('N', 0, 1)])
            P.op('gpsimd', 'tensor_tensor', vb[:, :, :], vtok[:, :, :], bc_h(beta3[:, tt, :]), ALU.mult, reads=[('vtok', pb), 'beta'], writes=['vb'])
            P.op('gpsimd', 'tensor_tensor', kbg[:, :, :], ktok[:, :, :], bc_h(bgc3[:, tt, :]), ALU.mult, reads=[('ktok', pb), 'bgc'], writes=['kbg'])
            P.op('gpsimd', 'tensor_tensor', kdec[:, :, :], ktok[:, :, :], bc_h(ekd3[:, tt, :]), ALU.mult, reads=[('ktok', pb), 'ekd'], writes=[('kdec', pb)])
            yield
            pt, ptk = ntb()
            for h in range(8):
                P.tr(pt[:, h * 128:(h + 1) * 128], N1[:, h, :], ident_bf[:, :], [('N', 0, 0), ('N', 0, 1), 'ident_bf'], [ptk], signal=(h == 7))
            P.op('vector', 'tensor_copy', M1[:, :, :], t8(pt), reads=[ptk], writes=[('M', 0, 0), ('M', 0, 1)])
            for hb in range(2):
                pps, ppk = ptbank[hb]
                for hl in range(4):
                    h = hb * 4 + hl
                    P.mm(pps[:, hl * 128:(hl + 1) * 128], N1[:, h, :], ident_bf[:, :], hl == 0, False, [('N', 0, 0), ('N', 0, 1), 'ident_bf'], [ppk], signal=False)
                    P.mm(pps[:, hl * 128:(hl + 1) * 128], ident_bf[:, :], ident_bf[:, :], False, True, ['ident_bf'], [ppk])
                P.op('scalar', 'copy', PTb[:, hb * 4:hb * 4 + 4, :], b4(pps), reads=[ppk], writes=[('PTb', hb)])
            yield
            pt, ptk = ntb()
            for h in range(8):
                P.tr(pt[:, h * 128:(h + 1) * 128], attn[:, h, :], ident_bf[:, :], [('attn', 0), ('attn', 1), 'ident_bf'], [ptk], signal=(h == 7))
            P.op('scalar', 'copy', attnT[:, :, :], t8(pt), reads=[ptk], writes=[('attnT', pb)])
            yield
            cur = 0
            for k in range(1, 6):
                N1, M1 = NM[cur]
                N2, M2 = NM[1 - cur]
                for hb in range(2):
                    ps, pk = nb()
                    for hl in range(4):
                        h = hb * 4 + hl
                        P.mm(ps[:, hl * 128:(hl + 1) * 128], M1[:, h, :], N1[:, h, :], True, True, [('N', cur, hb), ('M', cur, hb)], [pk])
                    P.op('scalar' if hb == 0 else 'vector', 'copy' if hb == 0 else 'tensor_copy', N2[:, hb * 4:hb * 4 + 4, :], b4(ps),
                         reads=[pk], writes=[('N', 1 - cur, hb)])
                    if k < 5:
                        ps2, pk2 = nb()
                        for hl in range(4):
                            h = hb * 4 + hl
                            P.mm(ps2[:, hl * 128:(hl + 1) * 128], N1[:, h, :], M1[:, h, :], True, True, [('N', cur, hb), ('M', cur, hb)], [pk2])
                        P.op('vector' if hb == 0 else 'scalar', 'tensor_copy' if hb == 0 else 'copy', M2[:, hb * 4:hb * 4 + 4, :], b4(ps2),
                             reads=[pk2], writes=[('M', 1 - cur, hb)])
                    yield
                for hb in range(2):
                    pps, ppk = ptbank[hb]
                    for hl in range(4):
                        h = hb * 4 + hl
                        P.mm(pps[:, hl * 128:(hl + 1) * 128], N2[:, h, :], PTb[:, h, :], False, True, [('N', 1 - cur, hb), ('PTb', hb), ppk], [ppk])
                    P.op('scalar', 'copy', PTb[:, hb * 4:hb * 4 + 4, :], b4(pps), reads=[ppk], writes=[('PTb', hb)])
                cur = 1 - cur
                yield
            for hb in range(2):
                ps, pk = nb()
                for hl in range(4):
                    h = hb * 4 + hl
                    P.mm(ps[:, hl * 128:(hl + 1) * 128], PTb[:, h, :], vb[:, h, :], True, True, [('PTb', hb), 'vb'], [pk])
                P.op('scalar', 'copy', uf[:, hb * 4:hb * 4 + 4, :], b4(ps), reads=[pk], writes=[('uf', pb, hb)])
                ps2, pk2 = nb()
                for hl in range(4):
                    h = hb * 4 + hl
                    P.mm(ps2[:, hl * 128:(hl + 1) * 128], kbg[:, h, :], PTb[:, h, :], True, True, [('PTb', hb), 'kbg'], [pk2])
                P.op('vector', 'tensor_copy', wT[:, hb * 4:hb * 4 + 4, :], b4(ps2), reads=[pk2], writes=[('wT', pb, hb)])
                yield

        def scan(tt):
            pb = tt % 2
            tl = slice(tt * 128, (tt + 1) * 128)
            attnT, kdec, uf, wT, zs = attnT2[pb], kdec2[pb], uf2[pb], wT2[pb], zs2[tt % 3]
            qTt = qTt3[tt % 3]
            for c in range(2):
                rows = slice(64 * c, 64 * c + 64)
                for hb in range(2):
                    hs = slice(hb * 4, hb * 4 + 4)
                    ps, pk = nb()
                    for hl in range(4):
                        h = hb * 4 + hl
                        P.mm(ps[:, hl * 128:(hl + 1) * 128], wT[:, h, :], Sb[:, h, :], True, True, [('wT', pb, hb), 'Sb'], [pk])
                    P.op('vector', 'tensor_tensor', vn[rows, hs, :], uf[rows, hs, :], b4(ps)[rows, :, :], ALU.subtract,
                         reads=[pk, ('uf', pb, hb)], writes=[('vn', hb)])
                    psq, pkq = nb()
                    for hl in range(4):
                        h = hb * 4 + hl
                        P.mm(psq[:, hl * 128:(hl + 1) * 128], qTt[:, h, :], Sb[:, h, :], True, True, [('qTt', tt % 3), 'Sb'], [pkq])
                    P.op('vector', 'tensor_tensor', tmpo[rows, hs, :], b4(psq)[rows, :, :], bc_h(egc3[:, tt, :])[rows, hs, :], ALU.mult,
                         reads=[pkq, 'egc'], writes=[('tmpo', hb)])
                    yield
                for hb in range(2):
                    hs = slice(hb * 4, hb * 4 + 4)
                    psa, pka = nb()
                    for hl in range(4):
                        h = hb * 4 + hl
                        P.mm(psa[:, hl * 128:(hl + 1) * 128], attnT[rows, h, :], vn[rows, h, :], True, True, [('attnT', pb), ('vn', hb)], [pka])
                    P.op('vector', 'tensor_tensor', oall[rows, hs, :], b4(psa)[rows, :, :], tmpo[rows, hs, :], ALU.add,
                         reads=[pka, ('tmpo', hb)], writes=[('oall', hb, c)])
                P.op('gpsimd', 'tensor_tensor', Sf[:, :, :], Sf[:, :, :], bc_h(egl2[:, tt, c, :]), ALU.mult,
                     reads=[('Sf', 0), ('Sf', 1), ('egl2', c)], writes=[('Sf', 0), ('Sf', 1)])
                yield
                for hb in range(2):
                    hs = slice(hb * 4, hb * 4 + 4)
                    psk, pkk = nb()
                    for hl in range(4):
                        h = hb * 4 + hl
                        P.mm(psk[:, hl * 128:(hl + 1) * 128], kdec[rows, h, :], vn[rows, h, :], True, True, [('kdec', pb), ('vn', hb)], [pkk])
                    P.op('vector', 'tensor_tensor', Sf[:, hs, :], b4(psk), Sf[:, hs, :], ALU.add, reads=[pkk, ('Sf', hb)], writes=[('Sf', hb)])
                P.op('scalar', 'copy', Sb[:, :, :], Sf[:, :, :], reads=[('Sf', 0), ('Sf', 1)], writes=['Sb'])
                yield
            okeys = [('oall', hb, c) for hb in range(2) for c in range(2)]
            P.op('gpsimd', 'tensor_tensor', o2[:, :, :], oall[:, :, :], oall[:, :, :], ALU.mult, reads=okeys, writes=['o2'])
            P.op('vector', 'tensor_reduce', ssq[:, :], o2[:, :, :], AX.X, ALU.add, reads=['o2'], writes=['gssq'])
            P.op('scalar', 'activation', ssq[:, :], ssq[:, :], AF.Sqrt, bias=epsb[:, 0:1], scale=1.0 / 128, reads=['gssq', 'epsb'], writes=['gssq'])
            P.op('vector', 'reciprocal', ssq[:, :], ssq[:, :], reads=['gssq'], writes=['gssq'])
            yield
            P.op('vector', 'tensor_tensor', o2[:, :, :], oall[:, :, :], bc_h(ssq[:, :]), ALU.mult, reads=okeys + ['gssq'], writes=['o2'])
            P.op('gpsimd', 'tensor_tensor', o2[:, :, :], o2[:, :, :], nwb[:, :].rearrange('p (a b) -> p a b', a=8), ALU.mult,
                 reads=['o2', 'nwb'], writes=['o2'])
            P.op('vector', 'tensor_tensor', onb[:, :, :], o2[:, :, :], zs[:, :].rearrange('p (a b) -> p a b', a=8), ALU.mult,
                 reads=['o2', ('zs', tt % 3)], writes=['onb'])
            yield
            pt, ptk = ntb()
            for h in range(8):
                P.tr(pt[:, h * 128:(h + 1) * 128], onb[:, h, :], ident_bf[:, :], ['onb', 'ident_bf'], [ptk], signal=(h == 7))
            P.op('scalar', 'copy', ostg[:, :, :], t8(pt), reads=[ptk], writes=['ostg'])
            P.dma('sync', G['odnT_d'][:, :, tl].rearrange('h p t -> p h t'), ostg[:, :, :], ['ostg'], [])
            yield

        loads(0)
        run_streams([prep(0)])
        for tt in range(16):
            run_streams([prep(tt + 1) if tt + 1 < 16 else None, scan(tt)])
        P.flush()


def norm_transpose(P, nc, es1, G, src_d, w_d, dstT, rstd_pre=None):
    tb = G['tbanks']
    ident_bf = G['ident_bf']
    epsb = G['epsb']
    w1b = _mk(es1, nc, 'nw_b', [128, D], F32)
    P.dma('scalar', w1b[:], w_d[0:1, :].partition_broadcast(128), [], ['nw_b'])
    xrot = _rot(es1, nc, 'nxt', 2, [128, D], F32)
    hbrot = _rot(es1, nc, 'nhb', 2, [128, D], BF16)
    junk = _mk(es1, nc, 'njunk', [128, D], BF16)
    ssq = _mk(es1, nc, 'nssq', [128, 16], F32)
    rstd = rstd_pre if rstd_pre is not None else _mk(es1, nc, 'nrstd', [128, 16], F32)
    for tt in range(16):
        xt, xk = xrot.next()
        hb, hk = hbrot.next()
        P.dma('sync', xt[:], src_d[tt * 128:(tt + 1) * 128, :], [], [xk])
        if rstd_pre is None:
            P.op('scalar', 'activation', junk[:], xt[:], AF.Square, accum_out=ssq[:, tt:tt + 1], reads=[xk], writes=['njunk', ('nssq', tt)])
            P.op('scalar', 'activation', rstd[:, tt:tt + 1], ssq[:, tt:tt + 1], AF.Sqrt, bias=epsb[:, 0:1], scale=1.0 / D,
                 reads=[('nssq', tt), 'epsb'], writes=[('nrstd', tt)])
            P.op('vector', 'reciprocal', rstd[:, tt:tt + 1], rstd[:, tt:tt + 1], reads=[('nrstd', tt)], writes=[('nrstd', tt)])
        P.op('vector', 'scalar_tensor_tensor', hb[:], xt[:], rstd[:, tt:tt + 1], w1b[:], ALU.mult, ALU.mult,
             reads=[xk, ('nrstd', tt), 'nrstd_all', 'nw_b'], writes=[hk])
        for half in range(2):
            pt, pk = tb[half]
            for j in range(8):
                kc = half * 8 + j
                P.tr(pt[:, j * 128:(j + 1) * 128], hb[:, kc * 128:(kc + 1) * 128], ident_bf[:], [hk, 'ident_bf'], [pk], signal=(j == 7))
            dst = dstT[:, half * 8:(half + 1) * 8, tt * 128:(tt + 1) * 128]
            src = pt[:, :].rearrange('p (j c) -> p j c', j=8)
            if half == 0:
                P.op('scalar', 'copy', dst, src, reads=[pk], writes=[('nT', tt, half)])
            else:
                P.op('vector', 'tensor_copy', dst, src, reads=[pk], writes=[('nT', tt, half)])


def phase_tail(P, nc, G):
    banks = G['banks']
    bi = [0]

    def nb():
        b = banks[bi[0] % 6]
        bi[0] += 1
        return b
    qi = [0]

    def oq():
        qi[0] += 1
        return ('sync', 'scalar')[qi[0] % 2]
    epsb = G['epsb']
    x1_d, x2_d, actT_d, mgT_d = G['x1_d'], G['x2_d'], G['actT_d'], G['mgT_d']
    with ExitStack() as es:
        ssq1 = _mk(es, nc, 'ssq1', [128, 16, 4], F32)
        ssq2 = _mk(es, nc, 'ssq2', [128, 16, 4], F32)
        rstd2 = _mk(es, nc, 'rstd2', [128, 16], F32)
        rstdf = _mk(es, nc, 'rstdf', [128, 16], F32)
        with ExitStack() as e1:
            mixedT = _mk(e1, nc, 'mixedT', [128, 16, S], BF16)
            with ExitStack() as e2:
                oa = _mk(e2, nc, 'onsaT_s', [128, 8, S], BF16)
                ob = _mk(e2, nc, 'odnT_s', [128, 8, S], BF16)
                P.dma('sync', oa[:, :, :], G['onsaT_d'].ap().rearrange('h p t -> p h t'), [], ['oa'])
                P.dma('scalar', ob[:, :, :], G['odnT_d'].ap().rearrange('h p t -> p h t'), [], ['ob'])
                wr = [_rot(e2, nc, 'wup%d' % i, 2, [128, 8, 512], BF16) for i in range(2)]
                grot = _rot(e2, nc, 'mg', 4, [128, 512], BF16)
                trot = _rot(e2, nc, 'mt', 4, [128, 512], F32)
                Wn = G['w_up_nsa'].ap().rearrange('(kc p) n -> p kc n', p=128)
                Wd = G['w_up_dn'].ap().rearrange('(kc p) n -> p kc n', p=128)
                for cb in range(4):
                    c0 = cb * 512
                    w1, w1k = wr[0].next()
                    w2, w2k = wr[1].next()
                    P.dma('gpsimd', w1[:, :, :], Wn[:, :, c0:c0 + 512], [], [w1k])
                    P.dma('gpsimd', w2[:, :, :], Wd[:, :, c0:c0 + 512], [], [w2k])
                    for m in range(4):
                        n0 = c0 + m * 128
                        for t4 in range(4):
                            t0 = t4 * 512
                            g1, g1k = grot.next()
                            g2, g2k = grot.next()
                            P.dma('sync', g1[:, :], mgT_d[n0:n0 + 128, t0:t0 + 512], [], [g1k])
                            P.dma('sync', g2[:, :], mgT_d[2048 + n0:2048 + n0 + 128, t0:t0 + 512], [], [g2k])
                            p1, p1k = nb()
                            for kc in range(8):
                                P.mm(p1[:, :], w1[:, kc, m * 128:(m + 1) * 128], oa[:, kc, t0:t0 + 512], kc == 0, kc == 7, [w1k, 'oa'], [p1k])
                            p2, p2k = nb()
                            for kc in range(8):
                                P.mm(p2[:, :], w2[:, kc, m * 128:(m + 1) * 128], ob[:, kc, t0:t0 + 512], kc == 0, kc == 7, [w2k, 'ob'], [p2k])
                            ta, tak = trot.next()
                            tb_, tbk = trot.next()
                            P.op('scalar', 'activation', g1[:, :], g1[:, :], AF.Sigmoid, reads=[g1k], writes=[g1k])
                            P.op('scalar', 'activation', g2[:, :], g2[:, :], AF.Sigmoid, reads=[g2k], writes=[g2k])
                            P.op('vector', 'tensor_tensor', ta[:, :], p1[:, :], g1[:, :], ALU.mult, reads=[p1k, g1k], writes=[tak])
                            P.op('vector', 'tensor_tensor', tb_[:, :], p2[:, :], g2[:, :], ALU.mult, reads=[p2k, g2k], writes=[tbk])
                            P.op('vector', 'tensor_tensor', mixedT[:, n0 // 128, t0:t0 + 512], ta[:, :], tb_[:, :], ALU.add,
                                 reads=[tak, tbk], writes=[('mixedT', n0 // 128, t4)])
                P.flush()
            if 'mixedT' in G['dbg_t']:
                P.dma('sync', G['dbg_t']['mixedT'].ap().rearrange('(kc p) t -> p kc t', p=128), mixedT[:, :, :], [], [])
                P.flush()
            with ExitStack() as e2:
                xr = _rot(e2, nc, 'xres', 3, [128, 512], F32)
                orr = _rot(e2, nc, 'x1o', 3, [128, 512], F32)
                junk = _mk(e2, nc, 'junk1', [128, 512], BF16)

                def epi_res(src_d, dst_d, ssq):
                    def epi(ps, pk, info):
                        t0, c0 = info['t0'], info['c0']
                        xt, xk = xr.next()
                        o, ok = orr.next()
                        P.dma('sync', xt[:, :], src_d[t0:t0 + 128, c0:c0 + 512], [], [xk])
                        P.op('vector', 'tensor_tensor', o[:, :], ps, xt[:, :], ALU.add, reads=[pk, xk], writes=[ok])
                        P.op('scalar', 'activation', junk[:, :], o[:, :], AF.Square, accum_out=ssq[:, t0 // 128, c0 // 512:c0 // 512 + 1],
                             reads=[ok], writes=['junk1', ('ssq', t0 // 128, c0 // 512)])
                        P.dma('scalar', dst_d[t0:t0 + 128, c0:c0 + 512], o[:, :], [ok], [])
                    return epi
                Wo = G['w_o'].ap().rearrange('(kc p) n -> p kc n', p=128)
                blocks = [(c0, 512, 'tok', epi_res(G['x'], x1_d, ssq1)) for c0 in range(0, D, 512)]
                dense(P, nc, e2, 'wo', mixedT, 'mixedT_all', 16, Wo, blocks, banks)
                P.flush()

        def finish_rstd(ssq, rstd):
            P.op('vector', 'tensor_reduce', rstd[:, :], ssq[:, :, :], AX.X, ALU.add, reads=[], writes=['nrstd_all'])
            P.op('scalar', 'activation', rstd[:, :], rstd[:, :], AF.Sqrt, bias=epsb[:, 0:1], scale=1.0 / D, reads=['nrstd_all', 'epsb'], writes=['nrstd_all'])
            P.op('vector', 'reciprocal', rstd[:, :], rstd[:, :], reads=['nrstd_all'], writes=['nrstd_all'])
        e_ffn = es
        KF = DFF // 128
        wdr = _rot(es, nc, 'wdn', 2, [128, KF, 512], BF16)
        Wdn = G['w_ffn_down'].ap().rearrange('(kc p) n -> p kc n', p=128)
        wd_pre = []
        with ExitStack() as e1:
            h2T = _mk(e1, nc, 'h2T', [128, 16, S], BF16)
            with ExitStack() as e2:
                finish_rstd(ssq1, rstd2)
                norm_transpose(P, nc, e2, G, x1_d, G['norm2_w'], h2T, rstd_pre=rstd2)
                P.flush()
            for cb in range(2):
                wt, wk = wdr.next()
                for part in range(4):
                    P.dma('gpsimd', wt[:, part * 11:(part + 1) * 11, :], Wdn[:, part * 11:(part + 1) * 11, cb * 512:(cb + 1) * 512], [], [(wk, part)])
                wd_pre.append((wt, wk))
            with ExitStack() as e2:
                wr = [_rot(e2, nc, 'wgu%d' % i, 2, [128, 16, 256], BF16) for i in range(2)]
                sgr = _rot(e2, nc, 'sg', 3, [128, 512], F32)
                acr = _rot(e2, nc, 'acb', 3, [128, 512], BF16)
                Wg = G['w_ffn_gate'].ap().rearrange('(kc p) n -> p kc n', p=128)
                Wu = G['w_ffn_up'].ap().rearrange('(kc p) n -> p kc n', p=128)
                for fb2 in range(DFF // 256):
                    c0 = fb2 * 256
                    w1, w1k = wr[0].next()
                    w2, w2k = wr[1].next()
                    P.dma('gpsimd', w1[:, :, :], Wg[:, :, c0:c0 + 256], [], [w1k])
                    P.dma('gpsimd', w2[:, :, :], Wu[:, :, c0:c0 + 256], [], [w2k])
                    for m in range(2):
                        fb = fb2 * 2 + m
                        for t4 in range(4):
                            t0 = t4 * 512
                            p1, p1k = nb()
                            for kc in range(16):
                                P.mm(p1[:, :], w1[:, kc, m * 128:(m + 1) * 128], h2T[:, kc, t0:t0 + 512], kc == 0, kc == 15, [w1k, 'h2T'], [p1k])
                            p2, p2k = nb()
                            for kc in range(16):
                                P.mm(p2[:, :], w2[:, kc, m * 128:(m + 1) * 128], h2T[:, kc, t0:t0 + 512], kc == 0, kc == 15, [w2k, 'h2T'], [p2k])
                            sg, sgk = sgr.next()
                            ac, ack = acr.next()
                            P.op('scalar', 'activation', sg[:, :], p1[:, :], AF.Silu, reads=[p1k], writes=[sgk])
                            P.op('vector', 'tensor_tensor', ac[:, :], p2[:, :], sg[:, :], ALU.mult, reads=[p2k, sgk], writes=[ack])
                            P.dma('scalar', actT_d[t4 * 4:(t4 + 1) * 4, :, fb, :].rearrange('a p t -> p a t'),
                                  ac[:, :].rearrange('p (a t) -> p a t', a=4), [ack], [])
                P.flush()
        if True:
            e1 = e_ffn
            atr = _rot(e1, nc, 'actt', 3, [128, KF, 128], BF16)
            xr = _rot(e1, nc, 'x1res', 3, [128, 512], F32)
            orr = _rot(e1, nc, 'x2o', 3, [128, 512], F32)
            junk = _mk(e1, nc, 'junk2', [128, 512], BF16)
            wfb = _mk(e1, nc, 'wfb', [128, D], F32)
            P.dma('sync', wfb[:], G['norm_f_w'][0:1, :].partition_broadcast(128), [], ['wfb'])
            x2rot = _rot(e1, nc, 'x2t', 2, [128, 1536], F32)
            outrot = _rot(e1, nc, 'outt', 2, [128, D], F32)
            for cb in range(4):
                c0 = cb * 512
                if cb < 2:
                    wt, wk = wd_pre[cb]
                else:
                    wt, wk = wdr.next()
                    for part in range(4):
                        P.dma('gpsimd', wt[:, part * 11:(part + 1) * 11, :], Wdn[:, part * 11:(part + 1) * 11, c0:c0 + 512], [], [(wk, part)])
                wks = [(wk, part) for part in range(4)]
                pend = []
                at0, ak0 = atr.next()
                P.dma('sync', at0[:, :, :], actT_d[0, :, :, :], [], [ak0])
                pend.append((at0, ak0))
                for tt in range(16):
                    t0 = tt * 128
                    if tt + 1 < 16:
                        at1, ak1 = atr.next()
                        P.dma('sync', at1[:, :, :], actT_d[tt + 1, :, :, :], [], [ak1])
                        pend.append((at1, ak1))
                    at, ak = pend.pop(0)
                    ps, pk = nb()
                    for kc in range(KF):
                        P.mm(ps[:, :], at[:, kc, :], wt[:, kc, :], kc == 0, kc == KF - 1, wks + [ak], [pk])
                    xt, xk = xr.next()
                    o, ok = orr.next()
                    P.dma('sync', xt[:, :], x1_d[t0:t0 + 128, c0:c0 + 512], [], [xk])
                    P.op('vector', 'tensor_tensor', o[:, :], ps[:, :], xt[:, :], ALU.add, reads=[pk, xk], writes=[ok])
                    P.op('scalar', 'activation', junk[:, :], o[:, :], AF.Square, accum_out=ssq2[:, tt, cb:cb + 1],
                         reads=[ok], writes=['junk2', ('ssq2', tt, cb)])
                    if cb < 3:
                        P.dma('scalar', x2_d[t0:t0 + 128, c0:c0 + 512], o[:, :], [ok], [('x2', tt, cb)])
                    else:
                        P.op('vector', 'tensor_reduce', rstdf[:, tt:tt + 1], ssq2[:, tt, :], AX.X, ALU.add,
                             reads=[('ssq2', tt, c_) for c_ in range(4)], writes=[('rstdf', tt)])
                        P.op('scalar', 'activation', rstdf[:, tt:tt + 1], rstdf[:, tt:tt + 1], AF.Sqrt, bias=epsb[:, 0:1], scale=1.0 / D,
                             reads=[('rstdf', tt), 'epsb'], writes=[('rstdf', tt)])
                        P.op('vector', 'reciprocal', rstdf[:, tt:tt + 1], rstdf[:, tt:tt + 1], reads=[('rstdf', tt)], writes=[('rstdf', tt)])
                        x2t, x2k = x2rot.next()
                        ot, otk = outrot.next()
                        P.dma('sync', x2t[:, 0:1536], x2_d[t0:t0 + 128, 0:1536], [('x2', tt, c_) for c_ in range(3)], [x2k])
                        P.op('vector', 'scalar_tensor_tensor', ot[:, 0:1536], x2t[:, 0:1536], rstdf[:, tt:tt + 1], wfb[:, 0:1536], ALU.mult, ALU.mult,
                             reads=[x2k, ('rstdf', tt), 'wfb'], writes=[(otk, 0)])
                        P.op('vector', 'scalar_tensor_tensor', ot[:, 1536:2048], o[:, :], rstdf[:, tt:tt + 1], wfb[:, 1536:2048], ALU.mult, ALU.mult,
                             reads=[ok, ('rstdf', tt), 'wfb'], writes=[(otk, 1)])
                        P.dma('scalar', G['out_d'][t0:t0 + 128, :], ot[:, :], [(otk, 0), (otk, 1)], [])
            P.flush()
        if False:
            finish_rstd(ssq2, rstdf)
            wfb = _mk(e1, nc, 'wfb', [128, D], F32)
            P.dma('scalar', wfb[:], G['norm_f_w'][0:1, :].partition_broadcast(128), [], ['wfb'])
            xr = _rot(e1, nc, 'x2t', 2, [128, D], F32)
            orr = _rot(e1, nc, 'outt', 2, [128, D], F32)
            for tt in range(16):
                xt, xk = xr.next()
                o, ok = orr.next()
                P.dma('sync', xt[:, :], x2_d[tt * 128:(tt + 1) * 128, :], [], [xk])
                P.op('vector', 'scalar_tensor_tensor', o[:, :], xt[:, :], rstdf[:, tt:tt + 1], wfb[:, :], ALU.mult, ALU.mult,
                     reads=[xk, 'nrstd_all', 'wfb'], writes=[ok])
                P.dma('scalar', G['out_d'][tt * 128:(tt + 1) * 128, :], o[:, :], [ok], [])
            P.flush()


def build(dbg=None, stop_after=99):
    nc = bass.Bass('TRN2', target_bir_lowering=False)
    P = Prog(nc)
    dbg = dbg or []

    def din(name, shape, dt=F32):
        return nc.dram_tensor(name, list(shape), dt, kind='ExternalInput')

    def dscr(name, shape, dt):
        return nc.dram_tensor(name, list(shape), dt)

    x = din('x', [S, D])
    norm1_w = din('norm1_w', [1, D])
    w_in = din('w_in', [D, NIN])
    ropec = din('ropec', [128, S])
    ropes = din('ropes', [128, S])
    ident_bf_d = din('ident_bf', [128, 128], BF16)
    ident_f_d = din('ident_f', [128, 128])
    out_d = nc.dram_tensor('out', [S, D], F32, kind='ExternalOutput')
    G = {}
    for nme, shape, dt in (('maskC', [128, S], BF16), ('ov', [128, 32], BF16), ('E', [32, S], BF16),
                           ('masks8', [128, 8, 512], BF16), ('mb8', [128, 8, 512], BF16), ('maskCb', [128, S], BF16), ('tinyrow', [1, 128], BF16), ('onesrow', [1, 512], BF16), ('sel24', [24, 24 * 128], F32),
                           ('fm_mul', [128, 16, 32], F32), ('fm_add', [128, 16, 32], F32), ('ones_bf', [128, 128], BF16),
                           ('cmp_w1_k', [4096, 128], F32), ('cmp_w2_k', [128, 128], F32),
                           ('cmp_w1_v', [4096, 128], F32), ('cmp_w2_v', [128, 128], F32),
                           ('peT_k', [128, 32], F32), ('peT_v', [128, 32], F32),
                           ('tri', [128, 128], F32), ('sellast', [128, 128], F32), ('sel63', [128, 128], F32),
                           ('sel127', [128, 128], F32), ('mblow', [128, 128], F32), ('strictm', [128, 128], F32),
                           ('ones_f', [128, 128], F32), ('negmask', [128, 128], F32), ('convw', [128, 24, 4], F32), ('alog_rep', [128, 128], F32),
                           ('dtb_rep', [128, 128], F32), ('nw_rep', [1, 1024], F32),
                           ('w_up_nsa', [1024, D], F32), ('w_up_dn', [1024, D], F32), ('w_o', [D, D], F32),
                           ('norm2_w', [1, D], F32), ('w_ffn_gate', [D, DFF], F32), ('w_ffn_up', [D, DFF], F32),
                           ('w_ffn_down', [DFF, D], F32), ('norm_f_w', [1, D], F32)):
        G[nme] = din(nme, shape, dt)

    qT_d = dscr('qT_d', [8, 128, S], BF16)
    kvT_d = dscr('kvT_d', [6, 2, 128, S], BF16)
    vtok_d = dscr('vtok_d', [2, S, 256], BF16)
    gT_d = dscr('gT_d', [24, S], F32)
    dnqkvT_d = dscr('dnqkvT_d', [3072, S], F32)
    dnz_d = dscr('dnz_d', [S, 1024], F32)
    dnab_d = dscr('dnab_d', [S, 16], F32)
    mgT_d = dscr('mgT_d', [4096, S], BF16)
    onsaT_d = dscr('onsaT_d', [8, 128, S], BF16)
    odnT_d = dscr('odnT_d', [8, 128, S], BF16)
    ktok_d = dscr('ktok_d', [8, S, 128], BF16)
    gcT_d = dscr('gcT_d', [16, 1, 1024], F32)
    gqT_d = dscr('gqT_d', [8, 128, S], BF16)
    gkT_d = dscr('gkT_d', [8, 128, S], BF16)
    x1_d = dscr('x1_d', [S, D], F32)
    x2_d = dscr('x2_d', [S, D], F32)
    actT_d = dscr('actT_d', [16, 128, DFF // 128, 128], BF16)
    vtok2_d = dscr('vtok2_d', [8, S, 128], BF16)

    dbg_t = {}
    for nme, shape, dt in dbg:
        dbg_t[nme] = nc.dram_tensor('dbg_' + nme, list(shape), dt, kind='ExternalOutput')

    banks = []
    for b in range(6):
        t = nc.alloc_psum_tensor('psb%d' % b, [128, 512], F32)
        banks.append((t, ('ps', b)))
    tbanks = []
    for b in range(2):
        t = nc.alloc_psum_tensor('pst%d' % b, [128, 1024], BF16)
        tbanks.append((t, ('pst', b)))

    with ExitStack() as gs:
        ident_bf = _mk(gs, nc, 'ident_bf_s', [128, 128], BF16)
        ident_f = _mk(gs, nc, 'ident_f_s', [128, 128], F32)
        P.dma('sync', ident_bf[:], ident_bf_d[:, :], [], ['ident_bf'])
        P.dma('sync', ident_f[:], ident_f_d[:, :], [], ['ident_f'])
        epsb = _mk(gs, nc, 'epsb', [128, 1], F32)
        P.add('vector', lambda e: e.memset(epsb[:], EPS), [], ['epsb'])

        G.update(banks=banks, tbanks=tbanks, ident_f=ident_f, ident_bf=ident_bf, dbg_t=dbg_t, qT_d=qT_d, kvT_d=kvT_d,
                 vtok_d=vtok_d, gT_d=gT_d, onsaT_d=onsaT_d, odnT_d=odnT_d, ktok_d=ktok_d, vtok2_d=vtok2_d,
                 dnqkvT_d=dnqkvT_d, dnz_d=dnz_d, dnab_d=dnab_d, epsb=epsb, gcT_d=gcT_d, gqT_d=gqT_d, gkT_d=gkT_d)
        with ExitStack() as es:
            hT = _mk(es, nc, 'hT', [128, 16, S], BF16)
            with ExitStack() as es1:
                w1b = _mk(es1, nc, 'w1b', [128, D], F32)
                P.dma('scalar', w1b[:], norm1_w[0:1, :].partition_broadcast(128), [], ['w1b'])
                xrot = _rot(es1, nc, 'xt', 2, [128, D], F32)
                hbrot = _rot(es1, nc, 'hb', 2, [128, D], BF16)
                junk = _mk(es1, nc, 'junk', [128, D], BF16)
                ssq = _mk(es1, nc, 'ssq', [128, 16], F32)
                rstd = _mk(es1, nc, 'rstd', [128, 16], F32)
                for tt in range(16):
                    xt, xk = xrot.next()
                    hb, hk = hbrot.next()
                    P.dma('sync', xt[:], x[tt * 128:(tt + 1) * 128, :], [], [xk])
                    P.add('scalar', lambda e, xt=xt, tt=tt: e.activation(
                        junk[:], xt[:], AF.Square, accum_out=ssq[:, tt:tt + 1]),
                        reads=[xk], writes=['junk', ('ssq', tt)])
                    P.add('scalar', lambda e, tt=tt: e.activation(
                        rstd[:, tt:tt + 1], ssq[:, tt:tt + 1], AF.Sqrt, bias=epsb[:, 0:1], scale=1.0 / D),
                        reads=[('ssq', tt), 'epsb'], writes=[('rstd', tt)])
                    P.add('vector', lambda e, tt=tt: e.reciprocal(rstd[:, tt:tt + 1], rstd[:, tt:tt + 1]),
                        reads=[('rstd', tt)], writes=[('rstd', tt)])
                    P.add('vector', lambda e, xt=xt, hb=hb, tt=tt: e.scalar_tensor_tensor(
                        hb[:], xt[:], rstd[:, tt:tt + 1], w1b[:], ALU.mult, ALU.mult),
                        reads=[xk, ('rstd', tt), 'w1b'], writes=[hk])
                    for half in range(2):
                        pt, pk = tbanks[half]
                        for j in range(8):
                            kc = half * 8 + j
                            P.tr(pt[:, j * 128:(j + 1) * 128], hb[:, kc * 128:(kc + 1) * 128], ident_bf[:],
                                 reads=[hk, 'ident_bf'], writes=[pk], signal=(j == 7))
                        eng = 'scalar' if half == 0 else 'vector'
                        dst = hT[:, half * 8:(half + 1) * 8, tt * 128:(tt + 1) * 128]
                        src = pt[:, :].rearrange('p (j c) -> p j c', j=8)
                        if eng == 'scalar':
                            P.add('scalar', lambda e, dst=dst, src=src: e.copy(dst, src),
                                  reads=[pk], writes=[('hT', tt, half)])
                        else:
                            P.add('vector', lambda e, dst=dst, src=src: e.tensor_copy(dst, src),
                                  reads=[pk], writes=[('hT', tt, half)])
                P.flush()
            if 'hT' in dbg_t:
                P.dma('sync', dbg_t['hT'].ap().rearrange('(kc p) t -> p kc t', p=128), hT[:], [], [])
                P.flush()

            if stop_after >= 2:
                with ExitStack() as es2:
                    stf = _rot(es2, nc, 'stf', 4, [128, 512], F32)
                    stb = _rot(es2, nc, 'stb', 4, [128, 512], BF16)
                    wrot_in = _rot(es2, nc, 'inproj_w', 3, [128, 16, 512], BF16)
                    er = ExitStack()
                    cc = _mk(er, nc, 'cc', [128, S], F32)
                    ss = _mk(er, nc, 'ss', [128, S], F32)
                    P.dma('sync', cc[:], ropec[:, :], [], ['cc'])
                    P.dma('scalar', ss[:], ropes[:, :], [], ['ss'])
                    stf2 = _rot(er, nc, 'stf2', 3, [128, 512], F32)
                    oq = ['sync', 'scalar']
                    oqi = [0]

                    def outq():
                        oqi[0] += 1
                        return oq[oqi[0] % 2]

                    def epi_rope(dst_fn):
                        def epi(ps, pk, info):
                            t0 = info['t0']
                            a, ak = stf.next()
                            b, bk = stf2.next()
                            o, ok = stb.next()
                            P.add('vector', lambda e: e.tensor_tensor(a[:], ps, cc[:, t0:t0 + 512], ALU.mult),
                                  reads=[pk, 'cc'], writes=[ak])
                            P.add('vector', lambda e: e.tensor_tensor(b[0:64, :], ps[64:128, :], ss[0:64, t0:t0 + 512], ALU.mult),
                                  reads=[pk, 'ss'], writes=[bk])
                            P.add('vector', lambda e: e.tensor_tensor(b[64:128, :], ps[0:64, :], ss[64:128, t0:t0 + 512], ALU.mult),
                                  reads=[pk, 'ss'], writes=[(bk, 1)])
                            P.add('vector', lambda e: e.tensor_tensor(o[:], a[:], b[:], ALU.add),
                                  reads=[ak, bk, (bk, 1)], writes=[ok])
                            P.dma(outq(), dst_fn(info), o[:], reads=[ok], writes=[])
                        return epi

                    def epi_copy_feat(dst_fn, dt, func=None, key_fn=None):
                        def epi(ps, pk, info):
                            mw = info['mw']
                            if dt == BF16:
                                o, ok = stb.next()
                            else:
                                o, ok = stf.next()
                            if func is None:
                                P.add('scalar', lambda e: e.copy(o[0:mw, :], ps), reads=[pk], writes=[ok])
                            else:
                                P.add('scalar', lambda e: e.activation(o[0:mw, :], ps, func), reads=[pk], writes=[ok])
                            P.dma(outq(), dst_fn(info), o[0:mw, :], reads=[ok], writes=([key_fn(info)] if key_fn else []))
                        return epi

                    def epi_copy_tok(dst_fn, dt, func=None):
                        def epi(ps, pk, info):
                            nw = info['nw']
                            if dt == BF16:
                                o, ok = stb.next()
                            else:
                                o, ok = stf.next()
                            if func is None:
                                P.add('scalar', lambda e: e.copy(o[:, 0:nw], ps), reads=[pk], writes=[ok])
                            else:
                                P.add('scalar', lambda e: e.activation(o[:, 0:nw], ps, func), reads=[pk], writes=[ok])
                            P.dma(outq(), dst_fn(info), o[:, 0:nw], reads=[ok], writes=[])
                        return epi

                    blocks_q = []
                    for c0 in range(0, 1024, 512):
                        blocks_q.append((c0, 512, 'feat', epi_rope(
                            lambda info: qT_d[info['c0'] // 128, :, info['t0']:info['t0'] + 512])))
                    for i in range(6):
                        base = 1024 + i * 256
                        if i in (0, 2, 4):
                            blocks_q.append((base, 256, 'feat', epi_rope(
                                lambda info, i=i, base=base: kvT_d[i, (info['c0'] - base) // 128, :, info['t0']:info['t0'] + 512])))
                        elif i == 1:
                            blocks_q.append((base, 256, 'feat', epi_copy_feat(
                                lambda info, i=i, base=base: kvT_d[i, (info['c0'] - base) // 128, :, info['t0']:info['t0'] + 512], BF16)))
                        else:
                            blocks_q.append((base, 256, 'tok', epi_copy_tok(
                                lambda info, i=i: vtok_d[(i - 3) // 2, info['t0']:info['t0'] + 128, :], BF16)))
                    blocks_q.append((2560, 24, 'feat', epi_copy_feat(
                        lambda info: gT_d[0:24, info['t0']:info['t0'] + 512], F32, AF.Sigmoid)))
                    blocks_dn = []
                    for c0 in range(2584, 5656, 512):
                        blocks_dn.append((c0, 512, 'feat', epi_copy_feat(
                            lambda info: dnqkvT_d[info['c0'] - 2584:info['c0'] - 2584 + 128, info['t0']:info['t0'] + 512], F32,
                            key_fn=lambda info: ('dnqkvT', (info['c0'] - 2584) // 128, info['t0'] // 512))))
                    blocks_m = []
                    blocks_mg = []
                    for c0 in range(6696, 10792, 512):
                        blocks_mg.append((c0, 512, 'feat', epi_copy_feat(
                            lambda info: mgT_d[info['c0'] - 6696:info['c0'] - 6696 + 128, info['t0']:info['t0'] + 512], BF16)))
                    for c0 in range(5656, 6680, 512):
                        blocks_m.append((c0, 512, 'tok', epi_copy_tok(
                            lambda info: dnz_d[info['t0']:info['t0'] + 128, info['c0'] - 5656:info['c0'] - 5656 + 512], F32, AF.Silu)))
                    blocks_m.append((6680, 16, 'tok', epi_copy_tok(
                        lambda info: dnab_d[info['t0']:info['t0'] + 128, :], F32)))
                    blocks_m = blocks_m + blocks_mg
                    Wv = w_in.ap().rearrange('(kc p) n -> p kc n', p=128)
                    dense(P, nc, es2, 'inproj', hT, 'hTall', 16, Wv, blocks_dn + blocks_q, banks, wrot=wrot_in)
                    P.flush()
                    er.close()
                    with ExitStack() as ea:
                        gen = dense_gen(P, nc, es2, 'inproj', hT, 'hTall', 16, Wv, blocks_m, banks, wrot=wrot_in)
                        run_streams([gen, gdn_stepA_gen(P, nc, ea, G)])
                        P.flush()

        import os as _os
        if stop_after >= 3 and not _os.environ.get('SKIP_NSA'):
            phase_nsa(P, nc, G)
        if stop_after >= 4 and not _os.environ.get('SKIP_GDN'):
            phase_gdn(P, nc, G)
        G.update(x1_d=x1_d, x2_d=x2_d, actT_d=actT_d, mgT_d=mgT_d, x=x, out_d=out_d)
        if stop_after >= 5:
            phase_tail(P, nc, G)

        for nme, src in (('qT', qT_d), ('kvT', kvT_d), ('vtok', vtok_d), ('gT', gT_d), ('dnqkvT', dnqkvT_d),
                         ('dnz', dnz_d), ('dnab', dnab_d), ('mgT', mgT_d), ('onsaT', onsaT_d), ('odnT', odnT_d), ('x1', x1_d), ('x2', x2_d)):
            if nme in dbg_t:
                P.dma('sync', dbg_t[nme].ap(), src.ap(), [], [])
        P.flush()
        print('PROG stats', P.stats)
    return nc


_CACHE = {}


def make_shared_inputs(inputs):
    m = dict(host_consts())
    g = lambda k: np.ascontiguousarray(np.asarray(inputs[k], dtype=np.float32))
    m['norm1_w'] = g('norm1_w').reshape(1, D)
    m['w_in'] = g('w_in').reshape(D, NIN)
    for nm in ('k', 'v'):
        m['cmp_w1_' + nm] = g('cmp_w1_' + nm).reshape(4096, 128)
        m['cmp_w2_' + nm] = g('cmp_w2_' + nm).reshape(128, 128)
        m['peT_' + nm] = np.ascontiguousarray(g('cmp_pe_' + nm).reshape(32, 128).T)
    m['convw'] = np.ascontiguousarray(g('conv_w').reshape(4, 24, 128).transpose(2, 1, 0))
    m['alog_rep'] = np.ascontiguousarray(np.broadcast_to(np.tile(g('a_log').reshape(8), 16)[None, :], (128, 128)))
    m['dtb_rep'] = np.ascontiguousarray(np.broadcast_to(np.tile(g('dt_bias').reshape(8), 16)[None, :], (128, 128)))
    m['nw_rep'] = np.ascontiguousarray(np.tile(g('dn_norm_w').reshape(128), 8)[None, :])
    m['w_up_nsa'] = g('w_up_nsa').reshape(1024, D)
    m['w_up_dn'] = g('w_up_dn').reshape(1024, D)
    m['w_o'] = g('w_o').reshape(D, D)
    m['norm2_w'] = g('norm2_w').reshape(1, D)
    m['w_ffn_gate'] = g('w_ffn_gate').reshape(D, DFF)
    m['w_ffn_up'] = g('w_ffn_up').reshape(D, DFF)
    m['w_ffn_down'] = g('w_ffn_down').reshape(DFF, D)
    m['norm_f_w'] = g('norm_f_w').reshape(1, D)
    return m


def kernel(**inputs):
    if 'nc' not in _CACHE:
        _CACHE['nc'] = build()
    nc = _CACHE['nc']
    shared = make_shared_inputs(inputs)
    x = np.asarray(inputs['x'], dtype=np.float32)
    B = x.shape[0]
    in_maps = []
    for b in range(B):
        m = dict(shared)
        m['x'] = np.ascontiguousarray(x[b])
        in_maps.append(m)
    res = run_bass_kernel_spmd(nc, in_maps, core_ids=list(range(B)))
    return np.stack([np.asarray(r['out'], dtype=np.float32) for r in res.results], axis=0)
```
